# Optimizing a Trainium2 kernel written in Bass

```python
import math
import jax
import jax.numpy as jnp
from jax import lax
import numpy as np

D_MODEL = 2048
BATCH = 16
SEQ = 256
DEPTH = 4
DEC_BATCH = 8
DEC_SEQ = 2048
PAST_LEN = 256

GRID_W = 64
N_MIXERS = 4
EPS = 1e-6
NEG_INF = -1e30
ROPE_BASE = 10000.0
Q_BLOCK = 128

MLA_HEADS = 16
MLA_Q_RANK = 512
MLA_KV_RANK = 512
MLA_NOPE = 128
MLA_ROPE = 64
MLA_V = 128

S5_GROUP = 16
S5_GROUPS = D_MODEL // S5_GROUP
S5_STATE = 64
S5_DT_MIN = 1e-3
S5_DT_MAX = 1e-1

HY_ORDER = 2
HY_BANDS = 8
HY_EMB = 2 * HY_BANDS + 1
HY_HIDDEN = 64
HY_DECAY_TARGET = 1e-2
HY_FAST_PCT = 0.3
HY_SLOW_PCT = 1.5
HY_MIN_DECAY = math.log(1.0 / HY_DECAY_TARGET) / HY_SLOW_PCT
HY_MAX_DECAY = math.log(1.0 / HY_DECAY_TARGET) / HY_FAST_PCT

SWA_HEADS = 16
SWA_KV_HEADS = 4
SWA_GROUP = SWA_HEADS // SWA_KV_HEADS
SWA_HEAD_DIM = D_MODEL // SWA_HEADS
SWA_WINDOW = 128
SWA_BLOCK = 128

PEER_HEADS = 8
PEER_N_KEYS = 128
PEER_EXPERTS = PEER_N_KEYS * PEER_N_KEYS
PEER_KEY_DIM = 256
PEER_TOPK = 16
PEER_CHUNK = 128

F32 = jnp.float32

kernel_name = 'hybrid_diffusion_mla_s5_hyena_swa_peer'


def _layers_using(kind):
    return len(range(kind, DEPTH, N_MIXERS))


def _rmsnorm(x, g):
    xf = x.astype(F32)
    y = xf * lax.rsqrt(jnp.mean(xf * xf, axis=-1, keepdims=True) + EPS)
    return (y * g.astype(F32)).astype(x.dtype)


def _axial_rope_tables(n_tokens, rot_dim):
    rows = n_tokens // GRID_W
    row = jnp.repeat(jnp.arange(rows, dtype=F32), GRID_W)
    col = jnp.tile(jnp.arange(GRID_W, dtype=F32), rows)
    axis_dim = rot_dim // 2
    inv = ROPE_BASE ** (-jnp.arange(0, axis_dim, 2, dtype=F32) / axis_dim)
    ang_r = row[:, None] * inv[None, :]
    ang_c = col[:, None] * inv[None, :]
    return (jnp.cos(ang_r), jnp.sin(ang_r), jnp.cos(ang_c), jnp.sin(ang_c))


def _rotate(x, cos, sin):
    x1, x2 = jnp.split(x, 2, axis=-1)
    c = cos[None, :, None, :]
    s = sin[None, :, None, :]
    return jnp.concatenate([x1 * c - x2 * s, x1 * s + x2 * c], axis=-1)


def _apply_axial_rope(x, tables):
    cr, sr, cc, sc = tables
    xf = x.astype(F32)
    a = x.shape[-1] // 2
    out = jnp.concatenate([_rotate(xf[..., :a], cr, sr), _rotate(xf[..., a:], cc, sc)], axis=-1)
    return out.astype(x.dtype)


def _blocked_attention(q, k, v, scale, sink=None):
    B, Lq, KV, G, dk = q.shape
    nb = Lq // Q_BLOCK
    qb = jnp.moveaxis(q.reshape(B, nb, Q_BLOCK, KV, G, dk), 1, 0)

    def one_block(qblk):
        s = jnp.einsum('bqngd,bknd->bngqk', qblk, k).astype(F32) * scale
        if sink is not None:
            s_sink = jnp.broadcast_to(sink.astype(F32).reshape(1, KV, G, 1, 1), s.shape[:-1] + (1,))
            s = jnp.concatenate([s, s_sink], axis=-1)
        p = jax.nn.softmax(s, axis=-1)
        if sink is not None:
            p = p[..., :-1]
        return jnp.einsum('bngqk,bknd->bqngd', p.astype(v.dtype), v)

    out = lax.map(one_block, qb)
    return jnp.moveaxis(out, 0, 1).reshape(B, Lq, -1)


def _mla_project(h, p, tables):
    B, L, _ = h.shape
    down = h @ p['w_down']
    cq = _rmsnorm(down[..., :MLA_Q_RANK], p['g_q'])
    ckv = _rmsnorm(down[..., MLA_Q_RANK:MLA_Q_RANK + MLA_KV_RANK], p['g_kv'])
    kpe = down[..., MLA_Q_RANK + MLA_KV_RANK:][:, :, None, :]
    q = (cq @ p['w_uq']).reshape(B, L, MLA_HEADS, MLA_NOPE + MLA_ROPE)
    if tables is not None:
        q = jnp.concatenate([q[..., :MLA_NOPE], _apply_axial_rope(q[..., MLA_NOPE:], tables)], axis=-1)
        kpe = _apply_axial_rope(kpe, tables)
    return q, ckv, kpe[:, :, 0, :]


def _mla_attend(q, ckv, kpe, p):
    B, Lk, _ = ckv.shape
    kv = (ckv @ p['w_ukv']).reshape(B, Lk, MLA_HEADS, MLA_NOPE + MLA_V)
    k = jnp.concatenate([kv[..., :MLA_NOPE], jnp.broadcast_to(kpe[:, :, None, :], (B, Lk, MLA_HEADS, MLA_ROPE))], axis=-1)
    v = kv[..., MLA_NOPE:]
    o = _blocked_attention(q[:, :, :, None, :], k, v, (MLA_NOPE + MLA_ROPE) ** -0.5)
    return o @ p['w_o']


def _ssm_combine(left, right):
    a_l, b_l = left
    a_r, b_r = right
    return a_l * a_r, a_r * b_l + b_r


def _s5_mixer(h, p, init_re=None, init_im=None):
    B, L, _ = h.shape
    u = (h @ p['w_in']).astype(F32)
    u_c = u.reshape(B, L, S5_GROUPS, S5_GROUP).astype(jnp.complex64)
    y = p['d'].astype(F32) * u
    finals = []
    for d in range(2):
        lam = lax.complex(p['lam_re'][d].astype(F32), p['lam_im'][d].astype(F32))
        dt = jnp.exp(p['log_dt'][d].astype(F32))[:, None]
        a_bar = jnp.exp(lam * dt)
        b_mat = lax.complex(p['b_re'][d].astype(F32), p['b_im'][d].astype(F32))
        b_bar = ((a_bar - 1.0) / lam)[..., None] * b_mat
        c_mat = lax.complex(p['c_re'][d].astype(F32), p['c_im'][d].astype(F32))
        bu = jnp.einsum('gpc,blgc->blgp', b_bar, u_c)
        if init_re is not None:
            h0 = lax.complex(init_re[:, d].astype(F32), init_im[:, d].astype(F32))
            first = L - 1 if d == 1 else 0
            bu = bu.at[:, first].add(a_bar[None] * h0)
        a_seq = jnp.broadcast_to(a_bar, (1, L) + a_bar.shape)
        _, states = lax.associative_scan(_ssm_combine, (a_seq, bu), axis=1, reverse=(d == 1))
        y = y + jnp.real(jnp.einsum('gcp,blgp->blgc', c_mat, states)).reshape(B, L, D_MODEL)
        finals.append(states[:, 0] if d == 1 else states[:, -1])
    z = jax.nn.gelu(y).astype(h.dtype)
    out = (z * jax.nn.sigmoid(z @ p['w_gate'])) @ p['w_out']
    return out, finals


def _hyena_filter_spectra(L, p):
    t = jnp.arange(L, dtype=F32) / L
    bands = jnp.arange(1, HY_BANDS + 1, dtype=F32)
    ang = 2.0 * math.pi * t[:, None] * bands[None, :]
    feats = jnp.concatenate([t[:, None], jnp.sin(ang), jnp.cos(ang)], axis=-1)
    freq = p['f_freq'].astype(F32)
    z = jnp.sin(freq[0] * (feats @ p['f_w1'].astype(F32) + p['f_b1'].astype(F32)))
    z = jnp.sin(freq[1] * (z @ p['f_w2'].astype(F32) + p['f_b2'].astype(F32)))
    filt = (z @ p['f_w3'].astype(F32)).reshape(L, 2, HY_ORDER, D_MODEL)
    filt = filt * jnp.exp(-t[:, None, None, None] * jnp.exp(p['log_decay'].astype(F32))[None])
    fwd = filt[:, 0]
    bwd = filt[1:, 1]
    two_sided = jnp.concatenate([fwd, jnp.zeros((1, HY_ORDER, D_MODEL), F32), bwd[::-1]], axis=0)
    return jnp.fft.rfft(two_sided, axis=0)


def _fft_long_conv(u, k_f, bias):
    L = u.shape[1]
    u_f = jnp.fft.rfft(u, n=2 * L, axis=1)
    y = jnp.fft.irfft(u_f * k_f[None], n=2 * L, axis=1)[:, :L]
    return y + u * bias


def _hyena_mixer(h, p):
    z = h @ p['w_in'] + p['b_in']
    zp = jnp.pad(z, ((0, 0), (1, 1), (0, 0)))
    w = p['short_w']
    z = zp[:, :-2] * w[0] + zp[:, 1:-1] * w[1] + zp[:, 2:] * w[2] + p['short_b']
    x1, x2, v = jnp.split(z.astype(F32), 3, axis=-1)
    k_f = _hyena_filter_spectra(h.shape[1], p)
    bias = p['bias'].astype(F32)
    v = x1 * _fft_long_conv(v, k_f[:, 0], bias[0])
    v = x2 * _fft_long_conv(v, k_f[:, 1], bias[1])
    return v.astype(h.dtype) @ p['w_out']


def _swa_project(h, p, tables):
    B, L, _ = h.shape
    nq = SWA_HEADS * SWA_HEAD_DIM
    nk = SWA_KV_HEADS * SWA_HEAD_DIM
    qkv = h @ p['w_qkv']
    q = qkv[..., :nq].reshape(B, L, SWA_HEADS, SWA_HEAD_DIM)
    k = qkv[..., nq:nq + nk].reshape(B, L, SWA_KV_HEADS, SWA_HEAD_DIM)
    v = qkv[..., nq + nk:].reshape(B, L, SWA_KV_HEADS, SWA_HEAD_DIM)
    if tables is not None:
        q = _apply_axial_rope(q, tables)
        k = _apply_axial_rope(k, tables)
    return q.reshape(B, L, SWA_KV_HEADS, SWA_GROUP, SWA_HEAD_DIM), k, v


def _banded_attention(q, k, v, k_ctx, v_ctx, sink, scale):
    B, L, KV, G, d = q.shape
    nb = L // SWA_BLOCK
    qb = q.reshape(B, nb, SWA_BLOCK, KV, G, d)

    def neighbourhood(t):
        tp = jnp.pad(t, ((0, 0), (SWA_BLOCK, SWA_BLOCK), (0, 0), (0, 0))).reshape(B, nb + 2, SWA_BLOCK, KV, d)
        return jnp.concatenate([tp[:, :-2], tp[:, 1:-1], tp[:, 2:]], axis=2)

    kb = neighbourhood(k)
    vb = neighbourhood(v)
    qpos = jnp.arange(L).reshape(nb, SWA_BLOCK)
    kpos = (jnp.arange(nb)[:, None] - 1) * SWA_BLOCK + jnp.arange(3 * SWA_BLOCK)[None, :]
    valid = ((jnp.abs(qpos[:, :, None] - kpos[:, None, :]) <= SWA_WINDOW)
             & (kpos[:, None, :] >= 0) & (kpos[:, None, :] < L))
    s_loc = jnp.einsum('bnqhgd,bnkhd->bnhgqk', qb, kb).astype(F32) * scale
    s_loc = jnp.where(valid[None, :, None, None], s_loc, NEG_INF)
    s_ctx = jnp.einsum('bnqhgd,bkhd->bnhgqk', qb, k_ctx).astype(F32) * scale
    s_sink = jnp.broadcast_to(sink.astype(F32).reshape(1, 1, KV, G, 1, 1), s_loc.shape[:-1] + (1,))
    p = jax.nn.softmax(jnp.concatenate([s_loc, s_ctx, s_sink], axis=-1), axis=-1)
    n_loc = 3 * SWA_BLOCK
    n_ctx = k_ctx.shape[1]
    o = (jnp.einsum('bnhgqk,bnkhd->bnqhgd', p[..., :n_loc].astype(v.dtype), vb)
         + jnp.einsum('bnhgqk,bkhd->bnqhgd', p[..., n_loc:n_loc + n_ctx].astype(v.dtype), v_ctx))
    return o.reshape(B, L, KV * G * d)


def _peer(x, p):
    B, L, D = x.shape
    T = B * L
    xf = x.reshape(T, D)
    q = (xf @ p['w_q']).reshape(T, PEER_HEADS, 2, PEER_KEY_DIM // 2)
    s = jnp.einsum('thpk,hpnk->thpn', q, p['keys']).astype(F32)
    sv, si = lax.top_k(s, PEER_TOPK)
    cand_s = (sv[:, :, 0, :, None] + sv[:, :, 1, None, :]).reshape(T, PEER_HEADS, PEER_TOPK * PEER_TOPK)
    cand_i = (si[:, :, 0, :, None] * PEER_N_KEYS + si[:, :, 1, None, :]).reshape(T, PEER_HEADS, PEER_TOPK * PEER_TOPK)
    top_s, top_pos = lax.top_k(cand_s, PEER_TOPK)
    eid = jnp.take_along_axis(cand_i, top_pos, axis=-1).reshape(T, PEER_HEADS * PEER_TOPK)
    gate = jax.nn.softmax(top_s, axis=-1).reshape(T, PEER_HEADS * PEER_TOPK)
    n_chunks = T // PEER_CHUNK
    u_tab = p['u']
    v_tab = p['v']

    def chunk(args):
        xc, ec, gc = args
        a = jnp.einsum('cd,ced->ce', xc, u_tab[ec]).astype(F32)
        w = (jax.nn.gelu(a) * gc).astype(xc.dtype)
        return jnp.einsum('ce,ced->cd', w, v_tab[ec])

    out = lax.map(chunk, (xf.reshape(n_chunks, PEER_CHUNK, D),
                          eid.reshape(n_chunks, PEER_CHUNK, -1),
                          gate.reshape(n_chunks, PEER_CHUNK, -1)))
    return out.reshape(B, L, D)


def _trunk(x, cond, W, cache=None):
    L = x.shape[1]
    latent = cache is not None
    rope_mla = _axial_rope_tables(L, MLA_ROPE) if latent else None
    rope_swa = _axial_rope_tables(L, SWA_HEAD_DIM) if latent else None
    new = {'mla_ckv': [], 'mla_kpe': [], 's5_re': [], 's5_im': [], 'swa_k': [], 'swa_v': []}
    cond_act = jax.nn.silu(cond)
    for i in range(DEPTH):
        kind, j = i % N_MIXERS, i // N_MIXERS
        mod = (cond_act @ W['mod_w'][i] + W['mod_b'][i])[:, None, :]
        sh1, sc1, g1, sh2, sc2, g2 = jnp.split(mod, 6, axis=-1)
        h = _rmsnorm(x, W['norm1_g'][i]) * (1.0 + sc1) + sh1
        if kind == 0:
            p = {name: arr[j] for name, arr in W['mla'].items()}
            q, ckv, kpe = _mla_project(h, p, rope_mla)
            if latent:
                ckv = jnp.concatenate([ckv, cache['mla_ckv'][:, j]], axis=1)
                kpe = jnp.concatenate([kpe, cache['mla_kpe'][:, j]], axis=1)
            else:
                new['mla_ckv'].append(ckv)
                new['mla_kpe'].append(kpe)
            y = _mla_attend(q, ckv, kpe, p)
        elif kind == 1:
            p = {name: arr[j] for name, arr in W['s5'].items()}
            if latent:
                y, _ = _s5_mixer(h, p, cache['s5_re'][:, j], cache['s5_im'][:, j])
            else:
                y, finals = _s5_mixer(h, p)
                fin = jnp.stack(finals, axis=1)
                new['s5_re'].append(jnp.real(fin))
                new['s5_im'].append(jnp.imag(fin))
        elif kind == 2:
            p = {name: arr[j] for name, arr in W['hy'].items()}
            y = _hyena_mixer(h, p)
        else:
            p = {name: arr[j] for name, arr in W['swa'].items()}
            q, k, v = _swa_project(h, p, rope_swa)
            scale = SWA_HEAD_DIM ** -0.5
            if latent:
                o = _banded_attention(q, k, v, cache['swa_k'][:, j], cache['swa_v'][:, j], p['sink'], scale)
            else:
                o = _blocked_attention(q, k, v, scale, p['sink'])
                new['swa_k'].append(k)
                new['swa_v'].append(v)
            y = o @ p['w_o']
        x = x + g1 * y
        h = _rmsnorm(x, W['norm2_g'][i]) * (1.0 + sc2) + sh2
        x = x + g2 * _peer(h, {name: arr[i] for name, arr in W['peer'].items()})
    return _rmsnorm(x, W['final_g']), new


def setup_inputs(seed: int = 0):
    key = jax.random.key(seed)
    keys = iter(jax.random.split(key, 64))

    def nrm(shape, scale):
        return jax.random.normal(next(keys), shape, F32) * scale

    D = D_MODEL
    n_mla, n_s5, n_hy, n_swa = [_layers_using(kind) for kind in range(N_MIXERS)]
    G, P = S5_GROUPS, S5_STATE
    return {
        'x_prompt': nrm((BATCH, SEQ, D), 1.0),
        'x_sample': nrm((DEC_BATCH, DEC_SEQ, D), 1.0),
        'c': nrm((DEC_BATCH, D), 1.0),
        'cache_mla_ckv': nrm((DEC_BATCH, n_mla, PAST_LEN, MLA_KV_RANK), 1.0),
        'cache_mla_kpe': nrm((DEC_BATCH, n_mla, PAST_LEN, MLA_ROPE), 1.0),
        'state_s5_re': nrm((DEC_BATCH, n_s5, 2, G, P), 0.1),
        'state_s5_im': nrm((DEC_BATCH, n_s5, 2, G, P), 0.1),
        'cache_swa_k': nrm((DEC_BATCH, n_swa, PAST_LEN, SWA_KV_HEADS, SWA_HEAD_DIM), 1.0),
        'cache_swa_v': nrm((DEC_BATCH, n_swa, PAST_LEN, SWA_KV_HEADS, SWA_HEAD_DIM), 1.0),
        'c_ctx': nrm((D,), 1.0),
        'mod_w': nrm((DEPTH, D, 6 * D), 0.5 * D ** -0.5),
        'mod_b': nrm((DEPTH, 6 * D), 0.02),
        'norm1_g': 1.0 + nrm((DEPTH, D), 0.02),
        'norm2_g': 1.0 + nrm((DEPTH, D), 0.02),
        'final_g': 1.0 + nrm((D,), 0.02),
        'mla_w_down': nrm((n_mla, D, MLA_Q_RANK + MLA_KV_RANK + MLA_ROPE), D ** -0.5),
        'mla_g_q': 1.0 + nrm((n_mla, MLA_Q_RANK), 0.02),
        'mla_g_kv': 1.0 + nrm((n_mla, MLA_KV_RANK), 0.02),
        'mla_w_uq': nrm((n_mla, MLA_Q_RANK, MLA_HEADS * (MLA_NOPE + MLA_ROPE)), MLA_Q_RANK ** -0.5),
        'mla_w_ukv': nrm((n_mla, MLA_KV_RANK, MLA_HEADS * (MLA_NOPE + MLA_V)), MLA_KV_RANK ** -0.5),
        'mla_w_o': nrm((n_mla, MLA_HEADS * MLA_V, D), (MLA_HEADS * MLA_V) ** -0.5),
        's5_w_in': nrm((n_s5, D, D), D ** -0.5),
        's5_lam_re': -0.5 + nrm((n_s5, 2, G, P), 0.01),
        's5_lam_im': math.pi * jnp.arange(P, dtype=F32) + nrm((n_s5, 2, G, P), 0.01),
        's5_log_dt': jax.random.uniform(next(keys), (n_s5, 2, G), F32, math.log(S5_DT_MIN), math.log(S5_DT_MAX)),
        's5_b_re': nrm((n_s5, 2, G, P, S5_GROUP), (2 * S5_GROUP) ** -0.5),
        's5_b_im': nrm((n_s5, 2, G, P, S5_GROUP), (2 * S5_GROUP) ** -0.5),
        's5_c_re': nrm((n_s5, 2, G, S5_GROUP, P), P ** -0.5),
        's5_c_im': nrm((n_s5, 2, G, S5_GROUP, P), P ** -0.5),
        's5_d': nrm((n_s5, D), 0.5),
        's5_w_gate': nrm((n_s5, D, D), D ** -0.5),
        's5_w_out': nrm((n_s5, D, D), D ** -0.5),
        'hy_w_in': nrm((n_hy, D, 3 * D), D ** -0.5),
        'hy_b_in': nrm((n_hy, 3 * D), 0.02),
        'hy_short_w': nrm((n_hy, 3, 3 * D), 3 ** -0.5),
        'hy_short_b': nrm((n_hy, 3 * D), 0.02),
        'hy_f_w1': nrm((n_hy, HY_EMB, HY_HIDDEN), HY_EMB ** -0.5),
        'hy_f_b1': nrm((n_hy, HY_HIDDEN), 0.5),
        'hy_f_freq': 1.0 + nrm((n_hy, 2, HY_HIDDEN), 0.1),
        'hy_f_w2': nrm((n_hy, HY_HIDDEN, HY_HIDDEN), HY_HIDDEN ** -0.5),
        'hy_f_b2': nrm((n_hy, HY_HIDDEN), 0.5),
        'hy_f_w3': nrm((n_hy, HY_HIDDEN, 2 * HY_ORDER * D), 0.005),
        'hy_log_decay': jnp.log(jnp.linspace(HY_MIN_DECAY, HY_MAX_DECAY, D, dtype=F32)) + nrm((n_hy, 2, HY_ORDER, D), 0.01),
        'hy_bias': nrm((n_hy, HY_ORDER, D), 1.0),
        'hy_w_out': nrm((n_hy, D, D), D ** -0.5),
        'swa_w_qkv': nrm((n_swa, D, (SWA_HEADS + 2 * SWA_KV_HEADS) * SWA_HEAD_DIM), D ** -0.5),
        'swa_w_o': nrm((n_swa, SWA_HEADS * SWA_HEAD_DIM, D), (SWA_HEADS * SWA_HEAD_DIM) ** -0.5),
        'swa_sink': nrm((n_swa, SWA_HEADS), 0.5),
        'peer_w_q': nrm((DEPTH, D, PEER_HEADS * PEER_KEY_DIM), D ** -0.5),
        'peer_keys': nrm((DEPTH, PEER_HEADS, 2, PEER_N_KEYS, PEER_KEY_DIM // 2), (PEER_KEY_DIM // 2) ** -0.5),
        'peer_u': nrm((DEPTH, PEER_EXPERTS, D), D ** -0.5),
        'peer_v': nrm((DEPTH, PEER_EXPERTS, D), PEER_HEADS ** -0.5),
    }


def reference(x_prompt, x_sample, c, cache_mla_ckv, cache_mla_kpe, state_s5_re, state_s5_im,
              cache_swa_k, cache_swa_v, c_ctx, mod_w, mod_b, norm1_g, norm2_g, final_g,
              mla_w_down, mla_g_q, mla_g_kv, mla_w_uq, mla_w_ukv, mla_w_o,
              s5_w_in, s5_lam_re, s5_lam_im, s5_log_dt, s5_b_re, s5_b_im, s5_c_re, s5_c_im,
              s5_d, s5_w_gate, s5_w_out,
              hy_w_in, hy_b_in, hy_short_w, hy_short_b, hy_f_w1, hy_f_b1, hy_f_freq, hy_f_w2,
              hy_f_b2, hy_f_w3, hy_log_decay, hy_bias, hy_w_out,
              swa_w_qkv, swa_w_o, swa_sink,
              peer_w_q, peer_keys, peer_u, peer_v):
    W = {
        'mod_w': mod_w, 'mod_b': mod_b, 'norm1_g': norm1_g, 'norm2_g': norm2_g, 'final_g': final_g,
        'mla': {'w_down': mla_w_down, 'g_q': mla_g_q, 'g_kv': mla_g_kv, 'w_uq': mla_w_uq,
                'w_ukv': mla_w_ukv, 'w_o': mla_w_o},
        's5': {'w_in': s5_w_in, 'lam_re': s5_lam_re, 'lam_im': s5_lam_im, 'log_dt': s5_log_dt,
               'b_re': s5_b_re, 'b_im': s5_b_im, 'c_re': s5_c_re, 'c_im': s5_c_im, 'd': s5_d,
               'w_gate': s5_w_gate, 'w_out': s5_w_out},
        'hy': {'w_in': hy_w_in, 'b_in': hy_b_in, 'short_w': hy_short_w, 'short_b': hy_short_b,
               'f_w1': hy_f_w1, 'f_b1': hy_f_b1, 'f_freq': hy_f_freq, 'f_w2': hy_f_w2, 'f_b2': hy_f_b2,
               'f_w3': hy_f_w3, 'log_decay': hy_log_decay, 'bias': hy_bias, 'w_out': hy_w_out},
        'swa': {'w_qkv': swa_w_qkv, 'w_o': swa_w_o, 'sink': swa_sink},
        'peer': {'w_q': peer_w_q, 'keys': peer_keys, 'u': peer_u, 'v': peer_v},
    }
    y_prompt, st = _trunk(x_prompt, c_ctx[None, :], W, None)
    new_mla_ckv = jnp.stack(st['mla_ckv'], axis=1)
    new_mla_kpe = jnp.stack(st['mla_kpe'], axis=1)
    new_s5_re = jnp.stack(st['s5_re'], axis=1)
    new_s5_im = jnp.stack(st['s5_im'], axis=1)
    new_swa_k = jnp.stack(st['swa_k'], axis=1)
    new_swa_v = jnp.stack(st['swa_v'], axis=1)
    cache = {'mla_ckv': cache_mla_ckv, 'mla_kpe': cache_mla_kpe, 's5_re': state_s5_re,
             's5_im': state_s5_im, 'swa_k': cache_swa_k, 'swa_v': cache_swa_v}
    y_sample, _ = _trunk(x_sample, c, W, cache)
    return (y_prompt, y_sample, new_mla_ckv, new_mla_kpe, new_s5_re, new_s5_im, new_swa_k, new_swa_v)
```

```python
import math
import numpy as np
import concourse.bass as bass
import concourse.mybir as mybir

F32 = mybir.dt.float32
BF16 = mybir.dt.bfloat16
I32 = mybir.dt.int32
U32 = mybir.dt.uint32
AF = mybir.ActivationFunctionType
ALU = mybir.AluOpType
AX = mybir.AxisListType


DEBUG_SEM = False


class Buf:
    __slots__ = ("name", "w", "r", "dsem", "sb")

    def __init__(self, name, sb=True):
        self.name = name
        self.sb = sb
        self.w = {}
        self.r = {}
        self.dsem = None


class Sched:
    ENGS = ("pe", "act", "dve", "pool", "sp")

    def __init__(self, nc):
        self.nc = nc
        self.eng = {"pe": nc.tensor, "act": nc.scalar, "dve": nc.vector, "pool": nc.gpsimd, "sp": nc.sync}
        self.sem = {e: nc.alloc_semaphore("S_" + e) for e in self.ENGS}
        self.cnt = {e: 0 for e in self.ENGS}
        self.seen = {e: {} for e in self.ENGS}
        self.semobj = {}
        for e in self.ENGS:
            self.semobj[id(self.sem[e])] = self.sem[e]
        self.dpool = {"sw": [], "hw": []}
        self.dcnt = {}
        self.dused = []
        self.n_inst = 0

    def _get_dsem(self, kind):
        if self.dpool[kind]:
            s = self.dpool[kind].pop()
        else:
            s = self.nc.alloc_semaphore("D%d" % len(self.dcnt))
            self.dcnt[id(s)] = 0
            self.semobj[id(s)] = s
        self.dused.append((kind, s))
        return s

    def _wait(self, e, deps):
        seen = self.seen[e]
        for sid, val in deps.items():
            if seen.get(sid, 0) < val:
                self.eng[e].wait_ge(self.semobj[sid], val)
                seen[sid] = val
                self.n_inst += 1

    @staticmethod
    def _merge(dst, src):
        for k, v in src.items():
            if dst.get(k, 0) < v:
                dst[k] = v

    def op(self, e, fn, reads=(), writes=()):
        deps = {}
        for b in reads:
            self._merge(deps, b.w)
        for b in writes:
            self._merge(deps, b.w)
            self._merge(deps, b.r)
        self._wait(e, deps)
        ins = fn()
        self.cnt[e] += 1
        ins.then_inc(self.sem[e], 1)
        self.n_inst += 1
        tok = {id(self.sem[e]): self.cnt[e]}
        self.seen[e][id(self.sem[e])] = max(self.seen[e].get(id(self.sem[e]), 0), 0)
        for b in writes:
            b.w = dict(tok)
            b.r = {}
        for b in reads:
            if b not in writes:
                self._merge(b.r, tok)
        return ins

    def mm_group(self, fns, reads=(), writes=()):
        deps = {}
        for b in reads:
            self._merge(deps, b.w)
        for b in writes:
            self._merge(deps, b.w)
            self._merge(deps, b.r)
        deps.pop(id(self.sem["pe"]), None)
        self._wait("pe", deps)
        ins = None
        for fn in fns:
            ins = fn()
            self.n_inst += 1
        self.cnt["pe"] += 1
        ins.then_inc(self.sem["pe"], 1)
        tok = {id(self.sem["pe"]): self.cnt["pe"]}
        for b in writes:
            b.w = dict(tok)
            b.r = {}
        for b in reads:
            if b not in writes:
                self._merge(b.r, tok)

    def dma(self, q, out_ap, in_ap, wbuf, rbuf, indirect=None, owner=None, **kw):
        if owner is None:
            owner = wbuf if (wbuf.sb or rbuf is None or not rbuf.sb) else rbuf
        kind = "sw" if q == "pool" else "hw"
        if owner.dsem is None:
            owner.dsem = {}
        if kind not in owner.dsem:
            owner.dsem[kind] = self._get_dsem(kind)
        ds = owner.dsem[kind]
        deps = {}
        if rbuf is not None:
            self._merge(deps, rbuf.w)
        for k, v in wbuf.w.items():
            if k not in self.dcnt:
                if deps.get(k, 0) < v:
                    deps[k] = v
        if wbuf.sb:
            for k, v in wbuf.w.items():
                if k in self.dcnt and k != id(ds) and deps.get(k, 0) < v:
                    deps[k] = v
        self._merge(deps, wbuf.r)
        if indirect is not None and indirect.get("idx_buf") is not None:
            self._merge(deps, indirect["idx_buf"].w)
        self._wait(q, deps)
        if indirect is None:
            ins = self.eng[q].dma_start(out=out_ap, in_=in_ap, **kw)
        else:
            ins = self.eng[q].indirect_dma_start(out=out_ap, out_offset=indirect.get("out_offset"),
                                                 in_=in_ap, in_offset=indirect.get("in_offset"), **kw)
        self.dcnt[id(ds)] += 16
        ins.then_inc(ds, 16)
        self.n_inst += 1
        tok = {id(ds): self.dcnt[id(ds)]}
        self._merge(wbuf.w, tok)
        if rbuf is not None:
            self._merge(rbuf.r, tok)
        if indirect is not None and indirect.get("idx_buf") is not None:
            self._merge(indirect["idx_buf"].r, tok)
        return ins

    def barrier(self, bufs=()):
        deps = {}
        for e in self.ENGS:
            if self.cnt[e] > 0:
                deps[id(self.sem[e])] = self.cnt[e]
        for _, s in self.dused:
            if self.dcnt[id(s)] > 0:
                deps[id(s)] = self.dcnt[id(s)]
        for e in self.ENGS:
            d = {k: v for k, v in deps.items() if k != id(self.sem[e]) or True}
            self._wait(e, d)
        for b in bufs:
            b.w = {}
            b.r = {}
            b.dsem = None
        for kind, s in self.dused:
            self.dpool[kind].append(s)
        self.dused = []


from contextlib import contextmanager, ExitStack


class Ctx:
    def __init__(self, nc):
        self.nc = nc
        self.S = Sched(nc)
        self.bufs = []
        self._st = None
        self._uid = 0
        self.ident = None

    def buf(self, name="b", sb=True):
        b = Buf(name, sb)
        self.bufs.append(b)
        return b

    def uid(self, name):
        self._uid += 1
        return "%s_%d" % (name, self._uid)

    @contextmanager
    def phase(self, name="ph"):
        st = ExitStack()
        old = self._st
        self._st = st
        nb = len(self.bufs)
        issue_casts(self, k=3)
        try:
            yield
        finally:
            pass
        self.S.barrier(self.bufs)
        del self.bufs[nb:]
        st.close()
        self._st = old

    def sb(self, name, shape, dtype=F32):
        t = self._st.enter_context(self.nc.sbuf_tensor(self.uid(name), list(shape), dtype))
        return t

    def ps(self, name, shape, dtype=F32):
        t = self._st.enter_context(self.nc.psum_tensor(self.uid(name), list(shape), dtype))
        return t

    def dram(self, name, shape, dtype=F32):
        t = self.nc.dram_tensor(self.uid(name), list(shape), dtype)
        return t.ap()

    def make_ident(self):
        nc, S = self.nc, self.S
        self.ident_t = nc.alloc_sbuf_tensor("ident", [128, 128], F32)
        self.ident = self.ident_t
        b = self.buf("ident")
        self.ident_b = b
        S.op("pool", lambda: nc.gpsimd.memset(self.ident_t[:], 0.0), writes=[b])
        S.op("pool", lambda: nc.gpsimd.affine_select(out=self.ident_t[:], in_=self.ident_t[:], pattern=[[-1, 128]],
                                                      compare_op=ALU.not_equal, fill=1.0, base=0,
                                                      channel_multiplier=1), reads=[b], writes=[b])
        S.barrier(self.bufs)


def cdiv(a, b):
    return (a + b - 1) // b


def linear(C, xT, T, K, wblocks, out_ap, mode="tm", act=None, bias_ap=None, mul_ap=None, tg=512, xbuf=None, obuf=None,
           out_col0=0):
    nc, S = C.nc, C.S
    KC = cdiv(K, 128)
    kp = min(K, 128)
    assert K % kp == 0
    with C.phase("lin"):
        xb = xbuf if xbuf is not None else C.buf("x", sb=False)
        ob = obuf if obuf is not None else C.buf("o", sb=False)
        wsrc = C.buf("wsrc", sb=False)
        nblk = len(wblocks)
        maxnb = max(nb for _, nb in wblocks)
        xt = [C.sb("xt%d" % i, [kp, KC, tg]) for i in range(2)]
        xtb = [C.buf("xt") for i in range(2)]
        wt = [C.sb("wt%d" % i, [kp, KC, maxnb]) for i in range(2)]
        wtb = [C.buf("wt") for i in range(2)]
        if mode == "tm":
            pst = [C.ps("ps%d" % i, [128, 512]) for i in range(4)]
            osb = [C.sb("osb%d" % i, [128, 512]) for i in range(3)]
        else:
            pst = [C.ps("ps%d" % i, [128, tg]) for i in range(4)]
            osb = [C.sb("osb%d" % i, [128, tg]) for i in range(3)]
        psb = [C.buf("ps") for i in range(4)]
        osbb = [C.buf("osb") for i in range(3)]
        mulsb = None
        if mul_ap is not None:
            mulsb = [C.sb("mul%d" % i, [128, tg if mode == "fm" else 512]) for i in range(2)]
            mulb = [C.buf("mul") for i in range(2)]
        biast = None
        if bias_ap is not None:
            if mode == "tm":
                Ntot = sum(nb for _, nb in wblocks)
                biast = C.sb("bias", [128, Ntot])
                biasb = C.buf("bias")
                S.dma("sp", biast[:], bias_ap.partition_broadcast(128), biasb, None)
            else:
                Ntot = sum(nb for _, nb in wblocks)
                biast = C.sb("bias", [128, cdiv(Ntot, 128)])
                biasb = C.buf("bias")
                S.dma("sp", biast[:], bias_ap.rearrange("(c p) -> p c", p=128), biasb, None,
                      allow_slow_non_contiguous=True)
        ngroups = cdiv(T, tg)
        it = 0
        pi = 0
        oi = 0
        mi = 0
        for g in range(ngroups):
            t0 = g * tg
            tn = min(tg, T - t0)
            xs = g % 2
            S.dma("sp", xt[xs][:, :, :tn], xT[:, t0:t0 + tn].rearrange("(c p) t -> p c t", p=kp), xtb[xs], xb)
            col = 0
            for bi, (pieces, nb) in enumerate(wblocks):
                ws = it % 2
                it += 1
                for (co, wap) in pieces:
                    n = wap.shape[-1]
                    S.dma("sp", wt[ws][:, :, co:co + n], wap.rearrange("(c p) n -> p c n", p=kp), wtb[ws], wsrc)
                if mode == "tm":
                    for tt in range(cdiv(tn, 128)):
                        tsz = min(128, tn - tt * 128)
                        p = pi % 4
                        pi += 1
                        fns = []
                        for kc in range(KC):
                            fns.append(lambda kc=kc, p=p, tt=tt, tsz=tsz, xs=xs, ws=ws, nb=nb: nc.tensor.matmul(
                                pst[p][:tsz, :nb], lhsT=xt[xs][:, kc, tt * 128:tt * 128 + tsz], rhs=wt[ws][:, kc, :nb],
                                start=(kc == 0), stop=(kc == KC - 1)))
                        S.mm_group(fns, reads=[xtb[xs], wtb[ws]], writes=[psb[p]])
                        o = oi % 3
                        oi += 1
                        _epilogue(C, pst[p][:tsz, :nb], psb[p], osb[o][:tsz, :nb], osbb[o], act,
                                  None if biast is None else biast[:tsz, col:col + nb], biasb if biast is not None else None,
                                  None, None)
                        if mul_ap is not None:
                            m = mi % 2
                            mi += 1
                            S.dma("sp", mulsb[m][:tsz, :nb], mul_ap[t0 + tt * 128:t0 + tt * 128 + tsz, col:col + nb], mulb[m], None)
                            S.op("dve", lambda o=o, m=m, tsz=tsz, nb=nb: nc.vector.tensor_tensor(
                                out=osb[o][:tsz, :nb], in0=osb[o][:tsz, :nb], in1=mulsb[m][:tsz, :nb], op=ALU.mult),
                                reads=[mulb[m]], writes=[osbb[o]])
                        S.dma("pool", out_ap[t0 + tt * 128:t0 + tt * 128 + tsz, out_col0 + col:out_col0 + col + nb],
                              osb[o][:tsz, :nb], ob, osbb[o])
                else:
                    assert nb <= 128
                    p = pi % 4
                    pi += 1
                    fns = []
                    for kc in range(KC):
                        fns.append(lambda kc=kc, p=p, xs=xs, ws=ws, nb=nb, tn=tn: nc.tensor.matmul(
                            pst[p][:nb, :tn], lhsT=wt[ws][:, kc, :nb], rhs=xt[xs][:, kc, :tn],
                            start=(kc == 0), stop=(kc == KC - 1)))
                    S.mm_group(fns, reads=[xtb[xs], wtb[ws]], writes=[psb[p]])
                    o = oi % 3
                    oi += 1
                    bcol = None
                    if biast is not None:
                        assert col % 128 == 0
                        bcol = biast[:nb, col // 128:col // 128 + 1]
                    _epilogue(C, pst[p][:nb, :tn], psb[p], osb[o][:nb, :tn], osbb[o], act, None, None,
                              bcol, biasb if biast is not None else None)
                    if mul_ap is not None:
                        m = mi % 2
                        mi += 1
                        S.dma("sp", mulsb[m][:nb, :tn], mul_ap[out_col0 + col:out_col0 + col + nb, t0:t0 + tn], mulb[m], None)
                        S.op("dve", lambda o=o, m=m, tn=tn, nb=nb: nc.vector.tensor_tensor(
                            out=osb[o][:nb, :tn], in0=osb[o][:nb, :tn], in1=mulsb[m][:nb, :tn], op=ALU.mult),
                            reads=[mulb[m]], writes=[osbb[o]])
                    S.dma("pool", out_ap[out_col0 + col:out_col0 + col + nb, t0:t0 + tn], osb[o][:nb, :tn], ob, osbb[o])
                col += nb


def linear_bf(C, xT, T, K, wblocks, out_ap, mode="tm", act=None, bias_ap=None, mul_ap=None, out_col0=0):
    nc, S = C.nc, C.S
    KC = cdiv(K, 128)
    kp = min(K, 128)
    assert K % kp == 0
    TG = 512
    with C.phase("linbf"):
        xb = C.buf("x", sb=False)
        ob = C.buf("o", sb=False)
        wsrc = C.buf("wsrc", sb=False)
        ngroups = cdiv(T, TG)
        xres = C.sb("xres", [kp, KC, T], BF16)
        xrb = [C.buf("xres") for g in range(ngroups)]
        for g in range(ngroups):
            t0 = g * TG
            tn = min(TG, T - t0)
            S.dma("pool", xres[:, :, t0:t0 + tn], xT[:, t0:t0 + tn].rearrange("(c p) t -> p c t", p=kp), xrb[g], xb)
        maxnb = max(nb for _, nb in wblocks)
        wr = Rot(C, "wbf", [kp, KC, maxnb], 3, dtype=BF16)
        pw = 512
        pst = Rot(C, "ps", [128, pw], 4, psum=True)
        osb = Rot(C, "osb", [128, pw], 3)
        mulr = Rot(C, "mul", [128, pw], 2) if mul_ap is not None else None
        biast = None
        biasb = None
        Ntot = sum(nb for _, nb in wblocks)
        if bias_ap is not None:
            if mode == "tm":
                biast = C.sb("bias", [128, Ntot])
                biasb = C.buf("bias")
                S.dma("sp", biast[:], bias_ap.partition_broadcast(128), biasb, None)
            else:
                biast = C.sb("bias", [128, cdiv(Ntot, 128)])
                biasb = C.buf("bias")
                S.dma("sp", biast[:], bias_ap.rearrange("(c p) -> p c", p=128), biasb, None,
                      allow_slow_non_contiguous=True)

        def load_w(bi):
            pieces, nb = wblocks[bi]
            w, wb = wr.next()
            for (co, wap) in pieces:
                n = wap.shape[-1]
                S.dma("pool", w[:, :, co:co + n], wap.rearrange("(c p) n -> p c n", p=kp), wb, wsrc)
            return w, wb
        nxt = load_w(0)
        col = 0
        for bi, (pieces, nb) in enumerate(wblocks):
            w, wb = nxt
            if bi + 1 < len(wblocks):
                nxt = load_w(bi + 1)
            if mode == "tm":
                for tt in range(0, T, 128):
                    tsz = min(128, T - tt)
                    g = tt // TG
                    p, pb = pst.next()
                    fns = []
                    for kc in range(KC):
                        fns.append(lambda kc=kc, p=p, tt=tt, tsz=tsz, w=w, nb=nb: nc.tensor.matmul(
                            p[:tsz, :nb], lhsT=xres[:, kc, tt:tt + tsz], rhs=w[:, kc, :nb], start=(kc == 0), stop=(kc == KC - 1)))
                    S.mm_group(fns, reads=[xrb[g], wb], writes=[pb])
                    o, ob_ = osb.next()
                    _epilogue(C, p[:tsz, :nb], pb, o[:tsz, :nb], ob_, act,
                              None if biast is None else biast[:tsz, col:col + nb], biasb, None, None)
                    if mul_ap is not None:
                        m, mb = mulr.next()
                        S.dma("sp", m[:tsz, :nb], mul_ap[tt:tt + tsz, col:col + nb], mb, None)
                        S.op("dve", lambda o=o, m=m, tsz=tsz, nb=nb: nc.vector.tensor_tensor(
                            out=o[:tsz, :nb], in0=o[:tsz, :nb], in1=m[:tsz, :nb], op=ALU.mult), reads=[mb], writes=[ob_])
                    S.dma("sp", out_ap[tt:tt + tsz, out_col0 + col:out_col0 + col + nb], o[:tsz, :nb], ob, ob_)
            else:
                for sb0 in range(0, nb, 128):
                    sn = min(128, nb - sb0)
                    for g in range(ngroups):
                        t0 = g * TG
                        tn = min(TG, T - t0)
                        p, pb = pst.next()
                        fns = []
                        for kc in range(KC):
                            fns.append(lambda kc=kc, p=p, w=w, sb0=sb0, sn=sn, t0=t0, tn=tn: nc.tensor.matmul(
                                p[:sn, :tn], lhsT=w[:, kc, sb0:sb0 + sn], rhs=xres[:, kc, t0:t0 + tn],
                                start=(kc == 0), stop=(kc == KC - 1)))
                        S.mm_group(fns, reads=[xrb[g], wb], writes=[pb])
                        o, ob_ = osb.next()
                        bcol = None
                        if biast is not None:
                            assert (col + sb0) % 128 == 0
                            bcol = biast[:sn, (col + sb0) // 128:(col + sb0) // 128 + 1]
                        _epilogue(C, p[:sn, :tn], pb, o[:sn, :tn], ob_, act, None, None, bcol, biasb)
                        r0 = out_col0 + col + sb0
                        if mul_ap is not None:
                            m, mb = mulr.next()
                            S.dma("sp", m[:sn, :tn], mul_ap[r0:r0 + sn, t0:t0 + tn], mb, None)
                            S.op("dve", lambda o=o, m=m, tn=tn, sn=sn: nc.vector.tensor_tensor(
                                out=o[:sn, :tn], in0=o[:sn, :tn], in1=m[:sn, :tn], op=ALU.mult), reads=[mb], writes=[ob_])
                        S.dma("sp", out_ap[r0:r0 + sn, t0:t0 + tn], o[:sn, :tn], ob, ob_)
            col += nb


def _epilogue(C, ps_ap, psb, o_ap, ob, act, bias_bc, bias_bc_b, bias_col, bias_col_b):
    nc, S = C.nc, C.S
    if bias_bc is not None:
        S.op("dve", lambda: nc.vector.tensor_tensor(out=o_ap, in0=ps_ap, in1=bias_bc, op=ALU.add),
             reads=[psb, bias_bc_b], writes=[ob])
        if act is not None:
            S.op("act", lambda: nc.scalar.activation(out=o_ap, in_=o_ap, func=act), reads=[ob], writes=[ob])
    elif bias_col is not None:
        S.op("act", lambda: nc.scalar.activation(out=o_ap, in_=ps_ap, func=(act if act is not None else AF.Identity),
                                                 bias=bias_col), reads=[psb, bias_col_b], writes=[ob])
    else:
        if act is not None:
            S.op("act", lambda: nc.scalar.activation(out=o_ap, in_=ps_ap, func=act), reads=[psb], writes=[ob])
        else:
            S.op("act", lambda: nc.scalar.copy(out=o_ap, in_=ps_ap), reads=[psb], writes=[ob])


def wcols(w_ap, n0, n1, blk):
    out = []
    c = n0
    while c < n1:
        nb = min(blk, n1 - c)
        out.append(([(0, w_ap[:, c:c + nb])], nb))
        c += nb
    return out


class Rot:
    def __init__(self, C, name, shape, n, psum=False, dtype=F32):
        self.t = [(C.ps if psum else C.sb)(name + str(i), shape, dtype) for i in range(n)]
        self.b = [C.buf(name) for i in range(n)]
        self.i = 0
        self.n = n

    def next(self):
        k = self.i % self.n
        self.i += 1
        return self.t[k], self.b[k]


class TStore:
    def __init__(self, C):
        self.C = C
        self.pp = Rot(C, "tsp", [128, 4, 128], 2, psum=True)
        self.st = Rot(C, "tss", [128, 4, 128], 3)

    def run(self, src, srcb, ncols, dstT, dstb, t0, tsz, col0=0):
        C = self.C
        nc, S = C.nc, C.S
        nchunks = cdiv(ncols, 128)
        for c4 in range(0, nchunks, 4):
            pt, pb = self.pp.next()
            n4 = min(4, nchunks - c4)
            widths = []
            for j in range(n4):
                c = c4 + j
                w = min(128, ncols - c * 128)
                widths.append(w)
                S.op("pe", lambda j=j, c=c, w=w, pt=pt: nc.tensor.transpose(
                    out=pt[:w, j, :tsz], in_=src[:tsz, c * 128:c * 128 + w], identity=C.ident[:tsz, :tsz]),
                    reads=[srcb, C.ident_b], writes=[pb])
            st, sb_ = self.st.next()
            if all(w == 128 for w in widths):
                S.op("act", lambda pt=pt, st=st, n4=n4: nc.scalar.copy(out=st[:, :n4, :tsz], in_=pt[:, :n4, :tsz]),
                     reads=[pb], writes=[sb_])
                S.dma("pool", dstT[col0 + c4 * 128:col0 + (c4 + n4) * 128, t0:t0 + tsz].rearrange("(c p) t -> p c t", p=128),
                      st[:, :n4, :tsz], dstb, sb_)
            else:
                for j, w in enumerate(widths):
                    S.op("act", lambda pt=pt, st=st, j=j, w=w: nc.scalar.copy(out=st[:w, j, :tsz], in_=pt[:w, j, :tsz]),
                         reads=[pb], writes=[sb_])
                    S.dma("pool", dstT[col0 + (c4 + j) * 128:col0 + (c4 + j) * 128 + w, t0:t0 + tsz],
                          st[:w, j, :tsz], dstb, sb_)


def bc_load(C, name, row_ap, n):
    t = C.sb(name, [128, n])
    b = C.buf(name)
    C.S.dma("sp", t[:], row_ap.partition_broadcast(128), b, None)
    return t, b


def rms_rows(C, x, xb, tsz, ncols, scr, scrb, stat, statb, eps=1e-6):
    nc, S = C.nc, C.S
    S.op("act", lambda: nc.scalar.activation(out=scr[:tsz, :ncols], in_=x, func=AF.Square,
                                             accum_out=stat[:tsz, 0:1]), reads=[xb], writes=[scrb, statb])
    S.op("dve", lambda: nc.vector.tensor_scalar(out=stat[:tsz, 1:2], in0=stat[:tsz, 0:1], scalar1=1.0 / ncols,
                                                scalar2=eps, op0=ALU.mult, op1=ALU.add), reads=[statb], writes=[statb])
    S.op("act", lambda: nc.scalar.activation(out=stat[:tsz, 2:3], in_=stat[:tsz, 1:2], func=AF.Sqrt),
         reads=[statb], writes=[statb])
    S.op("dve", lambda: nc.vector.reciprocal(out=stat[:tsz, 3:4], in_=stat[:tsz, 2:3]), reads=[statb], writes=[statb])
    return stat[:tsz, 3:4]


def norm_mod_T(C, X, Xb, D, ranges, hT, hTb, tm_out=None, tm_b=None):
    nc, S = C.nc, C.S
    with C.phase("norm"):
        ts = TStore(C)
        xr = Rot(C, "nx", [128, D], 2)
        hr = Rot(C, "nh", [128, D], 2)
        sr = Rot(C, "nst", [128, 4], 2)
        scr = C.sb("nscr", [128, D])
        scrb = C.buf("nscr")
        for (t0, t1, G_ap, S_ap) in ranges:
            Gt, Gb = bc_load(C, "nG", G_ap, D)
            if S_ap is not None:
                St, Sb = bc_load(C, "nS", S_ap, D)
            for tt in range(t0, t1, 128):
                tsz = min(128, t1 - tt)
                x, xb = xr.next()
                S.dma("sp", x[:tsz, :], X[tt:tt + tsz, :], xb, Xb)
                st, stb = sr.next()
                rstd = rms_rows(C, x[:tsz, :], xb, tsz, D, scr, scrb, st, stb)
                h, hb = hr.next()
                S.op("dve", lambda x=x, h=h, rstd=rstd, tsz=tsz: nc.vector.scalar_tensor_tensor(
                    out=h[:tsz, :], in0=x[:tsz, :], scalar=rstd, in1=Gt[:tsz, :], op0=ALU.mult, op1=ALU.mult),
                    reads=[xb, stb, Gb], writes=[hb])
                if S_ap is not None:
                    S.op("dve", lambda h=h, tsz=tsz: nc.vector.tensor_tensor(
                        out=h[:tsz, :], in0=h[:tsz, :], in1=St[:tsz, :], op=ALU.add), reads=[Sb], writes=[hb])
                ts.run(h, hb, D, hT, hTb, tt, tsz)
                if tm_out is not None:
                    S.dma("pool", tm_out[tt:tt + tsz, :], h[:tsz, :], tm_b, hb)


def norm_out(C, X, Xb, D, T, g_ap, outs):
    nc, S = C.nc, C.S
    with C.phase("fnorm"):
        xr = Rot(C, "nx", [128, D], 2)
        hr = Rot(C, "nh", [128, D], 2)
        sr = Rot(C, "nst", [128, 4], 2)
        scr = C.sb("nscr", [128, D])
        scrb = C.buf("nscr")
        Gt, Gb = bc_load(C, "nG", g_ap, D)
        ob = C.buf("fo", sb=False)
        for (t0, t1, o_ap) in outs:
            for tt in range(t0, t1, 128):
                tsz = min(128, t1 - tt)
                x, xb = xr.next()
                S.dma("sp", x[:tsz, :], X[tt:tt + tsz, :], xb, Xb)
                st, stb = sr.next()
                rstd = rms_rows(C, x[:tsz, :], xb, tsz, D, scr, scrb, st, stb)
                h, hb = hr.next()
                S.op("dve", lambda x=x, h=h, rstd=rstd, tsz=tsz: nc.vector.scalar_tensor_tensor(
                    out=h[:tsz, :], in0=x[:tsz, :], scalar=rstd, in1=Gt[:tsz, :], op0=ALU.mult, op1=ALU.mult),
                    reads=[xb, stb, Gb], writes=[hb])
                S.dma("pool", o_ap[tt - t0:tt - t0 + tsz, :], h[:tsz, :], ob, hb)


def residual(C, X, Xb, Y, Yb, D, ranges):
    nc, S = C.nc, C.S
    with C.phase("resid"):
        xr = Rot(C, "rx", [128, D], 3)
        yr = Rot(C, "ry", [128, D], 2)
        for (t0, t1, g_ap) in ranges:
            Gt, Gb = bc_load(C, "rG", g_ap, D)
            for tt in range(t0, t1, 128):
                tsz = min(128, t1 - tt)
                x, xb = xr.next()
                y, yb = yr.next()
                S.dma("sp", x[:tsz, :], X[tt:tt + tsz, :], xb, Xb)
                S.dma("sp", y[:tsz, :], Y[tt:tt + tsz, :], yb, Yb)
                S.op("dve", lambda y=y, tsz=tsz: nc.vector.tensor_tensor(out=y[:tsz, :], in0=y[:tsz, :], in1=Gt[:tsz, :],
                                                                         op=ALU.mult), reads=[Gb], writes=[yb])
                S.op("pool", lambda x=x, y=y, tsz=tsz: nc.gpsimd.tensor_tensor(out=x[:tsz, :], in0=x[:tsz, :], in1=y[:tsz, :],
                                                                           op=ALU.add), reads=[yb], writes=[xb])
                S.dma("pool", X[tt:tt + tsz, :], x[:tsz, :], Xb, xb)


def make_masks(C):
    nc, S = C.nc, C.S
    C.m_ge = nc.alloc_sbuf_tensor("m_ge", [128, 128], F32)
    C.m_le = nc.alloc_sbuf_tensor("m_le", [128, 128], F32)
    C.m_ge_b = C.buf("m_ge")
    C.m_le_b = C.buf("m_le")
    S.op("pool", lambda: nc.gpsimd.memset(C.m_ge[:], 0.0), writes=[C.m_ge_b])
    S.op("pool", lambda: nc.gpsimd.affine_select(out=C.m_ge[:], in_=C.m_ge[:], pattern=[[1, 128]], compare_op=ALU.is_ge,
                                                  fill=-1e30, base=0, channel_multiplier=-1),
         reads=[C.m_ge_b], writes=[C.m_ge_b])
    S.op("pool", lambda: nc.gpsimd.memset(C.m_le[:], 0.0), writes=[C.m_le_b])
    S.op("pool", lambda: nc.gpsimd.affine_select(out=C.m_le[:], in_=C.m_le[:], pattern=[[-1, 128]], compare_op=ALU.is_ge,
                                                  fill=-1e30, base=0, channel_multiplier=1),
         reads=[C.m_le_b], writes=[C.m_le_b])
    S.barrier(C.bufs)


def attention(C, items, scale, dv, sink_ap=None, nsink=0, maxk=2304):
    nc, S = C.nc, C.S
    with C.phase("attn"):
        src = C.buf("asrc", sb=False)
        dst = C.buf("adst", sb=False)
        nkp = max(len(it["kparts"]) for it in items)
        kcap = max(it["Lk"] for it in items)
        ktr = [Rot(C, "ak%d" % i, [128, kcap], 2, dtype=BF16) for i in range(nkp)]
        vr = Rot(C, "av", [128, cdiv(kcap, 128), dv], 2, dtype=BF16)
        qtr = [Rot(C, "aq%d" % i, [128, 128], 3, dtype=BF16) for i in range(nkp)]
        nb = cdiv(maxk, 512)
        sps = Rot(C, "asp", [128, nb * 512], 2 if nb <= 2 else 1, psum=True)
        tps = Rot(C, "atp", [128, 4, 128], 2, psum=True, dtype=BF16)
        ops = Rot(C, "aop", [128, 128], 2 if nb <= 2 else 1, psum=True)
        pr = Rot(C, "ap", [128, maxk], 2, dtype=BF16)
        ptr = Rot(C, "apt", [128, cdiv(maxk, 128), 128], 2, dtype=BF16)
        otr = Rot(C, "aot", [128, 128], 2)
        str_ = Rot(C, "ast", [128, 8], 3)
        if sink_ap is not None:
            sk, skb = bc_load(C, "ask", sink_ap, nsink)
        for it in items:
            Lk, Lq = it["Lk"], it["Lq"]
            nkc = cdiv(Lk, 128)
            kts = []
            for i, (kap, kd) in enumerate(it["kparts"]):
                kt, kb = ktr[i].next()
                S.dma("pool", kt[:kd, :Lk], kap, kb, src)
                kts.append((kt, kb, kd))
            vt, vb = vr.next()
            S.dma("pool", vt[:, :nkc, :], it["v"].rearrange("(c p) d -> p c d", p=128), vb, src)
            for qh in it["qheads"]:
                for qt in range(cdiv(Lq, 128)):
                    q0 = qt * 128
                    qts = []
                    for i, (qap, kd) in enumerate(qh["qparts"]):
                        qtile, qb = qtr[i].next()
                        S.dma("pool", qtile[:kd, :], qap[:, q0:q0 + 128], qb, src)
                        qts.append((qtile, qb, kd))
                    segs = it["segs"](qt)
                    ntot = sum(s[1] for s in segs)
                    sp_, spb = sps.next()
                    col = 0
                    for (k0, klen, mask) in segs:
                        c = 0
                        while c < klen:
                            n = min(klen - c, 512 - (col % 512))
                            fns = []
                            for i in range(len(kts)):
                                fns.append(lambda i=i, col=col, n=n, k0=k0, c=c: nc.tensor.matmul(
                                    sp_[:, col:col + n], lhsT=qts[i][0][:qts[i][2], :], rhs=kts[i][0][:kts[i][2], k0 + c:k0 + c + n],
                                    start=(i == 0), stop=(i == len(kts) - 1)))
                            S.mm_group(fns, reads=[q[1] for q in qts] + [k[1] for k in kts], writes=[spb])
                            col += n
                            c += n
                    col = 0
                    for (k0, klen, mask) in segs:
                        if mask is not None:
                            mt, mb = (C.m_ge, C.m_ge_b) if mask == "ge" else (C.m_le, C.m_le_b)
                            S.op("dve", lambda col=col, mt=mt: nc.vector.tensor_tensor(
                                out=sp_[:, col:col + 128], in0=sp_[:, col:col + 128], in1=mt[:], op=ALU.add),
                                reads=[mb], writes=[spb])
                        col += klen
                    st, stb = str_.next()
                    S.op("dve", lambda: nc.vector.reduce_max(out=st[:, 0:1], in_=sp_[:, :ntot], axis=AX.X),
                         reads=[spb], writes=[stb])
                    if qh.get("sink") is not None:
                        si = qh["sink"]
                        S.op("dve", lambda si=si: nc.vector.tensor_scalar(out=st[:, 1:2], in0=st[:, 0:1], scalar1=scale,
                                                                    scalar2=sk[:, si:si + 1], op0=ALU.mult, op1=ALU.max),
                             reads=[stb, skb], writes=[stb])
                        S.op("dve", lambda: nc.vector.tensor_scalar(out=st[:, 2:3], in0=st[:, 1:2], scalar1=-1.0,
                                                                    scalar2=None, op0=ALU.mult),
                             reads=[stb], writes=[stb])
                    else:
                        S.op("dve", lambda: nc.vector.tensor_scalar(out=st[:, 2:3], in0=st[:, 0:1], scalar1=-scale,
                                                                    scalar2=None, op0=ALU.mult),
                             reads=[stb], writes=[stb])
                    p, pb = pr.next()
                    S.op("act", lambda: nc.scalar.activation(out=p[:, :ntot], in_=sp_[:, :ntot], func=AF.Exp,
                                                             bias=st[:, 2:3], scale=scale, accum_out=st[:, 3:4]),
                         reads=[spb, stb], writes=[pb, stb])
                    if qh.get("sink") is not None:
                        si = qh["sink"]
                        S.op("act", lambda si=si: nc.scalar.activation(out=st[:, 4:5], in_=sk[:, si:si + 1], func=AF.Exp,
                                                                 bias=st[:, 2:3], scale=1.0), reads=[skb, stb], writes=[stb])
                        S.op("dve", lambda: nc.vector.tensor_tensor(out=st[:, 3:4], in0=st[:, 3:4], in1=st[:, 4:5],
                                                                    op=ALU.add), reads=[stb], writes=[stb])
                    S.op("dve", lambda: nc.vector.reciprocal(out=st[:, 5:6], in_=st[:, 3:4]), reads=[stb], writes=[stb])
                    S.op("dve", lambda: nc.vector.tensor_scalar(out=p[:, :ntot], in0=p[:, :ntot], scalar1=st[:, 5:6],
                                                                scalar2=None, op0=ALU.mult), reads=[stb], writes=[pb])
                    pt, ptb = ptr.next()
                    nblk = ntot // 128
                    assert ntot % 128 == 0
                    for b4 in range(0, nblk, 4):
                        n4 = min(4, nblk - b4)
                        tp, tpb = tps.next()
                        for j in range(n4):
                            S.op("pe", lambda j=j, b4=b4, tp=tp: nc.tensor.transpose(
                                out=tp[:, j, :], in_=p[:, (b4 + j) * 128:(b4 + j + 1) * 128], identity=C.ident_bf[:]),
                                reads=[pb, C.ident_bf_b], writes=[tpb])
                        S.op("act" if (b4 // 4) % 2 == 0 else "dve",
                             (lambda tp=tp, b4=b4, n4=n4: nc.scalar.copy(out=pt[:, b4:b4 + n4, :], in_=tp[:, :n4, :]))
                             if (b4 // 4) % 2 == 0 else
                             (lambda tp=tp, b4=b4, n4=n4: nc.vector.tensor_copy(out=pt[:, b4:b4 + n4, :], in_=tp[:, :n4, :])),
                             reads=[tpb], writes=[ptb])
                    op_, opb = ops.next()
                    fns = []
                    blk = 0
                    for (k0, klen, mask) in segs:
                        assert k0 % 128 == 0 and klen % 128 == 0
                        for c in range(klen // 128):
                            fns.append(lambda blk=blk, kc=k0 // 128 + c: nc.tensor.matmul(
                                op_[:dv, :], lhsT=vt[:, kc, :], rhs=pt[:, blk, :], start=(blk == 0), stop=(blk == nblk - 1)))
                            blk += 1
                    S.mm_group(fns, reads=[vb, ptb], writes=[opb])
                    ot, otb = otr.next()
                    S.op("act", lambda: nc.scalar.copy(out=ot[:dv, :], in_=op_[:dv, :]), reads=[opb], writes=[otb])
                    S.dma("sp", qh["oT"][:, q0:q0 + 128], ot[:dv, :], dst, otb)


def make_iota16(C):
    nc, S = C.nc, C.S
    C.iota_i = nc.alloc_sbuf_tensor("iota_i", [128, 8, 16, 16], I32)
    C.iota_a = nc.alloc_sbuf_tensor("iota_a", [128, 8, 16, 16], F32)
    C.iota_b = C.buf("iota")
    S.op("pool", lambda: nc.gpsimd.iota(C.iota_i[:], pattern=[[0, 8], [0, 16], [1, 16]], base=0, channel_multiplier=0),
         writes=[C.iota_b])
    S.op("dve", lambda: nc.vector.tensor_copy(out=C.iota_a[:], in_=C.iota_i[:]), reads=[C.iota_b], writes=[C.iota_b])
    S.barrier(C.bufs)


def peer_prep(C):
    nc, S = C.nc, C.S
    C.ident_bf = nc.alloc_sbuf_tensor("ident_bf", [128, 128], BF16)
    C.ident_bf_b = C.buf("ident_bf")
    S.op("dve", lambda: nc.vector.tensor_copy(out=C.ident_bf[:], in_=C.ident[:]), reads=[C.ident_b], writes=[C.ident_bf_b])
    S.barrier(C.bufs)


def cast_tables(C, u_ap, v_ap, nrows, D, chunk=2048):
    nc, S = C.nc, C.S
    ub = C.nc.dram_tensor(C.uid("u_bf"), [nrows, D], BF16).ap()
    vb = C.nc.dram_tensor(C.uid("v_bf"), [nrows, D], BF16).ap()
    C.tab_buf = Buf("tables", sb=False)
    NOWN = 8
    C.cast_owners = []
    for i in range(NOWN):
        o = Buf("castown%d" % i, sb=False)
        sem = nc.alloc_semaphore("BG%d" % i)
        S.dcnt[id(sem)] = 0
        S.semobj[id(sem)] = sem
        o.dsem = {"sw": sem}
        C.cast_owners.append(o)
    C.cast_queue = []
    for r0 in range(0, nrows, chunk):
        n = min(chunk, nrows - r0)
        C.cast_queue.append((r0 + n, ub[r0:r0 + n, :], u_ap[r0:r0 + n, :]))
        C.cast_queue.append((r0 + n, vb[r0:r0 + n, :], v_ap[r0:r0 + n, :]))
    C.cast_i = 0
    return ub, vb


def issue_casts(C, k=None, upto_row=None):
    if not getattr(C, "cast_queue", None):
        return
    n = 0
    while C.cast_queue:
        end, dst_ap, src_ap = C.cast_queue[0]
        if upto_row is not None:
            if end > upto_row:
                break
        elif n >= k:
            break
        C.cast_queue.pop(0)
        own = C.cast_owners[C.cast_i % len(C.cast_owners)]
        C.cast_i += 1
        C.S.dma("pool", dst_ap, src_ap, C.tab_buf, None, owner=own)
        n += 1


def peer(C, QT, H2, X, Xb, D, T, keys_ap, u_ap, v_ap, ranges, PH=8, NKEY=128, TOPK=16, row_off=0, tab_buf=None):
    nc, S = C.nc, C.S
    NG = PH * 2
    NS = PH * TOPK
    if tab_buf is not None:
        issue_casts(C, upto_row=row_off + NKEY * NKEY)
    with C.phase("peer"):
        src = C.buf("psrc", sb=False)
        tsrc = tab_buf if tab_buf is not None else src
        if tab_buf is not None:
            u_ap = u_ap[0:row_off + NKEY * NKEY, :]
            v_ap = v_ap[0:row_off + NKEY * NKEY, :]
        keysT = C.sb("keysT", [128, NG, NKEY])
        keysTb = C.buf("keysT")
        kl = Rot(C, "kl", [128, 128], 2)
        spr = Rot(C, "psp", [128, NG * NKEY], 1, psum=True)
        kps, kpsb = spr.next()
        for g in range(NG):
            kt, kb = kl.next()
            S.dma("sp", kt[:], keys_ap[g // 2, g % 2], kb, src)
            S.op("pe", lambda kt=kt, g=g: nc.tensor.transpose(out=kps[:, g * NKEY:(g + 1) * NKEY], in_=kt[:], identity=C.ident[:]),
                 reads=[kb, C.ident_b], writes=[kpsb])
        S.op("act", lambda: nc.scalar.copy(out=keysT[:].rearrange("p g n -> p (g n)"), in_=kps[:]), reads=[kpsb], writes=[keysTb])
        qr = Rot(C, "pq", [128, NG, 128], 2)
        hr = Rot(C, "ph", [128, D], 2)
        xr = Rot(C, "px", [128, D], 2)
        s1 = C.sb("ps1", [128, NG, NKEY]); s1b = C.buf("ps1")
        s2 = C.sb("ps2", [128, NG, NKEY]); s2b = C.buf("ps2")
        sv = C.sb("psv", [128, NG, TOPK]); svb = C.buf("psv")
        si = C.sb("psi", [128, NG, TOPK], U32); sib = C.buf("psi")
        sif = C.sb("psif", [128, NG, TOPK]); sifb = C.buf("psif")
        cs = C.sb("pcs", [128, PH, TOPK, TOPK]); csb = C.buf("pcs")
        cs2 = C.sb("pcs2", [128, PH, TOPK, TOPK]); cs2b = C.buf("pcs2")
        tsv = C.sb("ptsv", [128, PH, TOPK]); tsvb = C.buf("ptsv")
        tpos = C.sb("ptpos", [128, PH, TOPK], U32); tposb = C.buf("ptpos")
        ab_i = C.sb("pabi", [128, 2, PH, TOPK], U32); ab_ib = C.buf("pabi")
        ab_f = C.sb("pabf", [128, 2, PH, TOPK]); ab_fb = C.buf("pabf")
        oh = C.sb("poh", [128, PH, TOPK, TOPK]); ohb = C.buf("poh")
        sel = C.sb("psel", [128, 2, PH, TOPK]); selb = C.buf("psel")
        eidf = C.sb("peidf", [128, NS]); eidfb = C.buf("peidf")
        eid = [C.sb("peid%d" % i, [128, NS], I32) for i in range(2)]
        eidb = [C.buf("peid") for i in range(2)]
        gate = C.sb("pgate", [128, PH, TOPK]); gateb = C.buf("pgate")
        gs = C.sb("pgs", [128, PH, 2]); gsb = C.buf("pgs")
        a_all = C.sb("pa", [128, NS]); a_allb = C.buf("pa")
        w_all = C.sb("pw", [128, NS]); w_allb = C.buf("pw")
        gtmp = C.sb("pgt", [128, NS]); gtmpb = C.buf("pgt")
        junk = C.sb("pjunk", [128, D], BF16); junkb = C.buf("pjunk")
        accr = Rot(C, "pacc", [128, D], 1, psum=True)
        accs = C.sb("paccs", [128, D]); accsb = C.buf("paccs")
        dgr = Rot(C, "pdg", [128, 128], 4, dtype=BF16)
        gr = Rot(C, "pgr", [128, D], 6, dtype=BF16)
        tiles = []
        for (t0, t1, g_ap) in ranges:
            Gt, Gb = bc_load(C, "pG", g_ap, D)
            for tt in range(t0, t1, 128):
                assert tt + 128 <= t1
                tiles.append((tt, Gt, Gb))
        st_ = {"ti": 0}

        def do_topk(tt):
            ti = st_["ti"]
            q, qb = qr.next()
            S.dma("sp", q[:], QT[:, tt:tt + 128].rearrange("(c p) t -> p c t", p=128), qb, src)
            h, hb = hr.next()
            S.dma("sp", h[:], H2[tt:tt + 128, :], hb, src)
            x, xb = xr.next()
            S.dma("sp", x[:], X[tt:tt + 128, :], xb, Xb)
            sp_, spb = spr.next()
            for g in range(NG):
                S.mm_group([lambda g=g: nc.tensor.matmul(sp_[:, g * NKEY:(g + 1) * NKEY], lhsT=q[:, g, :],
                                                         rhs=keysT[:, g, :], start=True, stop=True)],
                           reads=[qb, keysTb], writes=[spb])
            S.op("act", lambda: nc.scalar.copy(out=s1[:].rearrange("p g n -> p (g n)"), in_=sp_[:]),
                 reads=[spb], writes=[s1b])
            for g in range(NG):
                S.op("dve", lambda g=g: nc.vector.max(out=sv[:, g, 0:8], in_=s1[:, g, :]), reads=[s1b], writes=[svb])
                S.op("dve", lambda g=g: nc.vector.max_index(out=si[:, g, 0:8], in_max=sv[:, g, 0:8], in_values=s1[:, g, :]),
                     reads=[s1b, svb], writes=[sib])
                S.op("dve", lambda g=g: nc.vector.match_replace(out=s2[:, g, :], in_to_replace=sv[:, g, 0:8],
                                                                in_values=s1[:, g, :], imm_value=-1e30),
                     reads=[s1b, svb], writes=[s2b])
                S.op("dve", lambda g=g: nc.vector.max(out=sv[:, g, 8:16], in_=s2[:, g, :]), reads=[s2b], writes=[svb])
                S.op("dve", lambda g=g: nc.vector.max_index(out=si[:, g, 8:16], in_max=sv[:, g, 8:16], in_values=s2[:, g, :]),
                     reads=[s2b, svb], writes=[sib])
            S.op("dve", lambda: nc.vector.tensor_copy(out=sif[:], in_=si[:]), reads=[sib], writes=[sifb])
            sv4 = sv[:].rearrange("p (h two) k -> p h two k", two=2)
            sif4 = sif[:].rearrange("p (h two) k -> p h two k", two=2)
            S.op("dve", lambda: nc.vector.tensor_tensor(
                out=cs[:], in0=sv4[:, :, 0, :].unsqueeze(3).broadcast_to([128, PH, TOPK, TOPK]),
                in1=sv4[:, :, 1, :].unsqueeze(2).broadcast_to([128, PH, TOPK, TOPK]), op=ALU.add),
                reads=[svb], writes=[csb])
            for hh in range(PH):
                c1 = cs[:, hh].rearrange("p a b -> p (a b)")
                c2 = cs2[:, hh].rearrange("p a b -> p (a b)")
                S.op("dve", lambda hh=hh, c1=c1: nc.vector.max(out=tsv[:, hh, 0:8], in_=c1), reads=[csb], writes=[tsvb])
                S.op("dve", lambda hh=hh, c1=c1: nc.vector.max_index(out=tpos[:, hh, 0:8], in_max=tsv[:, hh, 0:8], in_values=c1),
                     reads=[csb, tsvb], writes=[tposb])
                S.op("dve", lambda hh=hh, c1=c1, c2=c2: nc.vector.match_replace(out=c2, in_to_replace=tsv[:, hh, 0:8],
                                                                         in_values=c1, imm_value=-1e30),
                     reads=[csb, tsvb], writes=[cs2b])
                S.op("dve", lambda hh=hh, c2=c2: nc.vector.max(out=tsv[:, hh, 8:16], in_=c2), reads=[cs2b], writes=[tsvb])
                S.op("dve", lambda hh=hh, c2=c2: nc.vector.max_index(out=tpos[:, hh, 8:16], in_max=tsv[:, hh, 8:16], in_values=c2),
                     reads=[cs2b, tsvb], writes=[tposb])
            S.op("dve", lambda: nc.vector.tensor_single_scalar(out=ab_i[:, 0], in_=tpos[:], scalar=4, op=ALU.logical_shift_right),
                 reads=[tposb], writes=[ab_ib])
            S.op("dve", lambda: nc.vector.tensor_single_scalar(out=ab_i[:, 1], in_=tpos[:], scalar=15, op=ALU.bitwise_and),
                 reads=[tposb], writes=[ab_ib])
            S.op("dve", lambda: nc.vector.tensor_copy(out=ab_f[:], in_=ab_i[:]), reads=[ab_ib], writes=[ab_fb])
            for two in range(2):
                S.op("dve", lambda two=two: nc.vector.tensor_tensor(
                    out=oh[:], in0=C.iota_a[:, :PH], in1=ab_f[:, two].unsqueeze(3).broadcast_to([128, PH, TOPK, TOPK]),
                    op=ALU.is_equal), reads=[C.iota_b, ab_fb], writes=[ohb])
                S.op("dve", lambda two=two: nc.vector.tensor_tensor(
                    out=oh[:], in0=oh[:], in1=sif4[:, :, two, :].unsqueeze(2).broadcast_to([128, PH, TOPK, TOPK]),
                    op=ALU.mult), reads=[sifb], writes=[ohb])
                S.op("dve", lambda two=two: nc.vector.tensor_reduce(out=sel[:, two], in_=oh[:], axis=AX.X, op=ALU.add),
                     reads=[ohb], writes=[selb])
            S.op("dve", lambda: nc.vector.scalar_tensor_tensor(
                out=eidf[:], in0=sel[:, 0].rearrange("p h k -> p (h k)"), scalar=float(NKEY),
                in1=sel[:, 1].rearrange("p h k -> p (h k)"), op0=ALU.mult, op1=ALU.add), reads=[selb], writes=[eidfb])
            if row_off:
                S.op("dve", lambda: nc.vector.tensor_scalar(out=eidf[:], in0=eidf[:], scalar1=float(row_off), scalar2=None,
                                                            op0=ALU.add), reads=[eidfb], writes=[eidfb])
            ei = eid[ti % 2]
            eib = eidb[ti % 2]
            S.op("dve", lambda ei=ei: nc.vector.tensor_copy(out=ei[:], in_=eidf[:]), reads=[eidfb], writes=[eib])
            S.op("dve", lambda: nc.vector.tensor_tensor(out=gate[:], in0=tsv[:], in1=tsv[:, :, 0:1].broadcast_to([128, PH, TOPK]),
                                                        op=ALU.subtract), reads=[tsvb], writes=[gateb])
            S.op("act", lambda: nc.scalar.activation(out=gate[:], in_=gate[:], func=AF.Exp), reads=[gateb], writes=[gateb])
            S.op("dve", lambda: nc.vector.tensor_reduce(out=gs[:, :, 0], in_=gate[:], axis=AX.X, op=ALU.add),
                 reads=[gateb], writes=[gsb])
            S.op("dve", lambda: nc.vector.reciprocal(out=gs[:, :, 1], in_=gs[:, :, 0]), reads=[gsb], writes=[gsb])
            S.op("dve", lambda: nc.vector.tensor_tensor(out=gate[:], in0=gate[:], in1=gs[:, :, 1:2].broadcast_to([128, PH, TOPK]),
                                                        op=ALU.mult), reads=[gsb], writes=[gateb])

            st_["ti"] += 1
            return dict(h=h, hb=hb, x=x, xb=xb, ei=ei, eib=eib)

        def do_upass(tt, T_):
            h, hb, ei, eib = T_["h"], T_["hb"], T_["ei"], T_["eib"]
            for e in range(NS):
                ub, ubb = gr.next()
                S.dma("pool", ub[:], u_ap, ubb, tsrc,
                      indirect=dict(in_offset=bass.IndirectOffsetOnAxis(ap=ei[:, e:e + 1], axis=0), idx_buf=eib))
                S.op("dve", lambda ub=ub, e=e, h=h: nc.vector.scalar_tensor_tensor(
                    out=junk[:], in0=ub[:], scalar=1.0, in1=h[:], op0=ALU.mult, op1=ALU.mult,
                    accum_out=a_all[:, e:e + 1]), reads=[ubb, hb], writes=[junkb, a_allb])
            gelu_tanh(C, None, w_all[:], a_all[:], gtmp[:], a_allb, w_allb, gtmpb)
            S.op("dve", lambda: nc.vector.tensor_tensor(out=w_all[:], in0=w_all[:], in1=gate[:].rearrange("p h k -> p (h k)"),
                                                        op=ALU.mult), reads=[gateb], writes=[w_allb])

        def do_vpass(tt, T_, Gt, Gb):
            x, xb, ei, eib = T_["x"], T_["xb"], T_["ei"], T_["eib"]
            acc, accb = accr.next()
            for e in range(NS):
                vb_, vbb = gr.next()
                S.dma("pool", vb_[:], v_ap, vbb, tsrc,
                      indirect=dict(in_offset=bass.IndirectOffsetOnAxis(ap=ei[:, e:e + 1], axis=0), idx_buf=eib))
                dg, dgb = dgr.next()
                S.op("act", lambda dg=dg, e=e: nc.scalar.activation(out=dg[:], in_=C.ident_bf[:], func=AF.Copy,
                                                                   scale=w_all[:, e:e + 1]),
                     reads=[C.ident_bf_b, w_allb], writes=[dgb])
                fns = []
                for c0 in range(0, D, 512):
                    n = min(512, D - c0)
                    fns.append(lambda dg=dg, vb_=vb_, c0=c0, n=n, e=e: nc.tensor.matmul(
                        acc[:, c0:c0 + n], lhsT=dg[:], rhs=vb_[:, c0:c0 + n], start=(e == 0), stop=(e == NS - 1)))
                S.mm_group(fns, reads=[dgb, vbb], writes=[accb])
            S.op("dve", lambda acc=acc: nc.vector.tensor_tensor(out=accs[:], in0=acc[:], in1=Gt[:], op=ALU.mult),
                 reads=[accb, Gb], writes=[accsb])
            S.op("dve", lambda x=x: nc.vector.tensor_tensor(out=x[:], in0=x[:], in1=accs[:], op=ALU.add),
                 reads=[accsb], writes=[xb])
            S.dma("pool", X[tt:tt + 128, :], x[:], Xb, xb)

        cur = do_topk(tiles[0][0])
        for k, (tt, Gt, Gb) in enumerate(tiles):
            do_upass(tt, cur)
            nxt = do_topk(tiles[k + 1][0]) if k + 1 < len(tiles) else None
            do_vpass(tt, cur, Gt, Gb)
            cur = nxt


def gelu_tanh(C, e_small, out_ap, in_ap, tmp_ap, inb, outb, tmpb):
    nc, S = C.nc, C.S
    S.op("dve", lambda: nc.vector.tensor_tensor(out=tmp_ap, in0=in_ap, in1=in_ap, op=ALU.mult), reads=[inb], writes=[tmpb])
    S.op("dve", lambda: nc.vector.tensor_scalar(out=tmp_ap, in0=tmp_ap, scalar1=0.044715, scalar2=1.0, op0=ALU.mult,
                                                op1=ALU.add), reads=[tmpb], writes=[tmpb])
    S.op("dve", lambda: nc.vector.tensor_tensor(out=tmp_ap, in0=tmp_ap, in1=in_ap, op=ALU.mult), reads=[inb], writes=[tmpb])
    S.op("act", lambda: nc.scalar.activation(out=tmp_ap, in_=tmp_ap, func=AF.Sigmoid, scale=1.5957691216057308),
         reads=[tmpb], writes=[tmpb])
    S.op("dve", lambda: nc.vector.tensor_tensor(out=out_ap, in0=tmp_ap, in1=in_ap, op=ALU.mult), reads=[tmpb, inb],
         writes=[outb])


PI = 3.14159265358979


def range_reduce(C, x, xb, ti, tf, tb):
    nc, S = C.nc, C.S
    S.op("dve", lambda: nc.vector.tensor_scalar(out=tf, in0=x, scalar1=1.0 / (2 * PI), scalar2=None, op0=ALU.mult),
         reads=[xb], writes=[tb])
    S.op("dve", lambda: nc.vector.tensor_copy(out=ti, in_=tf), reads=[tb], writes=[tb])
    S.op("dve", lambda: nc.vector.tensor_copy(out=tf, in_=ti), reads=[tb], writes=[tb])
    S.op("dve", lambda: nc.vector.scalar_tensor_tensor(out=x, in0=tf, scalar=-2 * PI, in1=x, op0=ALU.mult, op1=ALU.add),
         reads=[tb], writes=[xb])
    S.op("dve", lambda: nc.vector.tensor_scalar(out=tf, in0=x, scalar1=PI, scalar2=-2 * PI, op0=ALU.is_gt, op1=ALU.mult),
         reads=[xb], writes=[tb])
    S.op("dve", lambda: nc.vector.tensor_tensor(out=x, in0=x, in1=tf, op=ALU.add), reads=[tb], writes=[xb])
    S.op("dve", lambda: nc.vector.tensor_scalar(out=tf, in0=x, scalar1=-PI, scalar2=2 * PI, op0=ALU.is_lt, op1=ALU.mult),
         reads=[xb], writes=[tb])
    S.op("dve", lambda: nc.vector.tensor_tensor(out=x, in0=x, in1=tf, op=ALU.add), reads=[tb], writes=[xb])


class RopeTab:
    def __init__(self, C, nf, axis_dim, grid_w=64, base=10000.0):
        nc, S = C.nc, C.S
        self.C, self.nf = C, nf
        self.gw = grid_w
        self.pi_i = C.sb("rp_pi", [128, nf], I32)
        self.inv = C.sb("rp_inv", [128, nf])
        self.pc = C.sb("rp_pc", [128, 4])
        self.ang = C.sb("rp_ang", [128, 2, nf])
        self.tmp = C.sb("rp_tmp", [128, 2, nf])
        self.tab = C.sb("rp_tab", [128, 4, nf])
        self.b = C.buf("rope")
        self.tabb = C.buf("ropetab")
        b = self.b
        S.op("pool", lambda: nc.gpsimd.iota(self.pi_i[:], pattern=[[1, nf]], base=0, channel_multiplier=0), writes=[b])
        S.op("dve", lambda: nc.vector.tensor_copy(out=self.inv[:], in_=self.pi_i[:]), reads=[b], writes=[b])
        import math
        S.op("act", lambda: nc.scalar.activation(out=self.inv[:], in_=self.inv[:], func=AF.Exp,
                                                 scale=-2.0 * math.log(base) / axis_dim), reads=[b], writes=[b])
        self.pidx = C.sb("rp_pidx", [128, 1], I32)
        S.op("pool", lambda: nc.gpsimd.iota(self.pidx[:], pattern=[[0, 1]], base=0, channel_multiplier=1), writes=[b])
        self.pint = C.sb("rp_pint", [128, 2], I32)
        S.op("dve", lambda: nc.vector.tensor_single_scalar(out=self.pint[:, 0:1], in_=self.pidx[:], scalar=grid_w - 1,
                                                           op=ALU.bitwise_and), reads=[b], writes=[b])
        sh = int(round(math.log2(grid_w)))
        assert (1 << sh) == grid_w
        S.op("dve", lambda: nc.vector.tensor_single_scalar(out=self.pint[:, 1:2], in_=self.pidx[:], scalar=sh,
                                                           op=ALU.arith_shift_right), reads=[b], writes=[b])
        S.op("dve", lambda: nc.vector.tensor_copy(out=self.pc[:, 1:3], in_=self.pint[:]), reads=[b], writes=[b])
        self.ti = C.sb("rp_ti", [128, 2, nf], I32)
        self.tf = C.sb("rp_tf", [128, 2, nf])

    def build(self, tile_idx):
        C = self.C
        nc, S = C.nc, C.S
        b, tb = self.b, self.tabb
        row0 = float(tile_idx * 128 // self.gw)
        S.op("dve", lambda: nc.vector.tensor_scalar(out=self.pc[:, 3:4], in0=self.pc[:, 2:3], scalar1=row0, scalar2=None,
                                                    op0=ALU.add), reads=[b], writes=[b])
        S.op("dve", lambda: nc.vector.tensor_scalar(out=self.ang[:, 0, :], in0=self.inv[:], scalar1=self.pc[:, 3:4],
                                                    scalar2=None, op0=ALU.mult), reads=[b], writes=[b])
        S.op("dve", lambda: nc.vector.tensor_scalar(out=self.ang[:, 1, :], in0=self.inv[:], scalar1=self.pc[:, 1:2],
                                                    scalar2=None, op0=ALU.mult), reads=[b], writes=[b])
        tabv = self.tab[:].rearrange("p (a cs) f -> p a cs f", cs=2)
        for cs, shift in ((0, 0.5 * PI), (1, 0.0)):
            S.op("dve", lambda shift=shift: nc.vector.tensor_scalar(out=self.tmp[:], in0=self.ang[:], scalar1=shift,
                                                                    scalar2=None, op0=ALU.add), reads=[b], writes=[b])
            range_reduce(C, self.tmp[:], b, self.ti[:], self.tf[:], b)
            S.op("act", lambda cs=cs: nc.scalar.activation(out=tabv[:, :, cs, :], in_=self.tmp[:], func=AF.Sin),
                 reads=[b], writes=[tb])

    def apply(self, x3, xb, H, tmpt, tmpb):
        C = self.C
        nc, S = C.nc, C.S
        nf = self.nf
        for a in range(2):
            x1 = x3[:, :, a * 2 * nf:a * 2 * nf + nf]
            x2 = x3[:, :, a * 2 * nf + nf:(a + 1) * 2 * nf]
            c = self.tab[:, 2 * a, :].unsqueeze(1).broadcast_to([128, H, nf])
            s = self.tab[:, 2 * a + 1, :].unsqueeze(1).broadcast_to([128, H, nf])
            for k, (xx, tt) in enumerate(((x1, c), (x2, s), (x1, s), (x2, c))):
                S.op("dve", lambda k=k, xx=xx, tt=tt: nc.vector.tensor_tensor(out=tmpt[:, k], in0=xx, in1=tt, op=ALU.mult),
                     reads=[xb, self.tabb], writes=[tmpb])
            S.op("dve", lambda x1=x1: nc.vector.tensor_tensor(out=x1, in0=tmpt[:, 0], in1=tmpt[:, 1], op=ALU.subtract),
                 reads=[tmpb], writes=[xb])
            S.op("dve", lambda x2=x2: nc.vector.tensor_tensor(out=x2, in0=tmpt[:, 2], in1=tmpt[:, 3], op=ALU.add),
                 reads=[tmpb], writes=[xb])


class Cfg:
    def __init__(self, **kw):
        self.D = 2048
        self.B_P, self.L_P, self.L_S, self.PAST, self.GRID_W = 2, 256, 2048, 256, 64
        self.DEPTH = 4
        self.MH, self.QR, self.KVR, self.NOPE, self.ROPE, self.MV = 16, 512, 512, 128, 64, 128
        self.S5P = 64
        self.SH, self.SKV, self.SHD = 16, 4, 128
        self.PH, self.NKEY, self.TOPK = 8, 128, 16
        self.HY_HID, self.HY_BANDS = 64, 8
        for k, v in kw.items():
            setattr(self, k, v)
        self.TP = self.B_P * self.L_P
        self.T = self.TP + self.L_S
        self.TK = self.T + self.PAST
        self.G = self.D // 16


def declare_io(nc, cf):
    D = cf.D
    n_mla, n_s5, n_hy, n_swa = [len(range(k, cf.DEPTH, 4)) for k in range(4)]
    I = {}

    def inp(name, shape):
        I[name] = nc.dram_tensor(name, list(shape), F32, kind="ExternalInput").ap()

    def outp(name, shape):
        I[name] = nc.dram_tensor(name, list(shape), F32, kind="ExternalOutput").ap()

    inp("x_p", [cf.TP, D]); inp("x_s", [cf.L_S, D]); inp("c", [D]); inp("c_ctx", [D])
    inp("cache_mla_ckv", [max(n_mla, 1), cf.PAST, cf.KVR]); inp("cache_mla_kpe", [max(n_mla, 1), cf.PAST, cf.ROPE])
    inp("state_s5_re", [max(n_s5, 1), 2, cf.G, cf.S5P]); inp("state_s5_im", [max(n_s5, 1), 2, cf.G, cf.S5P])
    inp("cache_swa_k", [max(n_swa, 1), cf.PAST, cf.SKV * cf.SHD]); inp("cache_swa_v", [max(n_swa, 1), cf.PAST, cf.SKV * cf.SHD])
    inp("mod_w", [cf.DEPTH, D, 6 * D]); inp("mod_b", [cf.DEPTH, 6 * D])
    inp("norm1_g", [cf.DEPTH, D]); inp("norm2_g", [cf.DEPTH, D]); inp("final_g", [D])
    if n_mla:
        inp("mla_w_down", [n_mla, D, cf.QR + cf.KVR + cf.ROPE]); inp("mla_g_q", [n_mla, cf.QR]); inp("mla_g_kv", [n_mla, cf.KVR])
        inp("mla_w_uq", [n_mla, cf.QR, cf.MH * (cf.NOPE + cf.ROPE)]); inp("mla_w_ukv", [n_mla, cf.KVR, cf.MH * (cf.NOPE + cf.MV)])
        inp("mla_w_o", [n_mla, cf.MH * cf.MV, D])
    if n_s5:
        G, P = cf.G, cf.S5P
        inp("s5_w_in", [n_s5, D, D]); inp("s5_lam_re", [n_s5, 2, G, P]); inp("s5_lam_im", [n_s5, 2, G, P])
        inp("s5_log_dt", [n_s5, 2, G]); inp("s5_b_re", [n_s5, 2, G, P, 16]); inp("s5_b_im", [n_s5, 2, G, P, 16])
        inp("s5_c_re", [n_s5, 2, G, 16, P]); inp("s5_c_im", [n_s5, 2, G, 16, P]); inp("s5_d", [n_s5, D])
        inp("s5_w_gate", [n_s5, D, D]); inp("s5_w_out", [n_s5, D, D])
    if n_hy:
        HE = 2 * cf.HY_BANDS + 1
        inp("hy_w_in", [n_hy, D, 3 * D]); inp("hy_b_in", [n_hy, 3 * D]); inp("hy_short_w", [n_hy, 3, 3 * D])
        inp("hy_short_b", [n_hy, 3 * D]); inp("hy_f_w1", [n_hy, HE, cf.HY_HID]); inp("hy_f_b1", [n_hy, cf.HY_HID])
        inp("hy_f_freq", [n_hy, 2, cf.HY_HID]); inp("hy_f_w2", [n_hy, cf.HY_HID, cf.HY_HID]); inp("hy_f_b2", [n_hy, cf.HY_HID])
        inp("hy_f_w3", [n_hy, cf.HY_HID, 4 * D]); inp("hy_log_decay", [n_hy, 4 * D]); inp("hy_bias", [n_hy, 2, D])
        inp("hy_w_out", [n_hy, D, D])
    if n_swa:
        inp("swa_w_qkv", [n_swa, D, (cf.SH + 2 * cf.SKV) * cf.SHD]); inp("swa_w_o", [n_swa, cf.SH * cf.SHD, D])
        inp("swa_sink", [n_swa, cf.SH])
    inp("peer_w_q", [cf.DEPTH, D, cf.PH * 2 * 128]); inp("peer_keys", [cf.DEPTH, cf.PH, 2, cf.NKEY, 128])
    inp("peer_u", [cf.DEPTH, cf.NKEY * cf.NKEY, D]); inp("peer_v", [cf.DEPTH, cf.NKEY * cf.NKEY, D])
    outp("y_p", [cf.TP, D]); outp("y_s", [cf.L_S, D])
    outp("o_ckv", [max(n_mla, 1), cf.TP, cf.KVR]); outp("o_kpe", [max(n_mla, 1), cf.TP, cf.ROPE])
    outp("o_s5re", [cf.B_P, max(n_s5, 1), 2, cf.G * cf.S5P]); outp("o_s5im", [cf.B_P, max(n_s5, 1), 2, cf.G * cf.S5P])
    outp("o_swak", [max(n_swa, 1), cf.TP, cf.SKV * cf.SHD]); outp("o_swav", [max(n_swa, 1), cf.TP, cf.SKV * cf.SHD])
    return I


def compute_mod(C, cf, I, MODROWS):
    nc, S = C.nc, C.S
    D = cf.D
    KC = D // 128
    condT = C.dram("condT", [D, 2])
    MODRAW = C.dram("modraw", [cf.DEPTH, 2, 6 * D])
    with C.phase("cond"):
        cc = C.sb("cc", [128, KC, 2]); ccb = C.buf("cc")
        db = C.buf("condT", sb=False)
        S.dma("sp", cc[:, :, 0], I["c_ctx"].rearrange("(c p) -> p c", p=128), ccb, None, allow_slow_non_contiguous=True)
        S.dma("sp", cc[:, :, 1], I["c"].rearrange("(c p) -> p c", p=128), ccb, None, allow_slow_non_contiguous=True)
        S.op("act", lambda: nc.scalar.activation(out=cc[:], in_=cc[:], func=AF.Silu), reads=[ccb], writes=[ccb])
        S.dma("pool", condT.rearrange("(c p) two -> p c two", p=128), cc[:], db, ccb)
    for i in range(cf.DEPTH):
        linear(C, condT, 2, D, wcols(I["mod_w"][i], 0, 6 * D, 512), MODRAW[i], mode="tm", bias_ap=I["mod_b"][i])
    with C.phase("modrows"):
        mr = Rot(C, "mr", [2, 6 * D], 2)
        gr = Rot(C, "mg", [2, 2, D], 2)
        db = C.buf("modrows", sb=False)
        for i in range(cf.DEPTH):
            m, mb = mr.next()
            g, gb = gr.next()
            S.dma("sp", m[:], MODRAW[i], mb, None)
            S.dma("sp", g[:, 0, :], I["norm1_g"][i].partition_broadcast(2), gb, None)
            S.dma("sp", g[:, 1, :], I["norm2_g"][i].partition_broadcast(2), gb, None)
            for k, col in ((0, D), (1, 4 * D)):
                S.op("dve", lambda m=m, g=g, k=k, col=col: nc.vector.scalar_tensor_tensor(
                    out=m[:, col:col + D], in0=m[:, col:col + D], scalar=1.0, in1=g[:, k, :], op0=ALU.add, op1=ALU.mult),
                    reads=[gb], writes=[mb])
            S.dma("pool", MODROWS[i].rearrange("c s d -> c (s d)"), m[:], db, mb)


def mla_layer(C, cf, I, j, hT, hTb, X, Xb, MR):
    nc, S = C.nc, C.S
    D, T, TP, TK = cf.D, cf.T, cf.TP, cf.TK
    QR, KVR, RP, H = cf.QR, cf.KVR, cf.ROPE, cf.MH
    ND = QR + KVR + RP
    DOWN = C.dram("down", [T, ND])
    linear_bf(C, hT, T, D, wcols(I["mla_w_down"][j], 0, ND, 512), DOWN, mode="tm")
    cqT = C.dram("cqT", [QR, T])
    ckvT = C.dram("ckvT", [KVR, TK])
    kpeT = C.dram("kpeT", [RP, TK])
    nf = RP // 4
    with C.phase("mla_rows"):
        src = C.buf("src", sb=False); dst = C.buf("dst", sb=False)
        ts = TStore(C)
        dr = Rot(C, "md", [128, ND], 2)
        scr = C.sb("mscr", [128, max(QR, KVR)]); scrb = C.buf("mscr")
        sr = Rot(C, "mst", [128, 4], 2)
        gq, gqb = bc_load(C, "gq", I["mla_g_q"][j], QR)
        gkv, gkvb = bc_load(C, "gkv", I["mla_g_kv"][j], KVR)
        rt = RopeTab(C, nf, RP // 2, cf.GRID_W)
        rtmp = C.sb("rtmp", [128, 4, 1, nf]); rtmpb = C.buf("rtmp")
        for tt in range(0, T, 128):
            d, db = dr.next()
            S.dma("sp", d[:], DOWN[tt:tt + 128, :], db, src)
            for (c0, n, gt, gb_) in ((0, QR, gq, gqb), (QR, KVR, gkv, gkvb)):
                st, stb = sr.next()
                rstd = rms_rows(C, d[:, c0:c0 + n], db, 128, n, scr, scrb, st, stb)
                S.op("dve", lambda d=d, c0=c0, n=n, gt=gt, rstd=rstd: nc.vector.scalar_tensor_tensor(
                    out=d[:, c0:c0 + n], in0=d[:, c0:c0 + n], scalar=rstd, in1=gt[:, :], op0=ALU.mult, op1=ALU.mult),
                    reads=[stb, gb_], writes=[db])
            if tt < TP:
                S.dma("pool", I["o_ckv"][j, tt:tt + 128, :], d[:, QR:QR + KVR], dst, db)
                S.dma("pool", I["o_kpe"][j, tt:tt + 128, :], d[:, QR + KVR:], dst, db)
            else:
                rt.build((tt - TP) // 128)
                rt.apply(d[:, QR + KVR:].rearrange("p (h r) -> p h r", h=1), db, 1, rtmp, rtmpb)
            ts.run(d[:, 0:QR], db, QR, cqT, dst, tt, 128)
            ts.run(d[:, QR:QR + KVR], db, KVR, ckvT, dst, tt, 128)
            ts.run(d[:, QR + KVR:], db, RP, kpeT, dst, tt, 128)
        cr = Rot(C, "mc", [128, KVR + RP], 2)
        for tt in range(0, cf.PAST, 128):
            c_, cb = cr.next()
            S.dma("sp", c_[:, :KVR], I["cache_mla_ckv"][j, tt:tt + 128, :], cb, src)
            S.dma("sp", c_[:, KVR:], I["cache_mla_kpe"][j, tt:tt + 128, :], cb, src)
            ts.run(c_[:, :KVR], cb, KVR, ckvT, dst, T + tt, 128)
            ts.run(c_[:, KVR:], cb, RP, kpeT, dst, T + tt, 128)
    HQ = cf.NOPE + RP
    Q = C.dram("Q", [T, H * HQ])
    linear_bf(C, cqT, T, QR, wcols(I["mla_w_uq"][j], 0, H * HQ, 512), Q, mode="tm")
    qT = C.dram("qT", [H * HQ, T])
    with C.phase("mla_q"):
        src = C.buf("src", sb=False); dst = C.buf("dst", sb=False)
        ts = TStore(C)
        qr_ = Rot(C, "mq", [128, H * HQ], 2)
        rt = RopeTab(C, nf, RP // 2, cf.GRID_W)
        rtmp = C.sb("rtmp", [128, 4, H, nf]); rtmpb = C.buf("rtmp")
        for tt in range(0, T, 128):
            q, qb = qr_.next()
            S.dma("sp", q[:], Q[tt:tt + 128, :], qb, src)
            if tt >= TP:
                rt.build((tt - TP) // 128)
                rt.apply(q[:].rearrange("p (h r) -> p h r", h=H)[:, :, cf.NOPE:], qb, H, rtmp, rtmpb)
            ts.run(q, qb, H * HQ, qT, dst, tt, 128)
    KNT = C.dram("KNT", [H * cf.NOPE, TK])
    Vm = C.dram("Vm", [TK, H * cf.MV])
    wkv = I["mla_w_ukv"][j]
    HW = cf.NOPE + cf.MV
    linear_bf(C, ckvT, TK, KVR, [([(0, wkv[:, h * HW:h * HW + cf.NOPE])], cf.NOPE) for h in range(H)], KNT, mode="fm")
    linear_bf(C, ckvT, TK, KVR, [([(0, wkv[:, h * HW + cf.NOPE:(h + 1) * HW])], cf.MV) for h in range(H)], Vm, mode="tm")
    OT = C.dram("OT", [H * cf.MV, T])
    items = []
    seqs = [(s * cf.L_P, cf.L_P, s * cf.L_P, cf.L_P) for s in range(cf.B_P)] + [(TP, cf.L_S, TP, cf.L_S + cf.PAST)]
    for (q0, Lq, k0, Lk) in seqs:
        for h in range(H):
            items.append(dict(
                kparts=[(KNT[h * cf.NOPE:(h + 1) * cf.NOPE, k0:k0 + Lk], cf.NOPE), (kpeT[:, k0:k0 + Lk], RP)],
                v=Vm[k0:k0 + Lk, h * cf.MV:(h + 1) * cf.MV], Lk=Lk, Lq=Lq,
                qheads=[dict(qparts=[(qT[h * HQ:h * HQ + cf.NOPE, q0:q0 + Lq], cf.NOPE),
                                     (qT[h * HQ + cf.NOPE:(h + 1) * HQ, q0:q0 + Lq], RP)],
                             oT=OT[h * cf.MV:(h + 1) * cf.MV, q0:q0 + Lq], sink=None)],
                segs=(lambda qt, Lk=Lk: [(0, Lk, None)])))
    attention(C, items, float(HQ) ** -0.5, cf.MV, maxk=cf.L_S + cf.PAST)
    Y = C.dram("Y", [T, D])
    linear_bf(C, OT, T, H * cf.MV, wcols(I["mla_w_o"][j], 0, D, 512), Y, mode="tm")
    Yb = C.buf("Y", sb=False)
    residual(C, X, Xb, Y, Yb, D, MR["g1"])


def peer_layer(C, cf, I, i, X, Xb, MR):
    D, T = cf.D, cf.T
    h2T = C.dram("h2T", [D, T])
    H2 = C.dram("H2", [T, D])
    b1 = C.buf("h2T", sb=False); b2 = C.buf("H2", sb=False)
    norm_mod_T(C, X, Xb, D, MR["n2"], h2T, b1, tm_out=H2, tm_b=b2)
    NQ = cf.PH * 2 * 128
    QT = C.dram("pQT", [NQ, T])
    linear(C, h2T, T, D, wcols(I["peer_w_q"][i], 0, NQ, 128), QT, mode="fm")
    NE = cf.NKEY * cf.NKEY
    peer(C, QT, H2, X, Xb, D, T, I["peer_keys"][i], C.u_bf, C.v_bf, MR["g2"], PH=cf.PH, NKEY=cf.NKEY, TOPK=cf.TOPK,
         row_off=i * NE, tab_buf=C.tab_buf)


def build_program(cf):
    nc = bass.Bass("TRN2", target_bir_lowering=False)
    I = declare_io(nc, cf)
    C = Ctx(nc)
    C.make_ident(); make_masks(C); make_iota16(C); peer_prep(C)
    D, T, TP = cf.D, cf.T, cf.TP
    C.u_bf, C.v_bf = cast_tables(C, I["peer_u"].rearrange("l e d -> (l e) d"), I["peer_v"].rearrange("l e d -> (l e) d"),
                                 cf.DEPTH * cf.NKEY * cf.NKEY, D)
    X = C.dram("X", [T, D])
    Xb = C.buf("X", sb=False)
    with C.phase("xin"):
        cb = C.buf("cp", sb=False)
        C.S.dma("sp", X[0:TP, :], I["x_p"][:, :], cb, None, owner=cb)
        C.S.dma("sp", X[TP:T, :], I["x_s"][:, :], cb, None, owner=cb)
    MODROWS = C.dram("modrows", [cf.DEPTH, 2, 6, D])
    compute_mod(C, cf, I, MODROWS)
    for i in range(cf.DEPTH):
        kind, j = i % 4, i // 4
        M = MODROWS[i]
        MR = dict(n1=[(0, TP, M[0, 1], M[0, 0]), (TP, T, M[1, 1], M[1, 0])],
                  g1=[(0, TP, M[0, 2]), (TP, T, M[1, 2])],
                  n2=[(0, TP, M[0, 4], M[0, 3]), (TP, T, M[1, 4], M[1, 3])],
                  g2=[(0, TP, M[0, 5]), (TP, T, M[1, 5])])
        hT = C.dram("hT", [D, T])
        hTb = C.buf("hT", sb=False)
        norm_mod_T(C, X, Xb, D, MR["n1"], hT, hTb)
        if kind == 0:
            mla_layer(C, cf, I, j, hT, hTb, X, Xb, MR)
        elif kind == 1:
            s5_layer(C, cf, I, j, hT, hTb, X, Xb, MR)
        elif kind == 2:
            hy_layer(C, cf, I, j, hT, hTb, X, Xb, MR)
        else:
            swa_layer(C, cf, I, j, hT, hTb, X, Xb, MR)
        peer_layer(C, cf, I, i, X, Xb, MR)
    norm_out(C, X, Xb, D, T, I["final_g"], [(0, TP, I["y_p"]), (TP, T, I["y_s"])])
    C.S.barrier(C.bufs)
    return nc, C


def make_in_maps(cf, inp, n_cores):
    n_mla, n_s5, n_hy, n_swa = [len(range(k, cf.DEPTH, 4)) for k in range(4)]
    shared = {}
    for k in ("c_ctx", "mod_w", "mod_b", "norm1_g", "norm2_g", "final_g", "peer_w_q", "peer_keys", "peer_u", "peer_v"):
        shared[k] = np.ascontiguousarray(inp[k], dtype=np.float32)
    if n_mla:
        for k in ("mla_w_down", "mla_g_q", "mla_g_kv", "mla_w_uq", "mla_w_ukv", "mla_w_o"):
            shared[k] = np.ascontiguousarray(inp[k], dtype=np.float32)
    if n_s5:
        for k in ("s5_w_in", "s5_lam_re", "s5_lam_im", "s5_log_dt", "s5_b_re", "s5_b_im", "s5_c_re", "s5_c_im", "s5_d",
                  "s5_w_gate", "s5_w_out"):
            shared[k] = np.ascontiguousarray(inp[k], dtype=np.float32)
    if n_hy:
        for k in ("hy_w_in", "hy_b_in", "hy_short_w", "hy_short_b", "hy_f_w1", "hy_f_b1", "hy_f_freq", "hy_f_w2", "hy_f_b2",
                  "hy_f_w3", "hy_bias", "hy_w_out"):
            shared[k] = np.ascontiguousarray(inp[k], dtype=np.float32)
        shared["hy_log_decay"] = np.ascontiguousarray(inp["hy_log_decay"], dtype=np.float32).reshape(n_hy, 4 * cf.D)
    if n_swa:
        for k in ("swa_w_qkv", "swa_w_o", "swa_sink"):
            shared[k] = np.ascontiguousarray(inp[k], dtype=np.float32)
    maps = []
    for c in range(n_cores):
        m = dict(shared)
        m["x_p"] = np.ascontiguousarray(inp["x_prompt"][c * cf.B_P:(c + 1) * cf.B_P]).reshape(cf.TP, cf.D)
        m["x_s"] = np.ascontiguousarray(inp["x_sample"][c])
        m["c"] = np.ascontiguousarray(inp["c"][c])
        m["cache_mla_ckv"] = np.ascontiguousarray(inp["cache_mla_ckv"][c]) if n_mla else np.zeros((1, cf.PAST, cf.KVR), np.float32)
        m["cache_mla_kpe"] = np.ascontiguousarray(inp["cache_mla_kpe"][c]) if n_mla else np.zeros((1, cf.PAST, cf.ROPE), np.float32)
        m["state_s5_re"] = np.ascontiguousarray(inp["state_s5_re"][c]) if n_s5 else np.zeros((1, 2, cf.G, cf.S5P), np.float32)
        m["state_s5_im"] = np.ascontiguousarray(inp["state_s5_im"][c]) if n_s5 else np.zeros((1, 2, cf.G, cf.S5P), np.float32)
        kvd = cf.SKV * cf.SHD
        m["cache_swa_k"] = np.ascontiguousarray(inp["cache_swa_k"][c]).reshape(-1, cf.PAST, kvd) if n_swa else np.zeros((1, cf.PAST, kvd), np.float32)
        m["cache_swa_v"] = np.ascontiguousarray(inp["cache_swa_v"][c]).reshape(-1, cf.PAST, kvd) if n_swa else np.zeros((1, cf.PAST, kvd), np.float32)
        maps.append(m)
    return maps


def assemble(cf, results, n_cores):
    n_mla, n_s5, n_hy, n_swa = [len(range(k, cf.DEPTH, 4)) for k in range(4)]
    D = cf.D
    y_p = np.concatenate([r["y_p"].reshape(cf.B_P, cf.L_P, D) for r in results], 0)
    y_s = np.stack([r["y_s"] for r in results], 0)
    ckv = np.concatenate([r["o_ckv"].reshape(-1, cf.B_P, cf.L_P, cf.KVR).transpose(1, 0, 2, 3) for r in results], 0)[:, :n_mla]
    kpe = np.concatenate([r["o_kpe"].reshape(-1, cf.B_P, cf.L_P, cf.ROPE).transpose(1, 0, 2, 3) for r in results], 0)[:, :n_mla]
    s5re = np.concatenate([r["o_s5re"].reshape(cf.B_P, -1, 2, cf.G, cf.S5P) for r in results], 0)[:, :n_s5]
    s5im = np.concatenate([r["o_s5im"].reshape(cf.B_P, -1, 2, cf.G, cf.S5P) for r in results], 0)[:, :n_s5]
    swk = np.concatenate([r["o_swak"].reshape(-1, cf.B_P, cf.L_P, cf.SKV, cf.SHD).transpose(1, 0, 2, 3, 4) for r in results], 0)[:, :n_swa]
    swv = np.concatenate([r["o_swav"].reshape(-1, cf.B_P, cf.L_P, cf.SKV, cf.SHD).transpose(1, 0, 2, 3, 4) for r in results], 0)[:, :n_swa]
    return tuple(np.ascontiguousarray(a, dtype=np.float32) for a in (y_p, y_s, ckv, kpe, s5re, s5im, swk, swv))


def kernel(**inputs):
    from concourse.bass_utils import run_bass_kernel_spmd
    cf = Cfg()
    nc, C = build_program(cf)
    maps = make_in_maps(cf, inputs, 8)
    res = run_bass_kernel_spmd(nc, maps, core_ids=list(range(8)))
    return assemble(cf, res.results, 8)


def s5_layer(C, cf, I, j, hT, hTb, X, Xb, MR):
    nc, S = C.nc, C.S
    D, T, TP = cf.D, cf.T, cf.TP
    G, P = cf.G, cf.S5P
    J = G // 2
    assert P == 64
    UT = C.dram("UT", [D, T])
    linear_bf(C, hT, T, D, wcols(I["s5_w_in"][j], 0, D, 512), UT, mode="fm")
    ZT = C.dram("ZT", [D, T])
    seqs = [(s * cf.L_P, cf.L_P, s) for s in range(cf.B_P)] + [(TP, cf.L_S, None)]
    Lmax = max(cf.L_P, cf.L_S)
    nsteps_max = int(math.log2(Lmax))
    with C.phase("s5"):
        src = C.buf("src", sb=False); dst = C.buf("dst", sb=False)
        pb = C.buf("s5par")
        lam = C.sb("lam", [128, 2, 2, J])
        dt = C.sb("dt", [128, 2, J])
        for d in range(2):
            S.dma("sp", lam[:, d, 0, :], I["s5_lam_re"][j, d].rearrange("(j g) p -> (g p) j", g=2), pb, src,
                  allow_slow_non_contiguous=True)
            S.dma("sp", lam[:, d, 1, :], I["s5_lam_im"][j, d].rearrange("(j g) p -> (g p) j", g=2), pb, src,
                  allow_slow_non_contiguous=True)
            ldt = I["s5_log_dt"][j, d].rearrange("(j g) -> g j", g=2)
            for g2 in range(2):
                S.dma("sp", dt[g2 * 64:(g2 + 1) * 64, d, :], ldt[g2].partition_broadcast(64), pb, src,
                      allow_slow_non_contiguous=True)
        S.op("act", lambda: nc.scalar.activation(out=dt[:], in_=dt[:], func=AF.Exp), reads=[pb], writes=[pb])
        mag = C.sb("mag", [128, 2, J]); th = C.sb("th", [128, 2, 2, J])
        ti_ = C.sb("s5ti", [128, 2, 2, J], I32); tf_ = C.sb("s5tf", [128, 2, 2, J])
        S.op("dve", lambda: nc.vector.tensor_tensor(out=mag[:], in0=lam[:, :, 0, :], in1=dt[:], op=ALU.mult), reads=[pb], writes=[pb])
        S.op("act", lambda: nc.scalar.activation(out=mag[:], in_=mag[:], func=AF.Exp), reads=[pb], writes=[pb])
        S.op("dve", lambda: nc.vector.tensor_tensor(out=th[:, :, 1, :], in0=lam[:, :, 1, :], in1=dt[:], op=ALU.mult), reads=[pb], writes=[pb])
        S.op("dve", lambda: nc.vector.tensor_scalar(out=th[:, :, 0, :], in0=th[:, :, 1, :], scalar1=0.5 * PI, scalar2=None,
                                                    op0=ALU.add), reads=[pb], writes=[pb])
        range_reduce(C, th[:], pb, ti_[:], tf_[:], pb)
        S.op("act", lambda: nc.scalar.activation(out=th[:], in_=th[:], func=AF.Sin), reads=[pb], writes=[pb])
        NST = nsteps_max
        apow = C.sb("apow", [128, 2, NST + 1, 3, J])
        for d in range(2):
            S.op("dve", lambda d=d: nc.vector.tensor_tensor(out=apow[:, d, 0, 0, :], in0=mag[:, d, :], in1=th[:, d, 0, :], op=ALU.mult),
                 reads=[pb], writes=[pb])
            S.op("dve", lambda d=d: nc.vector.tensor_tensor(out=apow[:, d, 0, 1, :], in0=mag[:, d, :], in1=th[:, d, 1, :], op=ALU.mult),
                 reads=[pb], writes=[pb])
        sq = C.sb("sq", [128, 2, 3, J])
        for k in range(NST):
            S.op("dve", lambda k=k: nc.vector.tensor_tensor(out=sq[:, :, 0, :], in0=apow[:, :, k, 0, :], in1=apow[:, :, k, 0, :], op=ALU.mult), reads=[pb], writes=[pb])
            S.op("dve", lambda k=k: nc.vector.tensor_tensor(out=sq[:, :, 1, :], in0=apow[:, :, k, 1, :], in1=apow[:, :, k, 1, :], op=ALU.mult), reads=[pb], writes=[pb])
            S.op("dve", lambda k=k: nc.vector.tensor_tensor(out=sq[:, :, 2, :], in0=apow[:, :, k, 0, :], in1=apow[:, :, k, 1, :], op=ALU.mult), reads=[pb], writes=[pb])
            S.op("dve", lambda k=k: nc.vector.tensor_tensor(out=apow[:, :, k + 1, 0, :], in0=sq[:, :, 0, :], in1=sq[:, :, 1, :], op=ALU.subtract), reads=[pb], writes=[pb])
            S.op("dve", lambda k=k: nc.vector.tensor_scalar(out=apow[:, :, k + 1, 1, :], in0=sq[:, :, 2, :], scalar1=2.0, scalar2=None, op0=ALU.mult), reads=[pb], writes=[pb])
        S.op("dve", lambda: nc.vector.tensor_scalar(out=apow[:, :, :, 2, :], in0=apow[:, :, :, 1, :], scalar1=-1.0, scalar2=None, op0=ALU.mult), reads=[pb], writes=[pb])
        cf_ = C.sb("coef", [128, 2, 2, J]); t1 = C.sb("ct1", [128, 2, J]); t2 = C.sb("ct2", [128, 2, J]); den = C.sb("den", [128, 2, J])
        arm1 = C.sb("arm1", [128, 2, J])
        S.op("dve", lambda: nc.vector.tensor_scalar(out=arm1[:], in0=apow[:, :, 0, 0, :], scalar1=-1.0, scalar2=None, op0=ALU.add), reads=[pb], writes=[pb])
        S.op("dve", lambda: nc.vector.tensor_tensor(out=den[:], in0=lam[:, :, 0, :], in1=lam[:, :, 0, :], op=ALU.mult), reads=[pb], writes=[pb])
        S.op("dve", lambda: nc.vector.tensor_tensor(out=t1[:], in0=lam[:, :, 1, :], in1=lam[:, :, 1, :], op=ALU.mult), reads=[pb], writes=[pb])
        S.op("dve", lambda: nc.vector.tensor_tensor(out=den[:], in0=den[:], in1=t1[:], op=ALU.add), reads=[pb], writes=[pb])
        S.op("dve", lambda: nc.vector.reciprocal(out=den[:], in_=den[:]), reads=[pb], writes=[pb])
        S.op("dve", lambda: nc.vector.tensor_tensor(out=t1[:], in0=arm1[:], in1=lam[:, :, 0, :], op=ALU.mult), reads=[pb], writes=[pb])
        S.op("dve", lambda: nc.vector.tensor_tensor(out=t2[:], in0=apow[:, :, 0, 1, :], in1=lam[:, :, 1, :], op=ALU.mult), reads=[pb], writes=[pb])
        S.op("dve", lambda: nc.vector.tensor_tensor(out=t1[:], in0=t1[:], in1=t2[:], op=ALU.add), reads=[pb], writes=[pb])
        S.op("dve", lambda: nc.vector.tensor_tensor(out=cf_[:, :, 0, :], in0=t1[:], in1=den[:], op=ALU.mult), reads=[pb], writes=[pb])
        S.op("dve", lambda: nc.vector.tensor_tensor(out=t1[:], in0=apow[:, :, 0, 1, :], in1=lam[:, :, 0, :], op=ALU.mult), reads=[pb], writes=[pb])
        S.op("dve", lambda: nc.vector.tensor_tensor(out=t2[:], in0=arm1[:], in1=lam[:, :, 1, :], op=ALU.mult), reads=[pb], writes=[pb])
        S.op("dve", lambda: nc.vector.tensor_tensor(out=t1[:], in0=t1[:], in1=t2[:], op=ALU.subtract), reads=[pb], writes=[pb])
        S.op("dve", lambda: nc.vector.tensor_tensor(out=cf_[:, :, 1, :], in0=t1[:], in1=den[:], op=ALU.mult), reads=[pb], writes=[pb])
        Bm = C.sb("Bm", [128, 2, 2, J, 16])
        for d in range(2):
            S.dma("sp", Bm[:, d, 0], I["s5_b_re"][j, d].rearrange("(j g) p c -> (g p) j c", g=2), pb, src)
            S.dma("sp", Bm[:, d, 1], I["s5_b_im"][j, d].rearrange("(j g) p c -> (g p) j c", g=2), pb, src)
        BD = C.sb("BD", [128, J, 2, 2, 2, 16])
        bt = C.sb("bt", [128, 2, J, 16]); bt2 = C.sb("bt2", [128, 2, J, 16])
        S.op("pool", lambda: nc.gpsimd.memset(BD[:], 0.0), writes=[pb])
        crb = cf_[:, :, 0, :].unsqueeze(3).broadcast_to([128, 2, J, 16])
        cib = cf_[:, :, 1, :].unsqueeze(3).broadcast_to([128, 2, J, 16])
        for ri, (xa, xb_, op) in enumerate(((0, 1, ALU.subtract), (1, 0, ALU.add))):
            S.op("dve", lambda xa=xa: nc.vector.tensor_tensor(out=bt[:], in0=Bm[:, :, xa], in1=crb, op=ALU.mult), reads=[pb], writes=[pb])
            S.op("dve", lambda xb_=xb_: nc.vector.tensor_tensor(out=bt2[:], in0=Bm[:, :, xb_], in1=cib, op=ALU.mult), reads=[pb], writes=[pb])
            for g2 in range(2):
                sl = slice(g2 * 64, (g2 + 1) * 64)
                S.op("dve", lambda ri=ri, g2=g2, sl=sl, op=op: nc.vector.tensor_tensor(
                    out=BD[sl, :, :, ri, g2, :], in0=bt[sl].rearrange("p d j c -> p j d c"),
                    in1=bt2[sl].rearrange("p d j c -> p j d c"), op=op), reads=[pb], writes=[pb])
        CD = C.sb("CD", [128, J, 2, 2, 2, 16])
        S.op("pool", lambda: nc.gpsimd.memset(CD[:], 0.0), writes=[pb])
        for d in range(2):
            for ri, nm in enumerate(("s5_c_re", "s5_c_im")):
                cview = I[nm][j, d].rearrange("(j g) c p -> g p j c", g=2)
                for g2 in range(2):
                    for jj in range(J):
                        S.dma("sp", CD[g2 * 64:(g2 + 1) * 64, jj, d, ri, g2, :], cview[g2, :, jj, :], pb, src,
                              allow_slow_non_contiguous=True)
        S.op("dve", lambda: nc.vector.tensor_scalar(out=CD[:, :, :, 1], in0=CD[:, :, :, 1], scalar1=-1.0, scalar2=None, op0=ALU.mult),
             reads=[pb], writes=[pb])
        h0 = C.sb("h0", [128, 2, 2, J]); init = C.sb("init", [128, 2, 2, J])
        for d in range(2):
            S.dma("sp", h0[:, d, 0, :], I["state_s5_re"][j, d].rearrange("(j g) p -> (g p) j", g=2), pb, src, allow_slow_non_contiguous=True)
            S.dma("sp", h0[:, d, 1, :], I["state_s5_im"][j, d].rearrange("(j g) p -> (g p) j", g=2), pb, src, allow_slow_non_contiguous=True)
        ar_ = apow[:, :, 0, 0, :]; ai_ = apow[:, :, 0, 1, :]
        S.op("dve", lambda: nc.vector.tensor_tensor(out=t1[:], in0=ar_, in1=h0[:, :, 0, :], op=ALU.mult), reads=[pb], writes=[pb])
        S.op("dve", lambda: nc.vector.tensor_tensor(out=t2[:], in0=ai_, in1=h0[:, :, 1, :], op=ALU.mult), reads=[pb], writes=[pb])
        S.op("dve", lambda: nc.vector.tensor_tensor(out=init[:, :, 0, :], in0=t1[:], in1=t2[:], op=ALU.subtract), reads=[pb], writes=[pb])
        S.op("dve", lambda: nc.vector.tensor_tensor(out=t1[:], in0=ar_, in1=h0[:, :, 1, :], op=ALU.mult), reads=[pb], writes=[pb])
        S.op("dve", lambda: nc.vector.tensor_tensor(out=t2[:], in0=ai_, in1=h0[:, :, 0, :], op=ALU.mult), reads=[pb], writes=[pb])
        S.op("dve", lambda: nc.vector.tensor_tensor(out=init[:, :, 1, :], in0=t1[:], in1=t2[:], op=ALU.add), reads=[pb], writes=[pb])
        BD_d = C.dram("s5BD", [128, J, 2, 2, 32]); CD_d = C.dram("s5CD", [128, J, 2, 2, 32])
        AP_d = C.dram("s5AP", [128, 2 * (NST + 1) * 3 * J]); IN_d = C.dram("s5IN", [128, 2 * 2 * J])
        S.dma("pool", BD_d.rearrange("p j d r q -> p (j d r q)"), BD[:].rearrange("p j d r g c -> p (j d r g c)"), dst, pb)
        S.dma("pool", CD_d.rearrange("p j d r q -> p (j d r q)"), CD[:].rearrange("p j d r g c -> p (j d r g c)"), dst, pb)
        S.dma("pool", AP_d, apow[:].rearrange("p d k c j -> p (d k c j)"), dst, pb)
        S.dma("pool", IN_d, init[:].rearrange("p d r j -> p (d r j)"), dst, pb)
    with C.phase("s5main"):
        src = C.buf("src", sb=False); dst = C.buf("dst", sb=False)
        pb = C.buf("s5par")
        apow = C.sb("apow", [128, 2, NST + 1, 3, J])
        init = C.sb("init", [128, 2, 2, J])
        dsk = C.sb("dsk", [32, J])
        S.dma("sp", apow[:].rearrange("p d k c j -> p (d k c j)"), AP_d, pb, src)
        S.dma("sp", init[:].rearrange("p d r j -> p (d r j)"), IN_d, pb, src)
        S.dma("sp", dsk[:], I["s5_d"][j].rearrange("(j q) -> q j", q=32), pb, src, allow_slow_non_contiguous=True)
        bdr = Rot(C, "s5bd", [128, 2, 2, 32], 2)
        cdr = Rot(C, "s5cd", [128, 2, 2, 32], 2)
        fin = C.sb("fin", [128, cf.B_P, 2, 2, J]); finb = C.buf("fin")
        ur = Rot(C, "s5u", [32, Lmax], 2)
        btp = Rot(C, "s5btp", [32, 128], 2, psum=True)
        bts = Rot(C, "s5bts", [32, 2, 128], 2)
        bup = Rot(C, "s5bup", [128, 512], 2, psum=True)
        yp = Rot(C, "s5yp", [32, 512], 2, psum=True)
        hbuf = [[[C.sb("s5h%d%d%d" % (d, pp, ri), [128, Lmax]) for ri in range(2)] for pp in range(2)] for d in range(2)]
        hb_ = [[C.buf("s5h") for pp in range(2)] for d in range(2)]
        zr = Rot(C, "s5z", [32, Lmax], 2)
        zt = C.sb("s5zt", [32, Lmax]); ztb = C.buf("s5zt")
        groups = [(0, cf.L_P, cf.B_P, True), (TP, cf.L_S, 1, False)]
        for (t0, L, nseq, is_p) in groups:
            nst = int(math.log2(L))
            assert (1 << nst) == L
            W = nseq * L
            assert W <= Lmax

            def v3(t):
                return t[:, :W].rearrange("p (s l) -> p s l", s=nseq)
            for jj in range(J):
                u, ub = ur.next()
                S.dma("sp", u[:, :W], UT[jj * 32:(jj + 1) * 32, t0:t0 + W], ub, src)
                BDp, BDpb = bdr.next()
                CDp, CDpb = cdr.next()
                S.dma("sp", BDp[:], BD_d[:, jj], BDpb, src)
                S.dma("sp", CDp[:], CD_d[:, jj], CDpb, src)
                for d in range(2):
                    bs, bsb = bts.next()
                    for ri in range(2):
                        tp_, tpb = btp.next()
                        S.op("pe", lambda d=d, ri=ri, tp_=tp_: nc.tensor.transpose(
                            out=tp_[:], in_=BDp[:, d, ri, :], identity=C.ident[:]),
                            reads=[BDpb, C.ident_b], writes=[tpb])
                        S.op("act", lambda ri=ri, tp_=tp_, bs=bs: nc.scalar.copy(out=bs[:, ri, :], in_=tp_[:]), reads=[tpb], writes=[bsb])
                    for ri in range(2):
                        for c0 in range(0, W, 512):
                            n = min(512, W - c0)
                            bp, bpb = bup.next()
                            S.mm_group([lambda c0=c0, n=n, ri=ri, bp=bp, bs=bs: nc.tensor.matmul(
                                bp[:, :n], lhsT=bs[:, ri, :], rhs=u[:, c0:c0 + n], start=True, stop=True)],
                                reads=[bsb, ub], writes=[bpb])
                            S.op("act", lambda ri=ri, d=d, bp=bp, c0=c0, n=n: nc.scalar.copy(
                                out=hbuf[d][0][ri][:, c0:c0 + n], in_=bp[:, :n]), reads=[bpb], writes=[hb_[d][0]])
                    if not is_p:
                        col = (L - 1) if d == 1 else 0
                        for ri in range(2):
                            S.op("dve", lambda ri=ri, d=d, col=col: nc.vector.tensor_tensor(
                                out=hbuf[d][0][ri][:, col:col + 1], in0=hbuf[d][0][ri][:, col:col + 1],
                                in1=init[:, d, ri, jj:jj + 1], op=ALU.add), reads=[pb], writes=[hb_[d][0]])
                cur = 0
                for k in range(nst):
                    s_ = 1 << k
                    for d in range(2):
                        o, n_ = hbuf[d][cur], hbuf[d][1 - cur]
                        ob, nb = hb_[d][cur], hb_[d][1 - cur]
                        ar = apow[:, d, k, 0, jj:jj + 1]; ai = apow[:, d, k, 1, jj:jj + 1]; nai = apow[:, d, k, 2, jj:jj + 1]
                        if d == 0:
                            dsl, ssl, psl = slice(s_, L), slice(0, L - s_), slice(0, s_)
                        else:
                            dsl, ssl, psl = slice(0, L - s_), slice(s_, L), slice(L - s_, L)
                        o0, o1, n0, n1 = v3(o[0]), v3(o[1]), v3(n_[0]), v3(n_[1])
                        S.op("dve", lambda: nc.vector.scalar_tensor_tensor(out=n0[:, :, dsl], in0=o0[:, :, ssl], scalar=ar, in1=o0[:, :, dsl],
                                                                          op0=ALU.mult, op1=ALU.add), reads=[ob, pb], writes=[nb])
                        S.op("dve", lambda: nc.vector.scalar_tensor_tensor(out=n0[:, :, dsl], in0=o1[:, :, ssl], scalar=nai, in1=n0[:, :, dsl],
                                                                          op0=ALU.mult, op1=ALU.add), reads=[ob, pb], writes=[nb])
                        S.op("dve", lambda: nc.vector.scalar_tensor_tensor(out=n1[:, :, dsl], in0=o1[:, :, ssl], scalar=ar, in1=o1[:, :, dsl],
                                                                          op0=ALU.mult, op1=ALU.add), reads=[ob, pb], writes=[nb])
                        S.op("dve", lambda: nc.vector.scalar_tensor_tensor(out=n1[:, :, dsl], in0=o0[:, :, ssl], scalar=ai, in1=n1[:, :, dsl],
                                                                          op0=ALU.mult, op1=ALU.add), reads=[ob, pb], writes=[nb])
                        S.op("act", lambda: nc.scalar.copy(out=n0[:, :, psl], in_=o0[:, :, psl]), reads=[ob], writes=[nb])
                        S.op("act", lambda: nc.scalar.copy(out=n1[:, :, psl], in_=o1[:, :, psl]), reads=[ob], writes=[nb])
                    cur = 1 - cur
                finals = [(hbuf[d][cur], hb_[d][cur]) for d in range(2)]
                if is_p:
                    for d in range(2):
                        col = 0 if d == 1 else L - 1
                        for ri in range(2):
                            S.op("act", lambda ri=ri, d=d, col=col, cur=cur: nc.scalar.copy(
                                out=fin[:, :, d, ri, jj:jj + 1], in_=v3(hbuf[d][cur][ri])[:, :, col:col + 1]),
                                reads=[hb_[d][cur]], writes=[finb])
                z, zb = zr.next()
                for c0 in range(0, W, 512):
                    n = min(512, W - c0)
                    y, yb = yp.next()
                    fns = []
                    for d in range(2):
                        for ri in range(2):
                            fns.append(lambda d=d, ri=ri, c0=c0, n=n, y=y: nc.tensor.matmul(
                                y[:, :n], lhsT=CDp[:, d, ri, :],
                                rhs=finals[d][0][ri][:, c0:c0 + n], start=(d == 0 and ri == 0), stop=(d == 1 and ri == 1)))
                    S.mm_group(fns, reads=[CDpb, finals[0][1], finals[1][1]], writes=[yb])
                    S.op("dve", lambda c0=c0, n=n, y=y: nc.vector.scalar_tensor_tensor(
                        out=z[:, c0:c0 + n], in0=u[:, c0:c0 + n], scalar=dsk[:, jj:jj + 1], in1=y[:, :n],
                        op0=ALU.mult, op1=ALU.add), reads=[ub, yb, pb], writes=[zb])
                gelu_tanh(C, None, z[:, :W], z[:, :W], zt[:, :W], zb, zb, ztb)
                S.dma("pool", ZT[jj * 32:(jj + 1) * 32, t0:t0 + W], z[:, :W], dst, zb)
        for sq_ in range(cf.B_P):
            for d in range(2):
                S.dma("pool", I["o_s5re"][sq_, j, d].rearrange("(jj q) -> q jj", q=128), fin[:, sq_, d, 0, :], dst, finb,
                      allow_slow_non_contiguous=True)
                S.dma("pool", I["o_s5im"][sq_, j, d].rearrange("(jj q) -> q jj", q=128), fin[:, sq_, d, 1, :], dst, finb,
                      allow_slow_non_contiguous=True)
    ZGT = C.dram("ZGT", [D, T])
    linear_bf(C, ZT, T, D, wcols(I["s5_w_gate"][j], 0, D, 512), ZGT, mode="fm", act=AF.Sigmoid, mul_ap=ZT)
    Y = C.dram("Y", [T, D])
    linear_bf(C, ZGT, T, D, wcols(I["s5_w_out"][j], 0, D, 512), Y, mode="tm")
    Yb = C.buf("Y", sb=False)
    residual(C, X, Xb, Y, Yb, D, MR["g1"])


def swa_layer(C, cf, I, j, hT, hTb, X, Xb, MR):
    nc, S = C.nc, C.S
    D, T, TP, TK = cf.D, cf.T, cf.TP, cf.TK
    H, KVH, HD = cf.SH, cf.SKV, cf.SHD
    NQ, NK = H * HD, KVH * HD
    NT = NQ + 2 * NK
    QKV = C.dram("QKV", [T, NT])
    linear_bf(C, hT, T, D, wcols(I["swa_w_qkv"][j], 0, NT, 512), QKV, mode="tm")
    qT = C.dram("sqT", [NQ, T])
    kT = C.dram("skT", [NK, TK])
    Vm = C.dram("sVm", [TK, NK])
    nf = HD // 4
    with C.phase("swa_rows"):
        src = C.buf("src", sb=False); dst = C.buf("dst", sb=False)
        ts = TStore(C)
        qr_ = Rot(C, "sq", [128, NT], 2)
        rt = RopeTab(C, nf, HD // 2, cf.GRID_W)
        rtmp = C.sb("rtmp", [128, 4, H + KVH, nf]); rtmpb = C.buf("rtmp")
        for tt in range(0, T, 128):
            q, qb = qr_.next()
            S.dma("sp", q[:], QKV[tt:tt + 128, :], qb, src)
            if tt < TP:
                S.dma("pool", I["o_swak"][j, tt:tt + 128, :], q[:, NQ:NQ + NK], dst, qb)
                S.dma("pool", I["o_swav"][j, tt:tt + 128, :], q[:, NQ + NK:], dst, qb)
            else:
                rt.build((tt - TP) // 128)
                rt.apply(q[:, :NQ + NK].rearrange("p (h r) -> p h r", h=H + KVH), qb, H + KVH, rtmp, rtmpb)
            S.dma("pool", Vm[tt:tt + 128, :], q[:, NQ + NK:], dst, qb)
            ts.run(q[:, :NQ], qb, NQ, qT, dst, tt, 128)
            ts.run(q[:, NQ:NQ + NK], qb, NK, kT, dst, tt, 128)
        cr = Rot(C, "sc", [128, 2 * NK], 2)
        for tt in range(0, cf.PAST, 128):
            c_, cb = cr.next()
            S.dma("sp", c_[:, :NK], I["cache_swa_k"][j, tt:tt + 128, :], cb, src)
            S.dma("sp", c_[:, NK:], I["cache_swa_v"][j, tt:tt + 128, :], cb, src)
            ts.run(c_[:, :NK], cb, NK, kT, dst, T + tt, 128)
            S.dma("pool", Vm[T + tt:T + tt + 128, :], c_[:, NK:], dst, cb)
    OT = C.dram("sOT", [NQ, T])
    items = []
    G = H // KVH
    nctx = cf.PAST
    for s in range(cf.B_P):
        q0 = s * cf.L_P
        for kv in range(KVH):
            items.append(dict(
                kparts=[(kT[kv * HD:(kv + 1) * HD, q0:q0 + cf.L_P], HD)], v=Vm[q0:q0 + cf.L_P, kv * HD:(kv + 1) * HD],
                Lk=cf.L_P, Lq=cf.L_P,
                qheads=[dict(qparts=[(qT[(kv * G + g) * HD:(kv * G + g + 1) * HD, q0:q0 + cf.L_P], HD)],
                             oT=OT[(kv * G + g) * HD:(kv * G + g + 1) * HD, q0:q0 + cf.L_P], sink=kv * G + g) for g in range(G)],
                segs=(lambda qt: [(0, cf.L_P, None)])))
    L = cf.L_S
    nbq = L // 128

    def segs(qt):
        sg = []
        if qt > 0:
            sg.append(((qt - 1) * 128, 128, "ge"))
        sg.append((qt * 128, 128, None))
        if qt < nbq - 1:
            sg.append(((qt + 1) * 128, 128, "le"))
        sg.append((L, nctx, None))
        return sg
    for kv in range(KVH):
        items.append(dict(
            kparts=[(kT[kv * HD:(kv + 1) * HD, TP:TP + L + nctx], HD)], v=Vm[TP:TP + L + nctx, kv * HD:(kv + 1) * HD],
            Lk=L + nctx, Lq=L,
            qheads=[dict(qparts=[(qT[(kv * G + g) * HD:(kv * G + g + 1) * HD, TP:TP + L], HD)],
                         oT=OT[(kv * G + g) * HD:(kv * G + g + 1) * HD, TP:TP + L], sink=kv * G + g) for g in range(G)],
            segs=segs))
    attention(C, items, float(HD) ** -0.5, HD, sink_ap=I["swa_sink"][j], nsink=H, maxk=max(cf.L_P, 3 * 128 + nctx))
    Y = C.dram("Y", [T, D])
    linear_bf(C, OT, T, NQ, wcols(I["swa_w_o"][j], 0, D, 512), Y, mode="tm")
    Yb = C.buf("Y", sb=False)
    residual(C, X, Xb, Y, Yb, D, MR["g1"])


def ilog2(n):
    k = int(round(math.log2(n)))
    assert (1 << k) == n
    return k


def hy_dft_gen(C, L, Cm, Sm, CmT, NSmT):
    nc, S = C.nc, C.S
    with C.phase("dftgen"):
        dst = C.buf("dst", sb=False)
        b = C.buf("dg")
        odi = C.sb("odi", [128, L], I32); od = C.sb("od", [128, L]); tti = C.sb("tti", [128, L], I32); ttf = C.sb("ttf", [128, L])
        pidx = C.sb("pidx", [128, 1], I32); pf = C.sb("pf", [128, 3])
        S.op("pool", lambda: nc.gpsimd.iota(odi[:], pattern=[[2, L]], base=1, channel_multiplier=0), writes=[b])
        S.op("pool", lambda: nc.gpsimd.iota(tti[:], pattern=[[1, L]], base=0, channel_multiplier=0), writes=[b])
        S.op("pool", lambda: nc.gpsimd.iota(pidx[:], pattern=[[0, 1]], base=0, channel_multiplier=1), writes=[b])
        S.op("dve", lambda: nc.vector.tensor_copy(out=od[:], in_=odi[:]), reads=[b], writes=[b])
        S.op("dve", lambda: nc.vector.tensor_copy(out=ttf[:], in_=tti[:]), reads=[b], writes=[b])
        S.op("dve", lambda: nc.vector.tensor_copy(out=pf[:, 0:1], in_=pidx[:]), reads=[b], writes=[b])
        wr = Rot(C, "dgw", [128, L], 2)
        wi = Rot(C, "dgi", [128, L], 2, dtype=I32)
        wm = Rot(C, "dgm", [128, L], 2)
        wo = Rot(C, "dgo", [128, L], 2, dtype=BF16)
        for r0 in range(0, L, 128):
            S.op("dve", lambda r0=r0: nc.vector.tensor_scalar(out=pf[:, 1:2], in0=pf[:, 0:1], scalar1=float(r0), scalar2=None,
                                                        op0=ALU.add), reads=[b], writes=[b])
            S.op("dve", lambda r0=r0: nc.vector.tensor_scalar(out=pf[:, 2:3], in0=pf[:, 0:1], scalar1=2.0, scalar2=float(2 * r0 + 1),
                                                        op0=ALU.mult, op1=ALU.add), reads=[b], writes=[b])
            for (dst_ap, base, col, shift, sgn) in ((Cm, od, 1, L, 1.0), (Sm, od, 1, 0, 1.0), (CmT, ttf, 2, L, 1.0), (NSmT, ttf, 2, 0, -1.0)):
                w, wb = wr.next(); wi_, wib = wi.next(); m, mb = wm.next()
                S.op("dve", lambda w=w, base=base, col=col, shift=shift: nc.vector.tensor_scalar(
                    out=w[:], in0=base[:], scalar1=pf[:, col:col + 1], scalar2=float(shift), op0=ALU.mult, op1=ALU.add),
                    reads=[b], writes=[wb])
                S.op("dve", lambda w=w, wi_=wi_: nc.vector.tensor_copy(out=wi_[:], in_=w[:]), reads=[wb], writes=[wib])
                S.op("dve", lambda wi_=wi_: nc.vector.tensor_single_scalar(out=wi_[:], in_=wi_[:], scalar=4 * L - 1, op=ALU.bitwise_and),
                     reads=[wib], writes=[wib])
                S.op("dve", lambda w=w, wi_=wi_: nc.vector.tensor_copy(out=w[:], in_=wi_[:]), reads=[wib], writes=[wb])
                S.op("dve", lambda w=w, m=m: nc.vector.tensor_scalar(out=m[:], in0=w[:], scalar1=float(2 * L), scalar2=float(-4 * L),
                                                                 op0=ALU.is_ge, op1=ALU.mult), reads=[wb], writes=[mb])
                S.op("dve", lambda w=w, m=m: nc.vector.tensor_tensor(out=w[:], in0=w[:], in1=m[:], op=ALU.add), reads=[mb], writes=[wb])
                o16, o16b = wo.next()
                S.op("act", lambda w=w, sgn=sgn, o16=o16: nc.scalar.activation(out=o16[:], in_=w[:], func=AF.Sin, scale=sgn * PI / (2 * L)),
                     reads=[wb], writes=[o16b])
                S.dma("sp", dst_ap[:, :, r0 // 128, :].rearrange("ot p m -> p ot m"),
                      o16[:].rearrange("p (ot m) -> p ot m", m=128), dst, o16b)


def hy_filter(C, cf, I, j, L, KA, KB):
    nc, S = C.nc, C.S
    D = cf.D
    HH = cf.HY_HID
    NB = cf.HY_BANDS
    HE = 2 * NB + 1
    with C.phase("hyfilt"):
        src = C.buf("src", sb=False); dst = C.buf("dst", sb=False)
        b = C.buf("hf")
        si_ = C.sb("sti", [HH, L], I32); sf_ = C.sb("stf", [HH, L])
        tti = si_[:HE, :]; fti = si_[:HE, :]; fm = sf_[:HE, :]
        tt = C.sb("tt", [HE, L]); ft = C.sb("ft", [HE, L])
        pidx = C.sb("pidx", [128, 1], I32); pi2 = C.sb("pi2", [128, 2], I32); pp = C.sb("pp", [128, 6])
        S.op("pool", lambda: nc.gpsimd.iota(tti[:], pattern=[[1, L]], base=0, channel_multiplier=0), writes=[b])
        S.op("pool", lambda: nc.gpsimd.iota(pidx[:], pattern=[[0, 1]], base=0, channel_multiplier=1), writes=[b])
        S.op("dve", lambda: nc.vector.tensor_copy(out=tt[:], in_=tti[:]), reads=[b], writes=[b])
        S.op("dve", lambda: nc.vector.tensor_single_scalar(out=pi2[:, 0:1], in_=pidx[:], scalar=NB - 1, op=ALU.add), reads=[b], writes=[b])
        S.op("dve", lambda: nc.vector.tensor_single_scalar(out=pi2[:, 0:1], in_=pi2[:, 0:1], scalar=NB - 1, op=ALU.bitwise_and), reads=[b], writes=[b])
        S.op("dve", lambda: nc.vector.tensor_copy(out=pp[:, 0:1], in_=pi2[:, 0:1]), reads=[b], writes=[b])
        S.op("dve", lambda: nc.vector.tensor_copy(out=pp[:, 1:2], in_=pidx[:]), reads=[b], writes=[b])
        S.op("dve", lambda: nc.vector.tensor_scalar(out=pp[:, 2:3], in0=pp[:, 1:2], scalar1=1.0, scalar2=None, op0=ALU.is_ge), reads=[b], writes=[b])
        S.op("dve", lambda: nc.vector.scalar_tensor_tensor(out=pp[:, 3:4], in0=pp[:, 0:1], scalar=1.0, in1=pp[:, 2:3], op0=ALU.add, op1=ALU.mult),
             reads=[b], writes=[b])
        S.op("dve", lambda: nc.vector.tensor_scalar(out=pp[:, 4:5], in0=pp[:, 1:2], scalar1=float(NB + 1), scalar2=float(L // 4), op0=ALU.is_ge,
                                                    op1=ALU.mult), reads=[b], writes=[b])
        S.op("dve", lambda: nc.vector.tensor_scalar(out=ft[:], in0=tt[:], scalar1=pp[:HE, 3:4], scalar2=pp[:HE, 4:5], op0=ALU.mult, op1=ALU.add),
             reads=[b], writes=[b])
        S.op("dve", lambda: nc.vector.tensor_copy(out=fti[:], in_=ft[:]), reads=[b], writes=[b])
        S.op("dve", lambda: nc.vector.tensor_single_scalar(out=fti[:], in_=fti[:], scalar=L - 1, op=ALU.bitwise_and), reads=[b], writes=[b])
        S.op("dve", lambda: nc.vector.tensor_copy(out=ft[:], in_=fti[:]), reads=[b], writes=[b])
        S.op("dve", lambda: nc.vector.tensor_scalar(out=fm[:], in0=ft[:], scalar1=float(L // 2), scalar2=float(-L), op0=ALU.is_ge, op1=ALU.mult),
             reads=[b], writes=[b])
        S.op("dve", lambda: nc.vector.tensor_tensor(out=ft[:], in0=ft[:], in1=fm[:], op=ALU.add), reads=[b], writes=[b])
        S.op("act", lambda: nc.scalar.activation(out=ft[:], in_=ft[:], func=AF.Sin, scale=2 * PI / L), reads=[b], writes=[b])
        S.op("dve", lambda: nc.vector.tensor_scalar(out=ft[0:1, :], in0=tt[0:1, :], scalar1=1.0 / L, scalar2=None, op0=ALU.mult), reads=[b], writes=[b])
        w1 = C.sb("w1", [HE, HH]); w2 = C.sb("w2", [HH, HH]); w3 = C.sb("w3", [HH, 4 * D]); fr = C.sb("fr", [HH, 6])
        S.dma("sp", w1[:], I["hy_f_w1"][j], b, src)
        S.dma("sp", w2[:], I["hy_f_w2"][j], b, src)
        S.dma("sp", w3[:], I["hy_f_w3"][j], b, src)
        S.dma("sp", fr[:, 0:2], I["hy_f_freq"][j].rearrange("two h -> h two"), b, src, allow_slow_non_contiguous=True)
        S.dma("sp", fr[:, 2:3], I["hy_f_b1"][j].rearrange("(h o) -> h o", o=1), b, src, allow_slow_non_contiguous=True)
        S.dma("sp", fr[:, 3:4], I["hy_f_b2"][j].rearrange("(h o) -> h o", o=1), b, src, allow_slow_non_contiguous=True)
        S.op("dve", lambda: nc.vector.tensor_tensor(out=fr[:, 4:6], in0=fr[:, 0:2], in1=fr[:, 2:4], op=ALU.mult), reads=[b], writes=[b])
        s1 = C.sb("s1T", [HH, L]); s2 = C.sb("s2T", [HH, L])
        zp = Rot(C, "hfz", [HH, 512], 2, psum=True)
        for (lw, rhs, out_, k) in ((w1, ft, s1, 0), (w2, s1, s2, 1)):
            for c0 in range(0, L, 512):
                n = min(512, L - c0)
                z, zb = zp.next()
                S.mm_group([lambda z=z, lw=lw, rhs=rhs, c0=c0, n=n: nc.tensor.matmul(z[:, :n], lhsT=lw[:], rhs=rhs[:, c0:c0 + n],
                                                                                start=True, stop=True)], reads=[b], writes=[zb])
                S.op("dve", lambda z=z, out_=out_, c0=c0, n=n, k=k: nc.vector.tensor_scalar(
                    out=out_[:, c0:c0 + n], in0=z[:, :n], scalar1=fr[:, k:k + 1], scalar2=fr[:, 4 + k:5 + k], op0=ALU.mult, op1=ALU.add),
                    reads=[zb, b], writes=[b])
            range_reduce(C, out_[:], b, si_[:], sf_[:], b)
            S.op("act", lambda out_=out_: nc.scalar.activation(out=out_[:], in_=out_[:], func=AF.Sin), reads=[b], writes=[b])
        E, Eb = bc_load(C, "E", I["hy_log_decay"][j], 4 * D)
        S.op("act", lambda: nc.scalar.activation(out=E[:], in_=E[:], func=AF.Exp), reads=[Eb], writes=[Eb])
        FT = C.sb("FT", [128, 4 * D]); FTb = C.buf("FT")
        dec = Rot(C, "dec", [128, 512], 2)
        fp = Rot(C, "hff", [128, 512], 2, psum=True)
        AB = Rot(C, "AB", [128, 2 * D], 1)
        negt = C.sb("negt", [128, 1])
        for t0 in range(0, L, 128):
            S.op("dve", lambda t0=t0: nc.vector.tensor_scalar(out=negt[:], in0=pp[:, 1:2], scalar1=float(t0), scalar2=-1.0 / L, op0=ALU.add,
                                                        op1=ALU.mult), reads=[b], writes=[b])
            for c0 in range(0, 4 * D, 512):
                f, fb = fp.next()
                S.mm_group([lambda f=f, t0=t0, c0=c0: nc.tensor.matmul(f[:], lhsT=s2[:, t0:t0 + 128], rhs=w3[:, c0:c0 + 512], start=True,
                                                                        stop=True)], reads=[b], writes=[fb])
                dc, dcb = dec.next()
                S.op("act", lambda dc=dc, c0=c0: nc.scalar.activation(out=dc[:], in_=E[:, c0:c0 + 512], func=AF.Exp, scale=negt[:, 0:1]),
                     reads=[Eb, b], writes=[dcb])
                S.op("dve", lambda f=f, dc=dc, c0=c0: nc.vector.tensor_tensor(out=FT[:, c0:c0 + 512], in0=f[:], in1=dc[:], op=ALU.mult),
                     reads=[fb, dcb], writes=[FTb])
            if t0 == 0:
                S.op("dve", lambda: nc.vector.memset(FT[0:1, 2 * D:4 * D], 0.0), writes=[FTb])
            A, Ab = AB.next()
            S.op("dve", lambda A=A: nc.vector.tensor_tensor(out=A[:], in0=FT[:, 0:2 * D], in1=FT[:, 2 * D:4 * D], op=ALU.add), reads=[FTb], writes=[Ab])
            S.dma("pool", KA[t0:t0 + 128, :], A[:], dst, Ab)
            B_, Bb = AB.next()
            S.op("dve", lambda B_=B_: nc.vector.tensor_tensor(out=B_[:], in0=FT[:, 2 * D:4 * D], in1=FT[:, 0:2 * D], op=ALU.subtract), reads=[FTb], writes=[Bb])
            S.dma("pool", KB[t0:t0 + 128, :], B_[:], dst, Bb)


class DFTPass:
    def __init__(self, C, L):
        self.C, self.L = C, L
        self.LC = L // 128
        self.mr = Rot(C, "dfm", [128, self.LC, 128], 6, dtype=BF16)
        self.src = C.buf("dfsrc", sb=False)

    def run(self, pairs, ps, psb, ot, n):
        C = self.C
        nc, S = C.nc, C.S
        fns = []
        reads = []
        tot = len(pairs) * self.LC
        k = 0
        for (mat, rhs, rhsb) in pairs:
            m, mb = self.mr.next()
            S.dma("sp", m[:], mat[ot], mb, self.src)
            reads += [mb, rhsb]
            for c in range(self.LC):
                fns.append(lambda m=m, rhs=rhs, c=c, k=k: nc.tensor.matmul(ps[:, :n], lhsT=m[:, c, :], rhs=rhs[:, c, :n],
                                                                           start=(k == 0), stop=(k == tot - 1)))
                k += 1
        S.mm_group(fns, reads=reads, writes=[psb])


def hy_layer(C, cf, I, j, hT, hTb, X, Xb, MR):
    nc, S = C.nc, C.S
    D, T, TP = cf.D, cf.T, cf.TP
    Z = C.dram("hyZ", [T, 3 * D])
    linear_bf(C, hT, T, D, wcols(I["hy_w_in"][j], 0, 3 * D, 512), Z, mode="tm", bias_ap=I["hy_b_in"][j])
    XV = C.dram("hyXV", [T, 3 * D])
    seqs = [(s * cf.L_P, cf.L_P) for s in range(cf.B_P)] + [(TP, cf.L_S)]
    CW = min(D, 2048)
    with C.phase("hyshort"):
        src = C.buf("src", sb=False); dst = C.buf("dst", sb=False)
        cr = Rot(C, "hc", [128, CW], 2); pr = Rot(C, "hp", [128, CW], 2); nr = Rot(C, "hn", [128, CW], 2)
        t1r = Rot(C, "ht1", [128, CW], 2); t2r = Rot(C, "ht2", [128, CW], 2)
        for c0 in range(0, 3 * D, CW):
            ws = []
            for k in range(3):
                ws.append(bc_load(C, "hw%d" % k, I["hy_short_w"][j, k, c0:c0 + CW], CW))
            sbt, sbb = bc_load(C, "hsb", I["hy_short_b"][j, c0:c0 + CW], CW)
            for (s0, L) in seqs:
                for tt in range(s0, s0 + L, 128):
                    cu, cub = cr.next(); pv, pvb = pr.next(); nx, nxb = nr.next()
                    S.dma("sp", cu[:], Z[tt:tt + 128, c0:c0 + CW], cub, src)
                    if tt == s0:
                        S.op("pool", lambda pv=pv: nc.gpsimd.memset(pv[:], 0.0), writes=[pvb])
                        S.dma("sp", pv[1:128, :], Z[tt:tt + 127, c0:c0 + CW], pvb, src)
                    else:
                        S.dma("sp", pv[:], Z[tt - 1:tt + 127, c0:c0 + CW], pvb, src)
                    if tt + 128 == s0 + L:
                        S.op("pool", lambda nx=nx: nc.gpsimd.memset(nx[:], 0.0), writes=[nxb])
                        S.dma("sp", nx[0:127, :], Z[tt + 1:tt + 128, c0:c0 + CW], nxb, src)
                    else:
                        S.dma("sp", nx[:], Z[tt + 1:tt + 129, c0:c0 + CW], nxb, src)
                    a, ab = t1r.next(); b2, b2b = t2r.next()
                    S.op("dve", lambda a=a, pv=pv: nc.vector.tensor_tensor(out=a[:], in0=pv[:], in1=ws[0][0][:], op=ALU.mult), reads=[pvb, ws[0][1]], writes=[ab])
                    S.op("pool", lambda b2=b2, cu=cu: nc.gpsimd.tensor_tensor(out=b2[:], in0=cu[:], in1=ws[1][0][:], op=ALU.mult), reads=[cub, ws[1][1]], writes=[b2b])
                    S.op("dve", lambda a=a, b2=b2: nc.vector.tensor_tensor(out=a[:], in0=a[:], in1=b2[:], op=ALU.add), reads=[b2b], writes=[ab])
                    S.op("pool", lambda b2=b2, nx=nx: nc.gpsimd.tensor_tensor(out=b2[:], in0=nx[:], in1=ws[2][0][:], op=ALU.mult), reads=[nxb, ws[2][1]], writes=[b2b])
                    S.op("dve", lambda a=a, b2=b2: nc.vector.tensor_tensor(out=a[:], in0=a[:], in1=b2[:], op=ALU.add), reads=[b2b], writes=[ab])
                    S.op("dve", lambda a=a: nc.vector.tensor_tensor(out=a[:], in0=a[:], in1=sbt[:], op=ALU.add), reads=[sbb], writes=[ab])
                    S.dma("pool", XV[tt:tt + 128, c0:c0 + CW], a[:], dst, ab)
    V1 = C.dram("hyV1", [T, D])
    V2T = C.dram("hyV2T", [D, T])
    done_L = {}
    for (s0, L) in seqs:
        if L not in done_L:
            LCd = L // 128
            Cm = C.dram("Cm", [LCd, 128, LCd, 128], BF16); Sm = C.dram("Sm", [LCd, 128, LCd, 128], BF16)
            CmT = C.dram("CmT", [LCd, 128, LCd, 128], BF16); NSmT = C.dram("NSmT", [LCd, 128, LCd, 128], BF16)
            hy_dft_gen(C, L, Cm, Sm, CmT, NSmT)
            KA = C.dram("KA", [L, 2 * D]); KB = C.dram("KB", [L, 2 * D])
            hy_filter(C, cf, I, j, L, KA, KB)
            KR = C.dram("KR", [L, 2 * D]); KI = C.dram("KI", [L, 2 * D])
            LC = L // 128
            with C.phase("hykhat"):
                src = C.buf("src", sb=False); dst = C.buf("dst", sb=False)
                dp = DFTPass(C, L)
                ar = Rot(C, "ka", [128, LC, 512], 2, dtype=BF16); br = Rot(C, "kb", [128, LC, 512], 2, dtype=BF16)
                pp = Rot(C, "kp", [128, 512], 4, psum=True)
                orr = Rot(C, "ko", [128, 512], 4)
                for c0 in range(0, 2 * D, 512):
                    a, ab = ar.next(); b_, bb = br.next()
                    S.dma("pool", a[:], KA[:, c0:c0 + 512].rearrange("(c p) n -> p c n", p=128), ab, src)
                    S.dma("pool", b_[:], KB[:, c0:c0 + 512].rearrange("(c p) n -> p c n", p=128), bb, src)
                    for ot in range(LC):
                        for (mat, rhs, rb, out_d) in ((Cm, a, ab, KR), (Sm, b_, bb, KI)):
                            p_, pb_ = pp.next()
                            dp.run([(mat, rhs, rb)], p_, pb_, ot, 512)
                            o, ob = orr.next()
                            S.op("act", lambda o=o, p_=p_: nc.scalar.copy(out=o[:], in_=p_[:]), reads=[pb_], writes=[ob])
                            S.dma("sp", out_d[ot * 128:(ot + 1) * 128, c0:c0 + 512], o[:], dst, ob)
            done_L[L] = (Cm, Sm, CmT, NSmT, KR, KI)
    for (s0, L) in seqs:
        Cm, Sm, CmT, NSmT, KR, KI = done_L[L]
        LC = L // 128
        for o_ in range(2):
            Uin = XV[:, 2 * D:3 * D] if o_ == 0 else V1
            gate = XV[:, o_ * D:(o_ + 1) * D]
            with C.phase("hyconv"):
                src = C.buf("src", sb=False); dst = C.buf("dst", sb=False)
                dp = DFTPass(C, L)
                ur = Rot(C, "cu", [128, LC, 512], 2, dtype=BF16)
                uf = Rot(C, "cuf", [128, 512], 2)
                yr = Rot(C, "cyr", [128, LC, 512], 1, dtype=BF16); yi = Rot(C, "cyi", [128, LC, 512], 1, dtype=BF16)
                pp = Rot(C, "cp", [128, 512], 4, psum=True)
                kr = Rot(C, "ckr", [128, 512], 2); ki = Rot(C, "cki", [128, 512], 2)
                tr = Rot(C, "ct", [128, 512], 4)
                gr_ = Rot(C, "cg", [128, 512], 2)
                orr = Rot(C, "co", [128, 512], 3)
                bia, biab = bc_load(C, "cb", I["hy_bias"][j, o_], D)
                ts = TStore(C) if o_ == 1 else None
                for c0 in range(0, D, 512):
                    n = min(512, D - c0)
                    u, ub = ur.next()
                    S.dma("pool", u[:, :, :n], Uin[s0:s0 + L, c0:c0 + n].rearrange("(c p) n -> p c n", p=128), ub, src)
                    Yr, Yrb = yr.next(); Yi, Yib = yi.next()
                    for ot in range(LC):
                        pc, pcb = pp.next(); ps_, psb_ = pp.next()
                        dp.run([(Cm, u, ub)], pc, pcb, ot, n)
                        dp.run([(Sm, u, ub)], ps_, psb_, ot, n)
                        k_r, krb = kr.next(); k_i, kib = ki.next()
                        S.dma("sp", k_r[:, :n], KR[ot * 128:(ot + 1) * 128, o_ * D + c0:o_ * D + c0 + n], krb, src)
                        S.dma("sp", k_i[:, :n], KI[ot * 128:(ot + 1) * 128, o_ * D + c0:o_ * D + c0 + n], kib, src)
                        ta, tab = tr.next(); tb_, tbb = tr.next()
                        S.op("dve", lambda: nc.vector.tensor_tensor(out=ta[:, :n], in0=pc[:, :n], in1=k_r[:, :n], op=ALU.mult), reads=[pcb, krb], writes=[tab])
                        S.op("dve", lambda: nc.vector.tensor_tensor(out=tb_[:, :n], in0=ps_[:, :n], in1=k_i[:, :n], op=ALU.mult), reads=[psb_, kib], writes=[tbb])
                        S.op("pool", lambda: nc.gpsimd.tensor_tensor(out=Yr[:, ot, :n], in0=ta[:, :n], in1=tb_[:, :n], op=ALU.add), reads=[tab, tbb], writes=[Yrb])
                        tc_, tcb = tr.next(); td, tdb = tr.next()
                        S.op("dve", lambda: nc.vector.tensor_tensor(out=tc_[:, :n], in0=pc[:, :n], in1=k_i[:, :n], op=ALU.mult), reads=[pcb, kib], writes=[tcb])
                        S.op("dve", lambda: nc.vector.tensor_tensor(out=td[:, :n], in0=ps_[:, :n], in1=k_r[:, :n], op=ALU.mult), reads=[psb_, krb], writes=[tdb])
                        S.op("pool", lambda: nc.gpsimd.tensor_tensor(out=Yi[:, ot, :n], in0=tc_[:, :n], in1=td[:, :n], op=ALU.subtract), reads=[tcb, tdb], writes=[Yib])
                    for ot in range(LC):
                        py, pyb = pp.next()
                        dp.run([(CmT, Yr, Yrb), (NSmT, Yi, Yib)], py, pyb, ot, n)
                        g, gb = gr_.next()
                        S.dma("sp", g[:, :n], gate[s0 + ot * 128:s0 + (ot + 1) * 128, c0:c0 + n], gb, src)
                        ta, tab = tr.next()
                        u32, u32b = uf.next()
                        S.dma("sp", u32[:, :n], Uin[s0 + ot * 128:s0 + (ot + 1) * 128, c0:c0 + n], u32b, src)
                        S.op("dve", lambda: nc.vector.tensor_tensor(out=ta[:, :n], in0=u32[:, :n], in1=bia[:, c0:c0 + n], op=ALU.mult), reads=[u32b, biab], writes=[tab])
                        S.op("dve", lambda: nc.vector.scalar_tensor_tensor(out=ta[:, :n], in0=py[:, :n], scalar=1.0 / L, in1=ta[:, :n], op0=ALU.mult,
                                                                          op1=ALU.add), reads=[pyb], writes=[tab])
                        o, ob = orr.next()
                        S.op("pool", lambda: nc.gpsimd.tensor_tensor(out=o[:, :n], in0=ta[:, :n], in1=g[:, :n], op=ALU.mult), reads=[tab, gb], writes=[ob])
                        if o_ == 0:
                            S.dma("sp", V1[s0 + ot * 128:s0 + (ot + 1) * 128, c0:c0 + n], o[:, :n], dst, ob)
                        else:
                            ts.run(o, ob, n, V2T, dst, s0 + ot * 128, 128, col0=c0)
    Y = C.dram("Y", [T, D])
    linear_bf(C, V2T, T, D, wcols(I["hy_w_out"][j], 0, D, 512), Y, mode="tm")
    Yb = C.buf("Y", sb=False)
    residual(C, X, Xb, Y, Yb, D, MR["g1"])
```

```python
import math
import numpy as np
import concourse.bass as bass
import concourse.mybir as mybir

F32 = mybir.dt.float32
BF16 = mybir.dt.bfloat16
I32 = mybir.dt.int32
U32 = mybir.dt.uint32
AF = mybir.ActivationFunctionType
ALU = mybir.AluOpType
AX = mybir.AxisListType


DEBUG_SEM = False


class Buf:
    __slots__ = ("name", "w", "r", "dsem", "sb")

    def __init__(self, name, sb=True):
        self.name = name
        self.sb = sb
        self.w = {}
        self.r = {}
        self.dsem = None


class Sched:
    ENGS = ("pe", "act", "dve", "pool", "sp")

    def __init__(self, nc):
        self.nc = nc
        self.eng = {"pe": nc.tensor, "act": nc.scalar, "dve": nc.vector, "pool": nc.gpsimd, "sp": nc.sync}
        self.sem = {e: nc.alloc_semaphore("S_" + e) for e in self.ENGS}
        self.cnt = {e: 0 for e in self.ENGS}
        self.seen = {e: {} for e in self.ENGS}
        self.semobj = {}
        for e in self.ENGS:
            self.semobj[id(self.sem[e])] = self.sem[e]
        self.dpool = {"sw": [], "hw": []}
        self.dcnt = {}
        self.dused = []
        self.n_inst = 0

    def _get_dsem(self, kind):
        if self.dpool[kind]:
            s = self.dpool[kind].pop()
        else:
            s = self.nc.alloc_semaphore("D%d" % len(self.dcnt))
            self.dcnt[id(s)] = 0
            self.semobj[id(s)] = s
        self.dused.append((kind, s))
        return s

    def _wait(self, e, deps):
        seen = self.seen[e]
        for sid, val in deps.items():
            if seen.get(sid, 0) < val:
                self.eng[e].wait_ge(self.semobj[sid], val)
                seen[sid] = val
                self.n_inst += 1

    @staticmethod
    def _merge(dst, src):
        for k, v in src.items():
            if dst.get(k, 0) < v:
                dst[k] = v

    def op(self, e, fn, reads=(), writes=()):
        deps = {}
        for b in reads:
            self._merge(deps, b.w)
        for b in writes:
            self._merge(deps, b.w)
            self._merge(deps, b.r)
        self._wait(e, deps)
        ins = fn()
        self.cnt[e] += 1
        ins.then_inc(self.sem[e], 1)
        self.n_inst += 1
        tok = {id(self.sem[e]): self.cnt[e]}
        self.seen[e][id(self.sem[e])] = max(self.seen[e].get(id(self.sem[e]), 0), 0)
        for b in writes:
            b.w = dict(tok)
            b.r = {}
        for b in reads:
            if b not in writes:
                self._merge(b.r, tok)
        return ins

    def mm_group(self, fns, reads=(), writes=()):
        deps = {}
        for b in reads:
            self._merge(deps, b.w)
        for b in writes:
            self._merge(deps, b.w)
            self._merge(deps, b.r)
        deps.pop(id(self.sem["pe"]), None)
        self._wait("pe", deps)
        ins = None
        for fn in fns:
            ins = fn()
            self.n_inst += 1
        self.cnt["pe"] += 1
        ins.then_inc(self.sem["pe"], 1)
        tok = {id(self.sem["pe"]): self.cnt["pe"]}
        for b in writes:
            b.w = dict(tok)
            b.r = {}
        for b in reads:
            if b not in writes:
                self._merge(b.r, tok)

    def dma(self, q, out_ap, in_ap, wbuf, rbuf, indirect=None, owner=None, **kw):
        if owner is None:
            owner = wbuf if (wbuf.sb or rbuf is None or not rbuf.sb) else rbuf
        kind = "sw" if q == "pool" else "hw"
        if owner.dsem is None:
            owner.dsem = {}
        if kind not in owner.dsem:
            owner.dsem[kind] = self._get_dsem(kind)
        ds = owner.dsem[kind]
        deps = {}
        if rbuf is not None:
            self._merge(deps, rbuf.w)
        for k, v in wbuf.w.items():
            if k not in self.dcnt:
                if deps.get(k, 0) < v:
                    deps[k] = v
        if wbuf.sb:
            for k, v in wbuf.w.items():
                if k in self.dcnt and k != id(ds) and deps.get(k, 0) < v:
                    deps[k] = v
        self._merge(deps, wbuf.r)
        if indirect is not None and indirect.get("idx_buf") is not None:
            self._merge(deps, indirect["idx_buf"].w)
        self._wait(q, deps)
        if indirect is None:
            ins = self.eng[q].dma_start(out=out_ap, in_=in_ap, **kw)
        else:
            ins = self.eng[q].indirect_dma_start(out=out_ap, out_offset=indirect.get("out_offset"),
                                                 in_=in_ap, in_offset=indirect.get("in_offset"), **kw)
        self.dcnt[id(ds)] += 16
        ins.then_inc(ds, 16)
        self.n_inst += 1
        tok = {id(ds): self.dcnt[id(ds)]}
        self._merge(wbuf.w, tok)
        if rbuf is not None:
            self._merge(rbuf.r, tok)
        if indirect is not None and indirect.get("idx_buf") is not None:
            self._merge(indirect["idx_buf"].r, tok)
        return ins

    def barrier(self, bufs=()):
        deps = {}
        for e in self.ENGS:
            if self.cnt[e] > 0:
                deps[id(self.sem[e])] = self.cnt[e]
        for _, s in self.dused:
            if self.dcnt[id(s)] > 0:
                deps[id(s)] = self.dcnt[id(s)]
        for e in self.ENGS:
            d = {k: v for k, v in deps.items() if k != id(self.sem[e]) or True}
            self._wait(e, d)
        for b in bufs:
            b.w = {}
            b.r = {}
            b.dsem = None
        for kind, s in self.dused:
            self.dpool[kind].append(s)
        self.dused = []


from contextlib import contextmanager, ExitStack


class Ctx:
    def __init__(self, nc):
        self.nc = nc
        self.S = Sched(nc)
        self.bufs = []
        self._st = None
        self._uid = 0
        self.ident = None

    def buf(self, name="b", sb=True):
        b = Buf(name, sb)
        self.bufs.append(b)
        return b

    def uid(self, name):
        self._uid += 1
        return "%s_%d" % (name, self._uid)

    @contextmanager
    def phase(self, name="ph"):
        st = ExitStack()
        old = self._st
        self._st = st
        nb = len(self.bufs)
        issue_casts(self, k=3)
        try:
            yield
        finally:
            pass
        self.S.barrier(self.bufs)
        del self.bufs[nb:]
        st.close()
        self._st = old

    def sb(self, name, shape, dtype=F32):
        t = self._st.enter_context(self.nc.sbuf_tensor(self.uid(name), list(shape), dtype))
        return t

    def ps(self, name, shape, dtype=F32):
        t = self._st.enter_context(self.nc.psum_tensor(self.uid(name), list(shape), dtype))
        return t

    def dram(self, name, shape, dtype=F32):
        t = self.nc.dram_tensor(self.uid(name), list(shape), dtype)
        return t.ap()

    def make_ident(self):
        nc, S = self.nc, self.S
        self.ident_t = nc.alloc_sbuf_tensor("ident", [128, 128], F32)
        self.ident = self.ident_t
        b = self.buf("ident")
        self.ident_b = b
        S.op("pool", lambda: nc.gpsimd.memset(self.ident_t[:], 0.0), writes=[b])
        S.op("pool", lambda: nc.gpsimd.affine_select(out=self.ident_t[:], in_=self.ident_t[:], pattern=[[-1, 128]],
                                                      compare_op=ALU.not_equal, fill=1.0, base=0,
                                                      channel_multiplier=1), reads=[b], writes=[b])
        S.barrier(self.bufs)


def cdiv(a, b):
    return (a + b - 1) // b


def linear(C, xT, T, K, wblocks, out_ap, mode="tm", act=None, bias_ap=None, mul_ap=None, tg=512, xbuf=None, obuf=None,
           out_col0=0):
    nc, S = C.nc, C.S
    KC = cdiv(K, 128)
    kp = min(K, 128)
    assert K % kp == 0
    with C.phase("lin"):
        xb = xbuf if xbuf is not None else C.buf("x", sb=False)
        ob = obuf if obuf is not None else C.buf("o", sb=False)
        wsrc = C.buf("wsrc", sb=False)
        nblk = len(wblocks)
        maxnb = max(nb for _, nb in wblocks)
        xt = [C.sb("xt%d" % i, [kp, KC, tg]) for i in range(2)]
        xtb = [C.buf("xt") for i in range(2)]
        wt = [C.sb("wt%d" % i, [kp, KC, maxnb]) for i in range(2)]
        wtb = [C.buf("wt") for i in range(2)]
        if mode == "tm":
            pst = [C.ps("ps%d" % i, [128, 512]) for i in range(4)]
            osb = [C.sb("osb%d" % i, [128, 512]) for i in range(3)]
        else:
            pst = [C.ps("ps%d" % i, [128, tg]) for i in range(4)]
            osb = [C.sb("osb%d" % i, [128, tg]) for i in range(3)]
        psb = [C.buf("ps") for i in range(4)]
        osbb = [C.buf("osb") for i in range(3)]
        mulsb = None
        if mul_ap is not None:
            mulsb = [C.sb("mul%d" % i, [128, tg if mode == "fm" else 512]) for i in range(2)]
            mulb = [C.buf("mul") for i in range(2)]
        biast = None
        if bias_ap is not None:
            if mode == "tm":
                Ntot = sum(nb for _, nb in wblocks)
                biast = C.sb("bias", [128, Ntot])
                biasb = C.buf("bias")
                S.dma("sp", biast[:], bias_ap.partition_broadcast(128), biasb, None)
            else:
                Ntot = sum(nb for _, nb in wblocks)
                biast = C.sb("bias", [128, cdiv(Ntot, 128)])
                biasb = C.buf("bias")
                S.dma("sp", biast[:], bias_ap.rearrange("(c p) -> p c", p=128), biasb, None,
                      allow_slow_non_contiguous=True)
        ngroups = cdiv(T, tg)
        it = 0
        pi = 0
        oi = 0
        mi = 0
        for g in range(ngroups):
            t0 = g * tg
            tn = min(tg, T - t0)
            xs = g % 2
            S.dma("sp", xt[xs][:, :, :tn], xT[:, t0:t0 + tn].rearrange("(c p) t -> p c t", p=kp), xtb[xs], xb)
            col = 0
            for bi, (pieces, nb) in enumerate(wblocks):
                ws = it % 2
                it += 1
                for (co, wap) in pieces:
                    n = wap.shape[-1]
                    S.dma("sp", wt[ws][:, :, co:co + n], wap.rearrange("(c p) n -> p c n", p=kp), wtb[ws], wsrc)
                if mode == "tm":
                    for tt in range(cdiv(tn, 128)):
                        tsz = min(128, tn - tt * 128)
                        p = pi % 4
                        pi += 1
                        fns = []
                        for kc in range(KC):
                            fns.append(lambda kc=kc, p=p, tt=tt, tsz=tsz, xs=xs, ws=ws, nb=nb: nc.tensor.matmul(
                                pst[p][:tsz, :nb], lhsT=xt[xs][:, kc, tt * 128:tt * 128 + tsz], rhs=wt[ws][:, kc, :nb],
                                start=(kc == 0), stop=(kc == KC - 1)))
                        S.mm_group(fns, reads=[xtb[xs], wtb[ws]], writes=[psb[p]])
                        o = oi % 3
                        oi += 1
                        _epilogue(C, pst[p][:tsz, :nb], psb[p], osb[o][:tsz, :nb], osbb[o], act,
                                  None if biast is None else biast[:tsz, col:col + nb], biasb if biast is not None else None,
                                  None, None)
                        if mul_ap is not None:
                            m = mi % 2
                            mi += 1
                            S.dma("sp", mulsb[m][:tsz, :nb], mul_ap[t0 + tt * 128:t0 + tt * 128 + tsz, col:col + nb], mulb[m], None)
                            S.op("dve", lambda o=o, m=m, tsz=tsz, nb=nb: nc.vector.tensor_tensor(
                                out=osb[o][:tsz, :nb], in0=osb[o][:tsz, :nb], in1=mulsb[m][:tsz, :nb], op=ALU.mult),
                                reads=[mulb[m]], writes=[osbb[o]])
                        S.dma("pool", out_ap[t0 + tt * 128:t0 + tt * 128 + tsz, out_col0 + col:out_col0 + col + nb],
                              osb[o][:tsz, :nb], ob, osbb[o])
                else:
                    assert nb <= 128
                    p = pi % 4
                    pi += 1
                    fns = []
                    for kc in range(KC):
                        fns.append(lambda kc=kc, p=p, xs=xs, ws=ws, nb=nb, tn=tn: nc.tensor.matmul(
                            pst[p][:nb, :tn], lhsT=wt[ws][:, kc, :nb], rhs=xt[xs][:, kc, :tn],
                            start=(kc == 0), stop=(kc == KC - 1)))
                    S.mm_group(fns, reads=[xtb[xs], wtb[ws]], writes=[psb[p]])
                    o = oi % 3
                    oi += 1
                    bcol = None
                    if biast is not None:
                        assert col % 128 == 0
                        bcol = biast[:nb, col // 128:col // 128 + 1]
                    _epilogue(C, pst[p][:nb, :tn], psb[p], osb[o][:nb, :tn], osbb[o], act, None, None,
                              bcol, biasb if biast is not None else None)
                    if mul_ap is not None:
                        m = mi % 2
                        mi += 1
                        S.dma("sp", mulsb[m][:nb, :tn], mul_ap[out_col0 + col:out_col0 + col + nb, t0:t0 + tn], mulb[m], None)
                        S.op("dve", lambda o=o, m=m, tn=tn, nb=nb: nc.vector.tensor_tensor(
                            out=osb[o][:nb, :tn], in0=osb[o][:nb, :tn], in1=mulsb[m][:nb, :tn], op=ALU.mult),
                            reads=[mulb[m]], writes=[osbb[o]])
                    S.dma("pool", out_ap[out_col0 + col:out_col0 + col + nb, t0:t0 + tn], osb[o][:nb, :tn], ob, osbb[o])
                col += nb


def linear_bf(C, xT, T, K, wblocks, out_ap, mode="tm", act=None, bias_ap=None, mul_ap=None, out_col0=0):
    nc, S = C.nc, C.S
    KC = cdiv(K, 128)
    kp = min(K, 128)
    assert K % kp == 0
    TG = 512
    with C.phase("linbf"):
        xb = C.buf("x", sb=False)
        ob = C.buf("o", sb=False)
        wsrc = C.buf("wsrc", sb=False)
        ngroups = cdiv(T, TG)
        xres = C.sb("xres", [kp, KC, T], BF16)
        xrb = [C.buf("xres") for g in range(ngroups)]
        for g in range(ngroups):
            t0 = g * TG
            tn = min(TG, T - t0)
            S.dma("pool", xres[:, :, t0:t0 + tn], xT[:, t0:t0 + tn].rearrange("(c p) t -> p c t", p=kp), xrb[g], xb)
        maxnb = max(nb for _, nb in wblocks)
        wr = Rot(C, "wbf", [kp, KC, maxnb], 3, dtype=BF16)
        pw = 512
        pst = Rot(C, "ps", [128, pw], 4, psum=True)
        osb = Rot(C, "osb", [128, pw], 3)
        mulr = Rot(C, "mul", [128, pw], 2) if mul_ap is not None else None
        biast = None
        biasb = None
        Ntot = sum(nb for _, nb in wblocks)
        if bias_ap is not None:
            if mode == "tm":
                biast = C.sb("bias", [128, Ntot])
                biasb = C.buf("bias")
                S.dma("sp", biast[:], bias_ap.partition_broadcast(128), biasb, None)
            else:
                biast = C.sb("bias", [128, cdiv(Ntot, 128)])
                biasb = C.buf("bias")
                S.dma("sp", biast[:], bias_ap.rearrange("(c p) -> p c", p=128), biasb, None,
                      allow_slow_non_contiguous=True)

        def load_w(bi):
            pieces, nb = wblocks[bi]
            w, wb = wr.next()
            for (co, wap) in pieces:
                n = wap.shape[-1]
                S.dma("pool", w[:, :, co:co + n], wap.rearrange("(c p) n -> p c n", p=kp), wb, wsrc)
            return w, wb
        nxt = load_w(0)
        col = 0
        for bi, (pieces, nb) in enumerate(wblocks):
            w, wb = nxt
            if bi + 1 < len(wblocks):
                nxt = load_w(bi + 1)
            if mode == "tm":
                for tt in range(0, T, 128):
                    tsz = min(128, T - tt)
                    g = tt // TG
                    p, pb = pst.next()
                    fns = []
                    for kc in range(KC):
                        fns.append(lambda kc=kc, p=p, tt=tt, tsz=tsz, w=w, nb=nb: nc.tensor.matmul(
                            p[:tsz, :nb], lhsT=xres[:, kc, tt:tt + tsz], rhs=w[:, kc, :nb], start=(kc == 0), stop=(kc == KC - 1)))
                    S.mm_group(fns, reads=[xrb[g], wb], writes=[pb])
                    o, ob_ = osb.next()
                    _epilogue(C, p[:tsz, :nb], pb, o[:tsz, :nb], ob_, act,
                              None if biast is None else biast[:tsz, col:col + nb], biasb, None, None)
                    if mul_ap is not None:
                        m, mb = mulr.next()
                        S.dma("sp", m[:tsz, :nb], mul_ap[tt:tt + tsz, col:col + nb], mb, None)
                        S.op("dve", lambda o=o, m=m, tsz=tsz, nb=nb: nc.vector.tensor_tensor(
                            out=o[:tsz, :nb], in0=o[:tsz, :nb], in1=m[:tsz, :nb], op=ALU.mult), reads=[mb], writes=[ob_])
                    S.dma("sp", out_ap[tt:tt + tsz, out_col0 + col:out_col0 + col + nb], o[:tsz, :nb], ob, ob_)
            else:
                for sb0 in range(0, nb, 128):
                    sn = min(128, nb - sb0)
                    for g in range(ngroups):
                        t0 = g * TG
                        tn = min(TG, T - t0)
                        p, pb = pst.next()
                        fns = []
                        for kc in range(KC):
                            fns.append(lambda kc=kc, p=p, w=w, sb0=sb0, sn=sn, t0=t0, tn=tn: nc.tensor.matmul(
                                p[:sn, :tn], lhsT=w[:, kc, sb0:sb0 + sn], rhs=xres[:, kc, t0:t0 + tn],
                                start=(kc == 0), stop=(kc == KC - 1)))
                        S.mm_group(fns, reads=[xrb[g], wb], writes=[pb])
                        o, ob_ = osb.next()
                        bcol = None
                        if biast is not None:
                            assert (col + sb0) % 128 == 0
                            bcol = biast[:sn, (col + sb0) // 128:(col + sb0) // 128 + 1]
                        _epilogue(C, p[:sn, :tn], pb, o[:sn, :tn], ob_, act, None, None, bcol, biasb)
                        r0 = out_col0 + col + sb0
                        if mul_ap is not None:
                            m, mb = mulr.next()
                            S.dma("sp", m[:sn, :tn], mul_ap[r0:r0 + sn, t0:t0 + tn], mb, None)
                            S.op("dve", lambda o=o, m=m, tn=tn, sn=sn: nc.vector.tensor_tensor(
                                out=o[:sn, :tn], in0=o[:sn, :tn], in1=m[:sn, :tn], op=ALU.mult), reads=[mb], writes=[ob_])
                        S.dma("sp", out_ap[r0:r0 + sn, t0:t0 + tn], o[:sn, :tn], ob, ob_)
            col += nb


def _epilogue(C, ps_ap, psb, o_ap, ob, act, bias_bc, bias_bc_b, bias_col, bias_col_b):
    nc, S = C.nc, C.S
    if bias_bc is not None:
        S.op("dve", lambda: nc.vector.tensor_tensor(out=o_ap, in0=ps_ap, in1=bias_bc, op=ALU.add),
             reads=[psb, bias_bc_b], writes=[ob])
        if act is not None:
            S.op("act", lambda: nc.scalar.activation(out=o_ap, in_=o_ap, func=act), reads=[ob], writes=[ob])
    elif bias_col is not None:
        S.op("act", lambda: nc.scalar.activation(out=o_ap, in_=ps_ap, func=(act if act is not None else AF.Identity),
                                                 bias=bias_col), reads=[psb, bias_col_b], writes=[ob])
    else:
        if act is not None:
            S.op("act", lambda: nc.scalar.activation(out=o_ap, in_=ps_ap, func=act), reads=[psb], writes=[ob])
        else:
            S.op("act", lambda: nc.scalar.copy(out=o_ap, in_=ps_ap), reads=[psb], writes=[ob])


def wcols(w_ap, n0, n1, blk):
    out = []
    c = n0
    while c < n1:
        nb = min(blk, n1 - c)
        out.append(([(0, w_ap[:, c:c + nb])], nb))
        c += nb
    return out


class Rot:
    def __init__(self, C, name, shape, n, psum=False, dtype=F32):
        self.t = [(C.ps if psum else C.sb)(name + str(i), shape, dtype) for i in range(n)]
        self.b = [C.buf(name) for i in range(n)]
        self.i = 0
        self.n = n

    def next(self):
        k = self.i % self.n
        self.i += 1
        return self.t[k], self.b[k]


class TStore:
    def __init__(self, C):
        self.C = C
        self.pp = Rot(C, "tsp", [128, 4, 128], 2, psum=True)
        self.st = Rot(C, "tss", [128, 4, 128], 3)

    def run(self, src, srcb, ncols, dstT, dstb, t0, tsz, col0=0):
        C = self.C
        nc, S = C.nc, C.S
        nchunks = cdiv(ncols, 128)
        for c4 in range(0, nchunks, 4):
            pt, pb = self.pp.next()
            n4 = min(4, nchunks - c4)
            widths = []
            for j in range(n4):
                c = c4 + j
                w = min(128, ncols - c * 128)
                widths.append(w)
                S.op("pe", lambda j=j, c=c, w=w, pt=pt: nc.tensor.transpose(
                    out=pt[:w, j, :tsz], in_=src[:tsz, c * 128:c * 128 + w], identity=C.ident[:tsz, :tsz]),
                    reads=[srcb, C.ident_b], writes=[pb])
            st, sb_ = self.st.next()
            if all(w == 128 for w in widths):
                S.op("act", lambda pt=pt, st=st, n4=n4: nc.scalar.copy(out=st[:, :n4, :tsz], in_=pt[:, :n4, :tsz]),
                     reads=[pb], writes=[sb_])
                S.dma("pool", dstT[col0 + c4 * 128:col0 + (c4 + n4) * 128, t0:t0 + tsz].rearrange("(c p) t -> p c t", p=128),
                      st[:, :n4, :tsz], dstb, sb_)
            else:
                for j, w in enumerate(widths):
                    S.op("act", lambda pt=pt, st=st, j=j, w=w: nc.scalar.copy(out=st[:w, j, :tsz], in_=pt[:w, j, :tsz]),
                         reads=[pb], writes=[sb_])
                    S.dma("pool", dstT[col0 + (c4 + j) * 128:col0 + (c4 + j) * 128 + w, t0:t0 + tsz],
                          st[:w, j, :tsz], dstb, sb_)


def bc_load(C, name, row_ap, n):
    t = C.sb(name, [128, n])
    b = C.buf(name)
    C.S.dma("sp", t[:], row_ap.partition_broadcast(128), b, None)
    return t, b


def rms_rows(C, x, xb, tsz, ncols, scr, scrb, stat, statb, eps=1e-6):
    nc, S = C.nc, C.S
    S.op("act", lambda: nc.scalar.activation(out=scr[:tsz, :ncols], in_=x, func=AF.Square,
                                             accum_out=stat[:tsz, 0:1]), reads=[xb], writes=[scrb, statb])
    S.op("dve", lambda: nc.vector.tensor_scalar(out=stat[:tsz, 1:2], in0=stat[:tsz, 0:1], scalar1=1.0 / ncols,
                                                scalar2=eps, op0=ALU.mult, op1=ALU.add), reads=[statb], writes=[statb])
    S.op("act", lambda: nc.scalar.activation(out=stat[:tsz, 2:3], in_=stat[:tsz, 1:2], func=AF.Sqrt),
         reads=[statb], writes=[statb])
    S.op("dve", lambda: nc.vector.reciprocal(out=stat[:tsz, 3:4], in_=stat[:tsz, 2:3]), reads=[statb], writes=[statb])
    return stat[:tsz, 3:4]


def norm_mod_T(C, X, Xb, D, ranges, hT, hTb, tm_out=None, tm_b=None):
    nc, S = C.nc, C.S
    with C.phase("norm"):
        ts = TStore(C)
        xr = Rot(C, "nx", [128, D], 2)
        hr = Rot(C, "nh", [128, D], 2)
        sr = Rot(C, "nst", [128, 4], 2)
        scr = C.sb("nscr", [128, D])
        scrb = C.buf("nscr")
        for (t0, t1, G_ap, S_ap) in ranges:
            Gt, Gb = bc_load(C, "nG", G_ap, D)
            if S_ap is not None:
                St, Sb = bc_load(C, "nS", S_ap, D)
            for tt in range(t0, t1, 128):
                tsz = min(128, t1 - tt)
                x, xb = xr.next()
                S.dma("sp", x[:tsz, :], X[tt:tt + tsz, :], xb, Xb)
                st, stb = sr.next()
                rstd = rms_rows(C, x[:tsz, :], xb, tsz, D, scr, scrb, st, stb)
                h, hb = hr.next()
                S.op("dve", lambda x=x, h=h, rstd=rstd, tsz=tsz: nc.vector.scalar_tensor_tensor(
                    out=h[:tsz, :], in0=x[:tsz, :], scalar=rstd, in1=Gt[:tsz, :], op0=ALU.mult, op1=ALU.mult),
                    reads=[xb, stb, Gb], writes=[hb])
                if S_ap is not None:
                    S.op("dve", lambda h=h, tsz=tsz: nc.vector.tensor_tensor(
                        out=h[:tsz, :], in0=h[:tsz, :], in1=St[:tsz, :], op=ALU.add), reads=[Sb], writes=[hb])
                ts.run(h, hb, D, hT, hTb, tt, tsz)
                if tm_out is not None:
                    S.dma("pool", tm_out[tt:tt + tsz, :], h[:tsz, :], tm_b, hb)


def norm_out(C, X, Xb, D, T, g_ap, outs):
    nc, S = C.nc, C.S
    with C.phase("fnorm"):
        xr = Rot(C, "nx", [128, D], 2)
        hr = Rot(C, "nh", [128, D], 2)
        sr = Rot(C, "nst", [128, 4], 2)
        scr = C.sb("nscr", [128, D])
        scrb = C.buf("nscr")
        Gt, Gb = bc_load(C, "nG", g_ap, D)
        ob = C.buf("fo", sb=False)
        for (t0, t1, o_ap) in outs:
            for tt in range(t0, t1, 128):
                tsz = min(128, t1 - tt)
                x, xb = xr.next()
                S.dma("sp", x[:tsz, :], X[tt:tt + tsz, :], xb, Xb)
                st, stb = sr.next()
                rstd = rms_rows(C, x[:tsz, :], xb, tsz, D, scr, scrb, st, stb)
                h, hb = hr.next()
                S.op("dve", lambda x=x, h=h, rstd=rstd, tsz=tsz: nc.vector.scalar_tensor_tensor(
                    out=h[:tsz, :], in0=x[:tsz, :], scalar=rstd, in1=Gt[:tsz, :], op0=ALU.mult, op1=ALU.mult),
                    reads=[xb, stb, Gb], writes=[hb])
                S.dma("pool", o_ap[tt - t0:tt - t0 + tsz, :], h[:tsz, :], ob, hb)


def residual(C, X, Xb, Y, Yb, D, ranges):
    nc, S = C.nc, C.S
    with C.phase("resid"):
        xr = Rot(C, "rx", [128, D], 3)
        yr = Rot(C, "ry", [128, D], 2)
        for (t0, t1, g_ap) in ranges:
            Gt, Gb = bc_load(C, "rG", g_ap, D)
            for tt in range(t0, t1, 128):
                tsz = min(128, t1 - tt)
                x, xb = xr.next()
                y, yb = yr.next()
                S.dma("sp", x[:tsz, :], X[tt:tt + tsz, :], xb, Xb)
                S.dma("sp", y[:tsz, :], Y[tt:tt + tsz, :], yb, Yb)
                S.op("dve", lambda y=y, tsz=tsz: nc.vector.tensor_tensor(out=y[:tsz, :], in0=y[:tsz, :], in1=Gt[:tsz, :],
                                                                         op=ALU.mult), reads=[Gb], writes=[yb])
                S.op("pool", lambda x=x, y=y, tsz=tsz: nc.gpsimd.tensor_tensor(out=x[:tsz, :], in0=x[:tsz, :], in1=y[:tsz, :],
                                                                           op=ALU.add), reads=[yb], writes=[xb])
                S.dma("pool", X[tt:tt + tsz, :], x[:tsz, :], Xb, xb)


def make_masks(C):
    nc, S = C.nc, C.S
    C.m_ge = nc.alloc_sbuf_tensor("m_ge", [128, 128], F32)
    C.m_le = nc.alloc_sbuf_tensor("m_le", [128, 128], F32)
    C.m_ge_b = C.buf("m_ge")
    C.m_le_b = C.buf("m_le")
    S.op("pool", lambda: nc.gpsimd.memset(C.m_ge[:], 0.0), writes=[C.m_ge_b])
    S.op("pool", lambda: nc.gpsimd.affine_select(out=C.m_ge[:], in_=C.m_ge[:], pattern=[[1, 128]], compare_op=ALU.is_ge,
                                                  fill=-1e30, base=0, channel_multiplier=-1),
         reads=[C.m_ge_b], writes=[C.m_ge_b])
    S.op("pool", lambda: nc.gpsimd.memset(C.m_le[:], 0.0), writes=[C.m_le_b])
    S.op("pool", lambda: nc.gpsimd.affine_select(out=C.m_le[:], in_=C.m_le[:], pattern=[[-1, 128]], compare_op=ALU.is_ge,
                                                  fill=-1e30, base=0, channel_multiplier=1),
         reads=[C.m_le_b], writes=[C.m_le_b])
    S.barrier(C.bufs)


def attention(C, items, scale, dv, sink_ap=None, nsink=0, maxk=2304):
    nc, S = C.nc, C.S
    with C.phase("attn"):
        src = C.buf("asrc", sb=False)
        dst = C.buf("adst", sb=False)
        nkp = max(len(it["kparts"]) for it in items)
        kcap = max(it["Lk"] for it in items)
        ktr = [Rot(C, "ak%d" % i, [128, kcap], 2, dtype=BF16) for i in range(nkp)]
        vr = Rot(C, "av", [128, cdiv(kcap, 128), dv], 2, dtype=BF16)
        qtr = [Rot(C, "aq%d" % i, [128, 128], 3, dtype=BF16) for i in range(nkp)]
        nb = cdiv(maxk, 512)
        sps = Rot(C, "asp", [128, nb * 512], 2 if nb <= 2 else 1, psum=True)
        tps = Rot(C, "atp", [128, 4, 128], 2, psum=True, dtype=BF16)
        ops = Rot(C, "aop", [128, 128], 2 if nb <= 2 else 1, psum=True)
        pr = Rot(C, "ap", [128, maxk], 2, dtype=BF16)
        ptr = Rot(C, "apt", [128, cdiv(maxk, 128), 128], 2, dtype=BF16)
        otr = Rot(C, "aot", [128, 128], 2)
        str_ = Rot(C, "ast", [128, 8], 3)
        if sink_ap is not None:
            sk, skb = bc_load(C, "ask", sink_ap, nsink)
        for it in items:
            Lk, Lq = it["Lk"], it["Lq"]
            nkc = cdiv(Lk, 128)
            kts = []
            for i, (kap, kd) in enumerate(it["kparts"]):
                kt, kb = ktr[i].next()
                S.dma("pool", kt[:kd, :Lk], kap, kb, src)
                kts.append((kt, kb, kd))
            vt, vb = vr.next()
            S.dma("pool", vt[:, :nkc, :], it["v"].rearrange("(c p) d -> p c d", p=128), vb, src)
            for qh in it["qheads"]:
                for qt in range(cdiv(Lq, 128)):
                    q0 = qt * 128
                    qts = []
                    for i, (qap, kd) in enumerate(qh["qparts"]):
                        qtile, qb = qtr[i].next()
                        S.dma("pool", qtile[:kd, :], qap[:, q0:q0 + 128], qb, src)
                        qts.append((qtile, qb, kd))
                    segs = it["segs"](qt)
                    ntot = sum(s[1] for s in segs)
                    sp_, spb = sps.next()
                    col = 0
                    for (k0, klen, mask) in segs:
                        c = 0
                        while c < klen:
                            n = min(klen - c, 512 - (col % 512))
                            fns = []
                            for i in range(len(kts)):
                                fns.append(lambda i=i, col=col, n=n, k0=k0, c=c: nc.tensor.matmul(
                                    sp_[:, col:col + n], lhsT=qts[i][0][:qts[i][2], :], rhs=kts[i][0][:kts[i][2], k0 + c:k0 + c + n],
                                    start=(i == 0), stop=(i == len(kts) - 1)))
                            S.mm_group(fns, reads=[q[1] for q in qts] + [k[1] for k in kts], writes=[spb])
                            col += n
                            c += n
                    col = 0
                    for (k0, klen, mask) in segs:
                        if mask is not None:
                            mt, mb = (C.m_ge, C.m_ge_b) if mask == "ge" else (C.m_le, C.m_le_b)
                            S.op("dve", lambda col=col, mt=mt: nc.vector.tensor_tensor(
                                out=sp_[:, col:col + 128], in0=sp_[:, col:col + 128], in1=mt[:], op=ALU.add),
                                reads=[mb], writes=[spb])
                        col += klen
                    st, stb = str_.next()
                    S.op("dve", lambda: nc.vector.reduce_max(out=st[:, 0:1], in_=sp_[:, :ntot], axis=AX.X),
                         reads=[spb], writes=[stb])
                    if qh.get("sink") is not None:
                        si = qh["sink"]
                        S.op("dve", lambda si=si: nc.vector.tensor_scalar(out=st[:, 1:2], in0=st[:, 0:1], scalar1=scale,
                                                                    scalar2=sk[:, si:si + 1], op0=ALU.mult, op1=ALU.max),
                             reads=[stb, skb], writes=[stb])
                        S.op("dve", lambda: nc.vector.tensor_scalar(out=st[:, 2:3], in0=st[:, 1:2], scalar1=-1.0,
                                                                    scalar2=None, op0=ALU.mult),
                             reads=[stb], writes=[stb])
                    else:
                        S.op("dve", lambda: nc.vector.tensor_scalar(out=st[:, 2:3], in0=st[:, 0:1], scalar1=-scale,
                                                                    scalar2=None, op0=ALU.mult),
                             reads=[stb], writes=[stb])
                    p, pb = pr.next()
                    S.op("act", lambda: nc.scalar.activation(out=p[:, :ntot], in_=sp_[:, :ntot], func=AF.Exp,
                                                             bias=st[:, 2:3], scale=scale, accum_out=st[:, 3:4]),
                         reads=[spb, stb], writes=[pb, stb])
                    if qh.get("sink") is not None:
                        si = qh["sink"]
                        S.op("act", lambda si=si: nc.scalar.activation(out=st[:, 4:5], in_=sk[:, si:si + 1], func=AF.Exp,
                                                                 bias=st[:, 2:3], scale=1.0), reads=[skb, stb], writes=[stb])
                        S.op("dve", lambda: nc.vector.tensor_tensor(out=st[:, 3:4], in0=st[:, 3:4], in1=st[:, 4:5],
                                                                    op=ALU.add), reads=[stb], writes=[stb])
                    S.op("dve", lambda: nc.vector.reciprocal(out=st[:, 5:6], in_=st[:, 3:4]), reads=[stb], writes=[stb])
                    S.op("dve", lambda: nc.vector.tensor_scalar(out=p[:, :ntot], in0=p[:, :ntot], scalar1=st[:, 5:6],
                                                                scalar2=None, op0=ALU.mult), reads=[stb], writes=[pb])
                    pt, ptb = ptr.next()
                    nblk = ntot // 128
                    assert ntot % 128 == 0
                    for b4 in range(0, nblk, 4):
                        n4 = min(4, nblk - b4)
                        tp, tpb = tps.next()
                        for j in range(n4):
                            S.op("pe", lambda j=j, b4=b4, tp=tp: nc.tensor.transpose(
                                out=tp[:, j, :], in_=p[:, (b4 + j) * 128:(b4 + j + 1) * 128], identity=C.ident_bf[:]),
                                reads=[pb, C.ident_bf_b], writes=[tpb])
                        S.op("act" if (b4 // 4) % 2 == 0 else "dve",
                             (lambda tp=tp, b4=b4, n4=n4: nc.scalar.copy(out=pt[:, b4:b4 + n4, :], in_=tp[:, :n4, :]))
                             if (b4 // 4) % 2 == 0 else
                             (lambda tp=tp, b4=b4, n4=n4: nc.vector.tensor_copy(out=pt[:, b4:b4 + n4, :], in_=tp[:, :n4, :])),
                             reads=[tpb], writes=[ptb])
                    op_, opb = ops.next()
                    fns = []
                    blk = 0
                    for (k0, klen, mask) in segs:
                        assert k0 % 128 == 0 and klen % 128 == 0
                        for c in range(klen // 128):
                            fns.append(lambda blk=blk, kc=k0 // 128 + c: nc.tensor.matmul(
                                op_[:dv, :], lhsT=vt[:, kc, :], rhs=pt[:, blk, :], start=(blk == 0), stop=(blk == nblk - 1)))
                            blk += 1
                    S.mm_group(fns, reads=[vb, ptb], writes=[opb])
                    ot, otb = otr.next()
                    S.op("act", lambda: nc.scalar.copy(out=ot[:dv, :], in_=op_[:dv, :]), reads=[opb], writes=[otb])
                    S.dma("sp", qh["oT"][:, q0:q0 + 128], ot[:dv, :], dst, otb)


def make_iota16(C):
    nc, S = C.nc, C.S
    C.iota_i = nc.alloc_sbuf_tensor("iota_i", [128, 8, 16, 16], I32)
    C.iota_a = nc.alloc_sbuf_tensor("iota_a", [128, 8, 16, 16], F32)
    C.iota_b = C.buf("iota")
    S.op("pool", lambda: nc.gpsimd.iota(C.iota_i[:], pattern=[[0, 8], [0, 16], [1, 16]], base=0, channel_multiplier=0),
         writes=[C.iota_b])
    S.op("dve", lambda: nc.vector.tensor_copy(out=C.iota_a[:], in_=C.iota_i[:]), reads=[C.iota_b], writes=[C.iota_b])
    S.barrier(C.bufs)


def peer_prep(C):
    nc, S = C.nc, C.S
    C.ident_bf = nc.alloc_sbuf_tensor("ident_bf", [128, 128], BF16)
    C.ident_bf_b = C.buf("ident_bf")
    S.op("dve", lambda: nc.vector.tensor_copy(out=C.ident_bf[:], in_=C.ident[:]), reads=[C.ident_b], writes=[C.ident_bf_b])
    S.barrier(C.bufs)


def cast_tables(C, u_ap, v_ap, nrows, D, chunk=2048):
    nc, S = C.nc, C.S
    ub = C.nc.dram_tensor(C.uid("u_bf"), [nrows, D], BF16).ap()
    vb = C.nc.dram_tensor(C.uid("v_bf"), [nrows, D], BF16).ap()
    C.tab_buf = Buf("tables", sb=False)
    NOWN = 8
    C.cast_owners = []
    for i in range(NOWN):
        o = Buf("castown%d" % i, sb=False)
        sem = nc.alloc_semaphore("BG%d" % i)
        S.dcnt[id(sem)] = 0
        S.semobj[id(sem)] = sem
        o.dsem = {"sw": sem}
        C.cast_owners.append(o)
    C.cast_queue = []
    for r0 in range(0, nrows, chunk):
        n = min(chunk, nrows - r0)
        C.cast_queue.append((r0 + n, ub[r0:r0 + n, :], u_ap[r0:r0 + n, :]))
        C.cast_queue.append((r0 + n, vb[r0:r0 + n, :], v_ap[r0:r0 + n, :]))
    C.cast_i = 0
    return ub, vb


def issue_casts(C, k=None, upto_row=None):
    if not getattr(C, "cast_queue", None):
        return
    n = 0
    while C.cast_queue:
        end, dst_ap, src_ap = C.cast_queue[0]
        if upto_row is not None:
            if end > upto_row:
                break
        elif n >= k:
            break
        C.cast_queue.pop(0)
        own = C.cast_owners[C.cast_i % len(C.cast_owners)]
        C.cast_i += 1
        C.S.dma("pool", dst_ap, src_ap, C.tab_buf, None, owner=own)
        n += 1


def peer(C, QT, H2, X, Xb, D, T, keys_ap, u_ap, v_ap, ranges, PH=8, NKEY=128, TOPK=16, row_off=0, tab_buf=None):
    nc, S = C.nc, C.S
    NG = PH * 2
    NS = PH * TOPK
    if tab_buf is not None:
        issue_casts(C, upto_row=row_off + NKEY * NKEY)
    with C.phase("peer"):
        src = C.buf("psrc", sb=False)
        tsrc = tab_buf if tab_buf is not None else src
        if tab_buf is not None:
            u_ap = u_ap[0:row_off + NKEY * NKEY, :]
            v_ap = v_ap[0:row_off + NKEY * NKEY, :]
        keysT = C.sb("keysT", [128, NG, NKEY])
        keysTb = C.buf("keysT")
        kl = Rot(C, "kl", [128, 128], 2)
        spr = Rot(C, "psp", [128, NG * NKEY], 1, psum=True)
        kps, kpsb = spr.next()
        for g in range(NG):
            kt, kb = kl.next()
            S.dma("sp", kt[:], keys_ap[g // 2, g % 2], kb, src)
            S.op("pe", lambda kt=kt, g=g: nc.tensor.transpose(out=kps[:, g * NKEY:(g + 1) * NKEY], in_=kt[:], identity=C.ident[:]),
                 reads=[kb, C.ident_b], writes=[kpsb])
        S.op("act", lambda: nc.scalar.copy(out=keysT[:].rearrange("p g n -> p (g n)"), in_=kps[:]), reads=[kpsb], writes=[keysTb])
        qr = Rot(C, "pq", [128, NG, 128], 2)
        hr = Rot(C, "ph", [128, D], 2)
        xr = Rot(C, "px", [128, D], 2)
        s1 = C.sb("ps1", [128, NG, NKEY]); s1b = C.buf("ps1")
        s2 = C.sb("ps2", [128, NG, NKEY]); s2b = C.buf("ps2")
        sv = C.sb("psv", [128, NG, TOPK]); svb = C.buf("psv")
        si = C.sb("psi", [128, NG, TOPK], U32); sib = C.buf("psi")
        sif = C.sb("psif", [128, NG, TOPK]); sifb = C.buf("psif")
        cs = C.sb("pcs", [128, PH, TOPK, TOPK]); csb = C.buf("pcs")
        cs2 = C.sb("pcs2", [128, PH, TOPK, TOPK]); cs2b = C.buf("pcs2")
        tsv = C.sb("ptsv", [128, PH, TOPK]); tsvb = C.buf("ptsv")
        tpos = C.sb("ptpos", [128, PH, TOPK], U32); tposb = C.buf("ptpos")
        ab_i = C.sb("pabi", [128, 2, PH, TOPK], U32); ab_ib = C.buf("pabi")
        ab_f = C.sb("pabf", [128, 2, PH, TOPK]); ab_fb = C.buf("pabf")
        oh = C.sb("poh", [128, PH, TOPK, TOPK]); ohb = C.buf("poh")
        sel = C.sb("psel", [128, 2, PH, TOPK]); selb = C.buf("psel")
        eidf = C.sb("peidf", [128, NS]); eidfb = C.buf("peidf")
        eid = [C.sb("peid%d" % i, [128, NS], I32) for i in range(2)]
        eidb = [C.buf("peid") for i in range(2)]
        gate = C.sb("pgate", [128, PH, TOPK]); gateb = C.buf("pgate")
        gs = C.sb("pgs", [128, PH, 2]); gsb = C.buf("pgs")
        a_all = C.sb("pa", [128, NS]); a_allb = C.buf("pa")
        w_all = C.sb("pw", [128, NS]); w_allb = C.buf("pw")
        gtmp = C.sb("pgt", [128, NS]); gtmpb = C.buf("pgt")
        junk = C.sb("pjunk", [128, D], BF16); junkb = C.buf("pjunk")
        accr = Rot(C, "pacc", [128, D], 1, psum=True)
        accs = C.sb("paccs", [128, D]); accsb = C.buf("paccs")
        dgr = Rot(C, "pdg", [128, 128], 4, dtype=BF16)
        gr = Rot(C, "pgr", [128, D], 6, dtype=BF16)
        hbfr = Rot(C, "phb", [128, D], 2, dtype=BF16)
        prodr = Rot(C, "ppr", [128, D], 3, dtype=BF16)
        junk2 = C.sb("pj2", [128, D], BF16); junk2b = C.buf("pj2")
        a_allb_o = C.buf("pa_odd")
        tiles = []
        for (t0, t1, g_ap) in ranges:
            Gt, Gb = bc_load(C, "pG", g_ap, D)
            for tt in range(t0, t1, 128):
                assert tt + 128 <= t1
                tiles.append((tt, Gt, Gb))
        st_ = {"ti": 0}

        def do_topk(tt):
            ti = st_["ti"]
            q, qb = qr.next()
            S.dma("sp", q[:], QT[:, tt:tt + 128].rearrange("(c p) t -> p c t", p=128), qb, src)
            h, hb = hr.next()
            S.dma("sp", h[:], H2[tt:tt + 128, :], hb, src)
            hbf, hbfb = hbfr.next()
            S.op("act", lambda: nc.scalar.copy(out=hbf[:], in_=h[:]), reads=[hb], writes=[hbfb])
            x, xb = xr.next()
            S.dma("sp", x[:], X[tt:tt + 128, :], xb, Xb)
            sp_, spb = spr.next()
            for g in range(NG):
                S.mm_group([lambda g=g: nc.tensor.matmul(sp_[:, g * NKEY:(g + 1) * NKEY], lhsT=q[:, g, :],
                                                         rhs=keysT[:, g, :], start=True, stop=True)],
                           reads=[qb, keysTb], writes=[spb])
            S.op("act", lambda: nc.scalar.copy(out=s1[:].rearrange("p g n -> p (g n)"), in_=sp_[:]),
                 reads=[spb], writes=[s1b])
            for g in range(NG):
                S.op("dve", lambda g=g: nc.vector.max(out=sv[:, g, 0:8], in_=s1[:, g, :]), reads=[s1b], writes=[svb])
                S.op("dve", lambda g=g: nc.vector.max_index(out=si[:, g, 0:8], in_max=sv[:, g, 0:8], in_values=s1[:, g, :]),
                     reads=[s1b, svb], writes=[sib])
                S.op("dve", lambda g=g: nc.vector.match_replace(out=s2[:, g, :], in_to_replace=sv[:, g, 0:8],
                                                                in_values=s1[:, g, :], imm_value=-1e30),
                     reads=[s1b, svb], writes=[s2b])
                S.op("dve", lambda g=g: nc.vector.max(out=sv[:, g, 8:16], in_=s2[:, g, :]), reads=[s2b], writes=[svb])
                S.op("dve", lambda g=g: nc.vector.max_index(out=si[:, g, 8:16], in_max=sv[:, g, 8:16], in_values=s2[:, g, :]),
                     reads=[s2b, svb], writes=[sib])
            S.op("dve", lambda: nc.vector.tensor_copy(out=sif[:], in_=si[:]), reads=[sib], writes=[sifb])
            sv4 = sv[:].rearrange("p (h two) k -> p h two k", two=2)
            sif4 = sif[:].rearrange("p (h two) k -> p h two k", two=2)
            S.op("dve", lambda: nc.vector.tensor_tensor(
                out=cs[:], in0=sv4[:, :, 0, :].unsqueeze(3).broadcast_to([128, PH, TOPK, TOPK]),
                in1=sv4[:, :, 1, :].unsqueeze(2).broadcast_to([128, PH, TOPK, TOPK]), op=ALU.add),
                reads=[svb], writes=[csb])
            for hh in range(PH):
                c1 = cs[:, hh].rearrange("p a b -> p (a b)")
                c2 = cs2[:, hh].rearrange("p a b -> p (a b)")
                S.op("dve", lambda hh=hh, c1=c1: nc.vector.max(out=tsv[:, hh, 0:8], in_=c1), reads=[csb], writes=[tsvb])
                S.op("dve", lambda hh=hh, c1=c1: nc.vector.max_index(out=tpos[:, hh, 0:8], in_max=tsv[:, hh, 0:8], in_values=c1),
                     reads=[csb, tsvb], writes=[tposb])
                S.op("dve", lambda hh=hh, c1=c1, c2=c2: nc.vector.match_replace(out=c2, in_to_replace=tsv[:, hh, 0:8],
                                                                         in_values=c1, imm_value=-1e30),
                     reads=[csb, tsvb], writes=[cs2b])
                S.op("dve", lambda hh=hh, c2=c2: nc.vector.max(out=tsv[:, hh, 8:16], in_=c2), reads=[cs2b], writes=[tsvb])
                S.op("dve", lambda hh=hh, c2=c2: nc.vector.max_index(out=tpos[:, hh, 8:16], in_max=tsv[:, hh, 8:16], in_values=c2),
                     reads=[cs2b, tsvb], writes=[tposb])
            S.op("dve", lambda: nc.vector.tensor_single_scalar(out=ab_i[:, 0], in_=tpos[:], scalar=4, op=ALU.logical_shift_right),
                 reads=[tposb], writes=[ab_ib])
            S.op("dve", lambda: nc.vector.tensor_single_scalar(out=ab_i[:, 1], in_=tpos[:], scalar=15, op=ALU.bitwise_and),
                 reads=[tposb], writes=[ab_ib])
            S.op("dve", lambda: nc.vector.tensor_copy(out=ab_f[:], in_=ab_i[:]), reads=[ab_ib], writes=[ab_fb])
            for two in range(2):
                S.op("dve", lambda two=two: nc.vector.tensor_tensor(
                    out=oh[:], in0=C.iota_a[:, :PH], in1=ab_f[:, two].unsqueeze(3).broadcast_to([128, PH, TOPK, TOPK]),
                    op=ALU.is_equal), reads=[C.iota_b, ab_fb], writes=[ohb])
                S.op("dve", lambda two=two: nc.vector.tensor_tensor(
                    out=oh[:], in0=oh[:], in1=sif4[:, :, two, :].unsqueeze(2).broadcast_to([128, PH, TOPK, TOPK]),
                    op=ALU.mult), reads=[sifb], writes=[ohb])
                S.op("dve", lambda two=two: nc.vector.tensor_reduce(out=sel[:, two], in_=oh[:], axis=AX.X, op=ALU.add),
                     reads=[ohb], writes=[selb])
            S.op("dve", lambda: nc.vector.scalar_tensor_tensor(
                out=eidf[:], in0=sel[:, 0].rearrange("p h k -> p (h k)"), scalar=float(NKEY),
                in1=sel[:, 1].rearrange("p h k -> p (h k)"), op0=ALU.mult, op1=ALU.add), reads=[selb], writes=[eidfb])
            if row_off:
                S.op("dve", lambda: nc.vector.tensor_scalar(out=eidf[:], in0=eidf[:], scalar1=float(row_off), scalar2=None,
                                                            op0=ALU.add), reads=[eidfb], writes=[eidfb])
            ei = eid[ti % 2]
            eib = eidb[ti % 2]
            S.op("dve", lambda ei=ei: nc.vector.tensor_copy(out=ei[:], in_=eidf[:]), reads=[eidfb], writes=[eib])
            S.op("dve", lambda: nc.vector.tensor_tensor(out=gate[:], in0=tsv[:], in1=tsv[:, :, 0:1].broadcast_to([128, PH, TOPK]),
                                                        op=ALU.subtract), reads=[tsvb], writes=[gateb])
            S.op("act", lambda: nc.scalar.activation(out=gate[:], in_=gate[:], func=AF.Exp), reads=[gateb], writes=[gateb])
            S.op("dve", lambda: nc.vector.tensor_reduce(out=gs[:, :, 0], in_=gate[:], axis=AX.X, op=ALU.add),
                 reads=[gateb], writes=[gsb])
            S.op("dve", lambda: nc.vector.reciprocal(out=gs[:, :, 1], in_=gs[:, :, 0]), reads=[gsb], writes=[gsb])
            S.op("dve", lambda: nc.vector.tensor_tensor(out=gate[:], in0=gate[:], in1=gs[:, :, 1:2].broadcast_to([128, PH, TOPK]),
                                                        op=ALU.mult), reads=[gsb], writes=[gateb])

            st_["ti"] += 1
            return dict(h=h, hb=hb, x=x, xb=xb, ei=ei, eib=eib, hbf=hbf, hbfb=hbfb)

        def do_upass(tt, T_):
            h, hb, ei, eib = T_["h"], T_["hb"], T_["ei"], T_["eib"]
            hbf, hbfb = T_["hbf"], T_["hbfb"]
            for e in range(NS):
                ub, ubb = gr.next()
                S.dma("pool", ub[:], u_ap, ubb, tsrc,
                      indirect=dict(in_offset=bass.IndirectOffsetOnAxis(ap=ei[:, e:e + 1], axis=0), idx_buf=eib))
                if e % 2 == 0:
                    S.op("dve", lambda ub=ub, e=e, h=h: nc.vector.scalar_tensor_tensor(
                        out=junk[:], in0=ub[:], scalar=1.0, in1=h[:], op0=ALU.mult, op1=ALU.mult,
                        accum_out=a_all[:, e:e + 1]), reads=[ubb, hb], writes=[junkb, a_allb])
                else:
                    pr_, prb = prodr.next()
                    S.op("dve", lambda ub=ub, pr_=pr_: nc.vector.tensor_tensor(out=pr_[:], in0=ub[:], in1=hbf[:], op=ALU.mult),
                         reads=[ubb, hbfb], writes=[prb])
                    S.op("act", lambda pr_=pr_, e=e: nc.scalar.activation(out=junk2[:], in_=pr_[:], func=AF.Copy,
                                                                        accum_out=a_all[:, e:e + 1]),
                         reads=[prb], writes=[junk2b, a_allb_o])
            S.op("dve", lambda: nc.vector.tensor_copy(out=a_all[:, 0:1], in_=a_all[:, 0:1]), reads=[a_allb_o], writes=[a_allb])
            gelu_tanh(C, None, w_all[:], a_all[:], gtmp[:], a_allb, w_allb, gtmpb)
            S.op("dve", lambda: nc.vector.tensor_tensor(out=w_all[:], in0=w_all[:], in1=gate[:].rearrange("p h k -> p (h k)"),
                                                        op=ALU.mult), reads=[gateb], writes=[w_allb])

        def do_vpass(tt, T_, Gt, Gb):
            x, xb, ei, eib = T_["x"], T_["xb"], T_["ei"], T_["eib"]
            acc, accb = accr.next()
            for e in range(NS):
                vb_, vbb = gr.next()
                S.dma("pool", vb_[:], v_ap, vbb, tsrc,
                      indirect=dict(in_offset=bass.IndirectOffsetOnAxis(ap=ei[:, e:e + 1], axis=0), idx_buf=eib))
                dg, dgb = dgr.next()
                S.op("act", lambda dg=dg, e=e: nc.scalar.activation(out=dg[:], in_=C.ident_bf[:], func=AF.Copy,
                                                                   scale=w_all[:, e:e + 1]),
                     reads=[C.ident_bf_b, w_allb], writes=[dgb])
                fns = []
                for c0 in range(0, D, 512):
                    n = min(512, D - c0)
                    fns.append(lambda dg=dg, vb_=vb_, c0=c0, n=n, e=e: nc.tensor.matmul(
                        acc[:, c0:c0 + n], lhsT=dg[:], rhs=vb_[:, c0:c0 + n], start=(e == 0), stop=(e == NS - 1)))
                S.mm_group(fns, reads=[dgb, vbb], writes=[accb])
            S.op("dve", lambda acc=acc: nc.vector.tensor_tensor(out=accs[:], in0=acc[:], in1=Gt[:], op=ALU.mult),
                 reads=[accb, Gb], writes=[accsb])
            S.op("dve", lambda x=x: nc.vector.tensor_tensor(out=x[:], in0=x[:], in1=accs[:], op=ALU.add),
                 reads=[accsb], writes=[xb])
            S.dma("pool", X[tt:tt + 128, :], x[:], Xb, xb)

        cur = do_topk(tiles[0][0])
        for k, (tt, Gt, Gb) in enumerate(tiles):
            do_upass(tt, cur)
            nxt = do_topk(tiles[k + 1][0]) if k + 1 < len(tiles) else None
            do_vpass(tt, cur, Gt, Gb)
            cur = nxt


def gelu_tanh(C, e_small, out_ap, in_ap, tmp_ap, inb, outb, tmpb):
    nc, S = C.nc, C.S
    S.op("dve", lambda: nc.vector.tensor_tensor(out=tmp_ap, in0=in_ap, in1=in_ap, op=ALU.mult), reads=[inb], writes=[tmpb])
    S.op("dve", lambda: nc.vector.tensor_scalar(out=tmp_ap, in0=tmp_ap, scalar1=0.044715, scalar2=1.0, op0=ALU.mult,
                                                op1=ALU.add), reads=[tmpb], writes=[tmpb])
    S.op("dve", lambda: nc.vector.tensor_tensor(out=tmp_ap, in0=tmp_ap, in1=in_ap, op=ALU.mult), reads=[inb], writes=[tmpb])
    S.op("act", lambda: nc.scalar.activation(out=tmp_ap, in_=tmp_ap, func=AF.Sigmoid, scale=1.5957691216057308),
         reads=[tmpb], writes=[tmpb])
    S.op("dve", lambda: nc.vector.tensor_tensor(out=out_ap, in0=tmp_ap, in1=in_ap, op=ALU.mult), reads=[tmpb, inb],
         writes=[outb])


PI = 3.14159265358979


def range_reduce(C, x, xb, ti, tf, tb):
    nc, S = C.nc, C.S
    S.op("dve", lambda: nc.vector.tensor_scalar(out=tf, in0=x, scalar1=1.0 / (2 * PI), scalar2=None, op0=ALU.mult),
         reads=[xb], writes=[tb])
    S.op("dve", lambda: nc.vector.tensor_copy(out=ti, in_=tf), reads=[tb], writes=[tb])
    S.op("dve", lambda: nc.vector.tensor_copy(out=tf, in_=ti), reads=[tb], writes=[tb])
    S.op("dve", lambda: nc.vector.scalar_tensor_tensor(out=x, in0=tf, scalar=-2 * PI, in1=x, op0=ALU.mult, op1=ALU.add),
         reads=[tb], writes=[xb])
    S.op("dve", lambda: nc.vector.tensor_scalar(out=tf, in0=x, scalar1=PI, scalar2=-2 * PI, op0=ALU.is_gt, op1=ALU.mult),
         reads=[xb], writes=[tb])
    S.op("dve", lambda: nc.vector.tensor_tensor(out=x, in0=x, in1=tf, op=ALU.add), reads=[tb], writes=[xb])
    S.op("dve", lambda: nc.vector.tensor_scalar(out=tf, in0=x, scalar1=-PI, scalar2=2 * PI, op0=ALU.is_lt, op1=ALU.mult),
         reads=[xb], writes=[tb])
    S.op("dve", lambda: nc.vector.tensor_tensor(out=x, in0=x, in1=tf, op=ALU.add), reads=[tb], writes=[xb])


class RopeTab:
    def __init__(self, C, nf, axis_dim, grid_w=64, base=10000.0):
        nc, S = C.nc, C.S
        self.C, self.nf = C, nf
        self.gw = grid_w
        self.pi_i = C.sb("rp_pi", [128, nf], I32)
        self.inv = C.sb("rp_inv", [128, nf])
        self.pc = C.sb("rp_pc", [128, 4])
        self.ang = C.sb("rp_ang", [128, 2, nf])
        self.tmp = C.sb("rp_tmp", [128, 2, nf])
        self.tab = C.sb("rp_tab", [128, 4, nf])
        self.b = C.buf("rope")
        self.tabb = C.buf("ropetab")
        b = self.b
        S.op("pool", lambda: nc.gpsimd.iota(self.pi_i[:], pattern=[[1, nf]], base=0, channel_multiplier=0), writes=[b])
        S.op("dve", lambda: nc.vector.tensor_copy(out=self.inv[:], in_=self.pi_i[:]), reads=[b], writes=[b])
        import math
        S.op("act", lambda: nc.scalar.activation(out=self.inv[:], in_=self.inv[:], func=AF.Exp,
                                                 scale=-2.0 * math.log(base) / axis_dim), reads=[b], writes=[b])
        self.pidx = C.sb("rp_pidx", [128, 1], I32)
        S.op("pool", lambda: nc.gpsimd.iota(self.pidx[:], pattern=[[0, 1]], base=0, channel_multiplier=1), writes=[b])
        self.pint = C.sb("rp_pint", [128, 2], I32)
        S.op("dve", lambda: nc.vector.tensor_single_scalar(out=self.pint[:, 0:1], in_=self.pidx[:], scalar=grid_w - 1,
                                                           op=ALU.bitwise_and), reads=[b], writes=[b])
        sh = int(round(math.log2(grid_w)))
        assert (1 << sh) == grid_w
        S.op("dve", lambda: nc.vector.tensor_single_scalar(out=self.pint[:, 1:2], in_=self.pidx[:], scalar=sh,
                                                           op=ALU.arith_shift_right), reads=[b], writes=[b])
        S.op("dve", lambda: nc.vector.tensor_copy(out=self.pc[:, 1:3], in_=self.pint[:]), reads=[b], writes=[b])
        self.ti = C.sb("rp_ti", [128, 2, nf], I32)
        self.tf = C.sb("rp_tf", [128, 2, nf])

    def build(self, tile_idx):
        C = self.C
        nc, S = C.nc, C.S
        b, tb = self.b, self.tabb
        row0 = float(tile_idx * 128 // self.gw)
        S.op("dve", lambda: nc.vector.tensor_scalar(out=self.pc[:, 3:4], in0=self.pc[:, 2:3], scalar1=row0, scalar2=None,
                                                    op0=ALU.add), reads=[b], writes=[b])
        S.op("dve", lambda: nc.vector.tensor_scalar(out=self.ang[:, 0, :], in0=self.inv[:], scalar1=self.pc[:, 3:4],
                                                    scalar2=None, op0=ALU.mult), reads=[b], writes=[b])
        S.op("dve", lambda: nc.vector.tensor_scalar(out=self.ang[:, 1, :], in0=self.inv[:], scalar1=self.pc[:, 1:2],
                                                    scalar2=None, op0=ALU.mult), reads=[b], writes=[b])
        tabv = self.tab[:].rearrange("p (a cs) f -> p a cs f", cs=2)
        for cs, shift in ((0, 0.5 * PI), (1, 0.0)):
            S.op("dve", lambda shift=shift: nc.vector.tensor_scalar(out=self.tmp[:], in0=self.ang[:], scalar1=shift,
                                                                    scalar2=None, op0=ALU.add), reads=[b], writes=[b])
            range_reduce(C, self.tmp[:], b, self.ti[:], self.tf[:], b)
            S.op("act", lambda cs=cs: nc.scalar.activation(out=tabv[:, :, cs, :], in_=self.tmp[:], func=AF.Sin),
                 reads=[b], writes=[tb])

    def apply(self, x3, xb, H, tmpt, tmpb):
        C = self.C
        nc, S = C.nc, C.S
        nf = self.nf
        for a in range(2):
            x1 = x3[:, :, a * 2 * nf:a * 2 * nf + nf]
            x2 = x3[:, :, a * 2 * nf + nf:(a + 1) * 2 * nf]
            c = self.tab[:, 2 * a, :].unsqueeze(1).broadcast_to([128, H, nf])
            s = self.tab[:, 2 * a + 1, :].unsqueeze(1).broadcast_to([128, H, nf])
            for k, (xx, tt) in enumerate(((x1, c), (x2, s), (x1, s), (x2, c))):
                S.op("dve", lambda k=k, xx=xx, tt=tt: nc.vector.tensor_tensor(out=tmpt[:, k], in0=xx, in1=tt, op=ALU.mult),
                     reads=[xb, self.tabb], writes=[tmpb])
            S.op("dve", lambda x1=x1: nc.vector.tensor_tensor(out=x1, in0=tmpt[:, 0], in1=tmpt[:, 1], op=ALU.subtract),
                 reads=[tmpb], writes=[xb])
            S.op("dve", lambda x2=x2: nc.vector.tensor_tensor(out=x2, in0=tmpt[:, 2], in1=tmpt[:, 3], op=ALU.add),
                 reads=[tmpb], writes=[xb])


class Cfg:
    def __init__(self, **kw):
        self.D = 2048
        self.B_P, self.L_P, self.L_S, self.PAST, self.GRID_W = 2, 256, 2048, 256, 64
        self.DEPTH = 4
        self.MH, self.QR, self.KVR, self.NOPE, self.ROPE, self.MV = 16, 512, 512, 128, 64, 128
        self.S5P = 64
        self.SH, self.SKV, self.SHD = 16, 4, 128
        self.PH, self.NKEY, self.TOPK = 8, 128, 16
        self.HY_HID, self.HY_BANDS = 64, 8
        for k, v in kw.items():
            setattr(self, k, v)
        self.TP = self.B_P * self.L_P
        self.T = self.TP + self.L_S
        self.TK = self.T + self.PAST
        self.G = self.D // 16


def declare_io(nc, cf):
    D = cf.D
    n_mla, n_s5, n_hy, n_swa = [len(range(k, cf.DEPTH, 4)) for k in range(4)]
    I = {}

    def inp(name, shape):
        I[name] = nc.dram_tensor(name, list(shape), F32, kind="ExternalInput").ap()

    def outp(name, shape):
        I[name] = nc.dram_tensor(name, list(shape), F32, kind="ExternalOutput").ap()

    inp("x_p", [cf.TP, D]); inp("x_s", [cf.L_S, D]); inp("c", [D]); inp("c_ctx", [D])
    inp("cache_mla_ckv", [max(n_mla, 1), cf.PAST, cf.KVR]); inp("cache_mla_kpe", [max(n_mla, 1), cf.PAST, cf.ROPE])
    inp("state_s5_re", [max(n_s5, 1), 2, cf.G, cf.S5P]); inp("state_s5_im", [max(n_s5, 1), 2, cf.G, cf.S5P])
    inp("cache_swa_k", [max(n_swa, 1), cf.PAST, cf.SKV * cf.SHD]); inp("cache_swa_v", [max(n_swa, 1), cf.PAST, cf.SKV * cf.SHD])
    inp("mod_w", [cf.DEPTH, D, 6 * D]); inp("mod_b", [cf.DEPTH, 6 * D])
    inp("norm1_g", [cf.DEPTH, D]); inp("norm2_g", [cf.DEPTH, D]); inp("final_g", [D])
    if n_mla:
        inp("mla_w_down", [n_mla, D, cf.QR + cf.KVR + cf.ROPE]); inp("mla_g_q", [n_mla, cf.QR]); inp("mla_g_kv", [n_mla, cf.KVR])
        inp("mla_w_uq", [n_mla, cf.QR, cf.MH * (cf.NOPE + cf.ROPE)]); inp("mla_w_ukv", [n_mla, cf.KVR, cf.MH * (cf.NOPE + cf.MV)])
        inp("mla_w_o", [n_mla, cf.MH * cf.MV, D])
    if n_s5:
        G, P = cf.G, cf.S5P
        inp("s5_w_in", [n_s5, D, D]); inp("s5_lam_re", [n_s5, 2, G, P]); inp("s5_lam_im", [n_s5, 2, G, P])
        inp("s5_log_dt", [n_s5, 2, G]); inp("s5_b_re", [n_s5, 2, G, P, 16]); inp("s5_b_im", [n_s5, 2, G, P, 16])
        inp("s5_c_re", [n_s5, 2, G, 16, P]); inp("s5_c_im", [n_s5, 2, G, 16, P]); inp("s5_d", [n_s5, D])
        inp("s5_w_gate", [n_s5, D, D]); inp("s5_w_out", [n_s5, D, D])
    if n_hy:
        HE = 2 * cf.HY_BANDS + 1
        inp("hy_w_in", [n_hy, D, 3 * D]); inp("hy_b_in", [n_hy, 3 * D]); inp("hy_short_w", [n_hy, 3, 3 * D])
        inp("hy_short_b", [n_hy, 3 * D]); inp("hy_f_w1", [n_hy, HE, cf.HY_HID]); inp("hy_f_b1", [n_hy, cf.HY_HID])
        inp("hy_f_freq", [n_hy, 2, cf.HY_HID]); inp("hy_f_w2", [n_hy, cf.HY_HID, cf.HY_HID]); inp("hy_f_b2", [n_hy, cf.HY_HID])
        inp("hy_f_w3", [n_hy, cf.HY_HID, 4 * D]); inp("hy_log_decay", [n_hy, 4 * D]); inp("hy_bias", [n_hy, 2, D])
        inp("hy_w_out", [n_hy, D, D])
    if n_swa:
        inp("swa_w_qkv", [n_swa, D, (cf.SH + 2 * cf.SKV) * cf.SHD]); inp("swa_w_o", [n_swa, cf.SH * cf.SHD, D])
        inp("swa_sink", [n_swa, cf.SH])
    inp("peer_w_q", [cf.DEPTH, D, cf.PH * 2 * 128]); inp("peer_keys", [cf.DEPTH, cf.PH, 2, cf.NKEY, 128])
    inp("peer_u", [cf.DEPTH, cf.NKEY * cf.NKEY, D]); inp("peer_v", [cf.DEPTH, cf.NKEY * cf.NKEY, D])
    outp("y_p", [cf.TP, D]); outp("y_s", [cf.L_S, D])
    outp("o_ckv", [max(n_mla, 1), cf.TP, cf.KVR]); outp("o_kpe", [max(n_mla, 1), cf.TP, cf.ROPE])
    outp("o_s5re", [cf.B_P, max(n_s5, 1), 2, cf.G * cf.S5P]); outp("o_s5im", [cf.B_P, max(n_s5, 1), 2, cf.G * cf.S5P])
    outp("o_swak", [max(n_swa, 1), cf.TP, cf.SKV * cf.SHD]); outp("o_swav", [max(n_swa, 1), cf.TP, cf.SKV * cf.SHD])
    return I


def compute_mod(C, cf, I, MODROWS):
    nc, S = C.nc, C.S
    D = cf.D
    KC = D // 128
    condT = C.dram("condT", [D, 2])
    MODRAW = C.dram("modraw", [cf.DEPTH, 2, 6 * D])
    with C.phase("cond"):
        cc = C.sb("cc", [128, KC, 2]); ccb = C.buf("cc")
        db = C.buf("condT", sb=False)
        S.dma("sp", cc[:, :, 0], I["c_ctx"].rearrange("(c p) -> p c", p=128), ccb, None, allow_slow_non_contiguous=True)
        S.dma("sp", cc[:, :, 1], I["c"].rearrange("(c p) -> p c", p=128), ccb, None, allow_slow_non_contiguous=True)
        S.op("act", lambda: nc.scalar.activation(out=cc[:], in_=cc[:], func=AF.Silu), reads=[ccb], writes=[ccb])
        S.dma("pool", condT.rearrange("(c p) two -> p c two", p=128), cc[:], db, ccb)
    for i in range(cf.DEPTH):
        linear(C, condT, 2, D, wcols(I["mod_w"][i], 0, 6 * D, 512), MODRAW[i], mode="tm", bias_ap=I["mod_b"][i])
    with C.phase("modrows"):
        mr = Rot(C, "mr", [2, 6 * D], 2)
        gr = Rot(C, "mg", [2, 2, D], 2)
        db = C.buf("modrows", sb=False)
        for i in range(cf.DEPTH):
            m, mb = mr.next()
            g, gb = gr.next()
            S.dma("sp", m[:], MODRAW[i], mb, None)
            S.dma("sp", g[:, 0, :], I["norm1_g"][i].partition_broadcast(2), gb, None)
            S.dma("sp", g[:, 1, :], I["norm2_g"][i].partition_broadcast(2), gb, None)
            for k, col in ((0, D), (1, 4 * D)):
                S.op("dve", lambda m=m, g=g, k=k, col=col: nc.vector.scalar_tensor_tensor(
                    out=m[:, col:col + D], in0=m[:, col:col + D], scalar=1.0, in1=g[:, k, :], op0=ALU.add, op1=ALU.mult),
                    reads=[gb], writes=[mb])
            S.dma("pool", MODROWS[i].rearrange("c s d -> c (s d)"), m[:], db, mb)


def mla_layer(C, cf, I, j, hT, hTb, X, Xb, MR):
    nc, S = C.nc, C.S
    D, T, TP, TK = cf.D, cf.T, cf.TP, cf.TK
    QR, KVR, RP, H = cf.QR, cf.KVR, cf.ROPE, cf.MH
    ND = QR + KVR + RP
    DOWN = C.dram("down", [T, ND])
    linear_bf(C, hT, T, D, wcols(I["mla_w_down"][j], 0, ND, 512), DOWN, mode="tm")
    cqT = C.dram("cqT", [QR, T])
    ckvT = C.dram("ckvT", [KVR, TK])
    kpeT = C.dram("kpeT", [RP, TK])
    nf = RP // 4
    with C.phase("mla_rows"):
        src = C.buf("src", sb=False); dst = C.buf("dst", sb=False)
        ts = TStore(C)
        dr = Rot(C, "md", [128, ND], 2)
        scr = C.sb("mscr", [128, max(QR, KVR)]); scrb = C.buf("mscr")
        sr = Rot(C, "mst", [128, 4], 2)
        gq, gqb = bc_load(C, "gq", I["mla_g_q"][j], QR)
        gkv, gkvb = bc_load(C, "gkv", I["mla_g_kv"][j], KVR)
        rt = RopeTab(C, nf, RP // 2, cf.GRID_W)
        rtmp = C.sb("rtmp", [128, 4, 1, nf]); rtmpb = C.buf("rtmp")
        for tt in range(0, T, 128):
            d, db = dr.next()
            S.dma("sp", d[:], DOWN[tt:tt + 128, :], db, src)
            for (c0, n, gt, gb_) in ((0, QR, gq, gqb), (QR, KVR, gkv, gkvb)):
                st, stb = sr.next()
                rstd = rms_rows(C, d[:, c0:c0 + n], db, 128, n, scr, scrb, st, stb)
                S.op("dve", lambda d=d, c0=c0, n=n, gt=gt, rstd=rstd: nc.vector.scalar_tensor_tensor(
                    out=d[:, c0:c0 + n], in0=d[:, c0:c0 + n], scalar=rstd, in1=gt[:, :], op0=ALU.mult, op1=ALU.mult),
                    reads=[stb, gb_], writes=[db])
            if tt < TP:
                S.dma("pool", I["o_ckv"][j, tt:tt + 128, :], d[:, QR:QR + KVR], dst, db)
                S.dma("pool", I["o_kpe"][j, tt:tt + 128, :], d[:, QR + KVR:], dst, db)
            else:
                rt.build((tt - TP) // 128)
                rt.apply(d[:, QR + KVR:].rearrange("p (h r) -> p h r", h=1), db, 1, rtmp, rtmpb)
            ts.run(d[:, 0:QR], db, QR, cqT, dst, tt, 128)
            ts.run(d[:, QR:QR + KVR], db, KVR, ckvT, dst, tt, 128)
            ts.run(d[:, QR + KVR:], db, RP, kpeT, dst, tt, 128)
        cr = Rot(C, "mc", [128, KVR + RP], 2)
        for tt in range(0, cf.PAST, 128):
            c_, cb = cr.next()
            S.dma("sp", c_[:, :KVR], I["cache_mla_ckv"][j, tt:tt + 128, :], cb, src)
            S.dma("sp", c_[:, KVR:], I["cache_mla_kpe"][j, tt:tt + 128, :], cb, src)
            ts.run(c_[:, :KVR], cb, KVR, ckvT, dst, T + tt, 128)
            ts.run(c_[:, KVR:], cb, RP, kpeT, dst, T + tt, 128)
    HQ = cf.NOPE + RP
    Q = C.dram("Q", [T, H * HQ])
    linear_bf(C, cqT, T, QR, wcols(I["mla_w_uq"][j], 0, H * HQ, 512), Q, mode="tm")
    qT = C.dram("qT", [H * HQ, T])
    with C.phase("mla_q"):
        src = C.buf("src", sb=False); dst = C.buf("dst", sb=False)
        ts = TStore(C)
        qr_ = Rot(C, "mq", [128, H * HQ], 2)
        rt = RopeTab(C, nf, RP // 2, cf.GRID_W)
        rtmp = C.sb("rtmp", [128, 4, H, nf]); rtmpb = C.buf("rtmp")
        for tt in range(0, T, 128):
            q, qb = qr_.next()
            S.dma("sp", q[:], Q[tt:tt + 128, :], qb, src)
            if tt >= TP:
                rt.build((tt - TP) // 128)
                rt.apply(q[:].rearrange("p (h r) -> p h r", h=H)[:, :, cf.NOPE:], qb, H, rtmp, rtmpb)
            ts.run(q, qb, H * HQ, qT, dst, tt, 128)
    KNT = C.dram("KNT", [H * cf.NOPE, TK])
    Vm = C.dram("Vm", [TK, H * cf.MV])
    wkv = I["mla_w_ukv"][j]
    HW = cf.NOPE + cf.MV
    linear_bf(C, ckvT, TK, KVR, [([(0, wkv[:, h * HW:h * HW + cf.NOPE])], cf.NOPE) for h in range(H)], KNT, mode="fm")
    linear_bf(C, ckvT, TK, KVR, [([(0, wkv[:, h * HW + cf.NOPE:(h + 1) * HW])], cf.MV) for h in range(H)], Vm, mode="tm")
    OT = C.dram("OT", [H * cf.MV, T])
    items = []
    seqs = [(s * cf.L_P, cf.L_P, s * cf.L_P, cf.L_P) for s in range(cf.B_P)] + [(TP, cf.L_S, TP, cf.L_S + cf.PAST)]
    for (q0, Lq, k0, Lk) in seqs:
        for h in range(H):
            items.append(dict(
                kparts=[(KNT[h * cf.NOPE:(h + 1) * cf.NOPE, k0:k0 + Lk], cf.NOPE), (kpeT[:, k0:k0 + Lk], RP)],
                v=Vm[k0:k0 + Lk, h * cf.MV:(h + 1) * cf.MV], Lk=Lk, Lq=Lq,
                qheads=[dict(qparts=[(qT[h * HQ:h * HQ + cf.NOPE, q0:q0 + Lq], cf.NOPE),
                                     (qT[h * HQ + cf.NOPE:(h + 1) * HQ, q0:q0 + Lq], RP)],
                             oT=OT[h * cf.MV:(h + 1) * cf.MV, q0:q0 + Lq], sink=None)],
                segs=(lambda qt, Lk=Lk: [(0, Lk, None)])))
    attention(C, items, float(HQ) ** -0.5, cf.MV, maxk=cf.L_S + cf.PAST)
    Y = C.dram("Y", [T, D])
    linear_bf(C, OT, T, H * cf.MV, wcols(I["mla_w_o"][j], 0, D, 512), Y, mode="tm")
    Yb = C.buf("Y", sb=False)
    residual(C, X, Xb, Y, Yb, D, MR["g1"])


def peer_layer(C, cf, I, i, X, Xb, MR):
    D, T = cf.D, cf.T
    h2T = C.dram("h2T", [D, T])
    H2 = C.dram("H2", [T, D])
    b1 = C.buf("h2T", sb=False); b2 = C.buf("H2", sb=False)
    norm_mod_T(C, X, Xb, D, MR["n2"], h2T, b1, tm_out=H2, tm_b=b2)
    NQ = cf.PH * 2 * 128
    QT = C.dram("pQT", [NQ, T])
    linear(C, h2T, T, D, wcols(I["peer_w_q"][i], 0, NQ, 128), QT, mode="fm")
    NE = cf.NKEY * cf.NKEY
    peer(C, QT, H2, X, Xb, D, T, I["peer_keys"][i], C.u_bf, C.v_bf, MR["g2"], PH=cf.PH, NKEY=cf.NKEY, TOPK=cf.TOPK,
         row_off=i * NE, tab_buf=C.tab_buf)


def build_program(cf):
    nc = bass.Bass("TRN2", target_bir_lowering=False)
    I = declare_io(nc, cf)
    C = Ctx(nc)
    C.make_ident(); make_masks(C); make_iota16(C); peer_prep(C)
    D, T, TP = cf.D, cf.T, cf.TP
    C.u_bf, C.v_bf = cast_tables(C, I["peer_u"].rearrange("l e d -> (l e) d"), I["peer_v"].rearrange("l e d -> (l e) d"),
                                 cf.DEPTH * cf.NKEY * cf.NKEY, D)
    X = C.dram("X", [T, D])
    Xb = C.buf("X", sb=False)
    with C.phase("xin"):
        cb = C.buf("cp", sb=False)
        C.S.dma("sp", X[0:TP, :], I["x_p"][:, :], cb, None, owner=cb)
        C.S.dma("sp", X[TP:T, :], I["x_s"][:, :], cb, None, owner=cb)
    MODROWS = C.dram("modrows", [cf.DEPTH, 2, 6, D])
    compute_mod(C, cf, I, MODROWS)
    for i in range(cf.DEPTH):
        kind, j = i % 4, i // 4
        M = MODROWS[i]
        MR = dict(n1=[(0, TP, M[0, 1], M[0, 0]), (TP, T, M[1, 1], M[1, 0])],
                  g1=[(0, TP, M[0, 2]), (TP, T, M[1, 2])],
                  n2=[(0, TP, M[0, 4], M[0, 3]), (TP, T, M[1, 4], M[1, 3])],
                  g2=[(0, TP, M[0, 5]), (TP, T, M[1, 5])])
        hT = C.dram("hT", [D, T])
        hTb = C.buf("hT", sb=False)
        norm_mod_T(C, X, Xb, D, MR["n1"], hT, hTb)
        if kind == 0:
            mla_layer(C, cf, I, j, hT, hTb, X, Xb, MR)
        elif kind == 1:
            s5_layer(C, cf, I, j, hT, hTb, X, Xb, MR)
        elif kind == 2:
            hy_layer(C, cf, I, j, hT, hTb, X, Xb, MR)
        else:
            swa_layer(C, cf, I, j, hT, hTb, X, Xb, MR)
        peer_layer(C, cf, I, i, X, Xb, MR)
    norm_out(C, X, Xb, D, T, I["final_g"], [(0, TP, I["y_p"]), (TP, T, I["y_s"])])
    C.S.barrier(C.bufs)
    return nc, C


def make_in_maps(cf, inp, n_cores):
    n_mla, n_s5, n_hy, n_swa = [len(range(k, cf.DEPTH, 4)) for k in range(4)]
    shared = {}
    for k in ("c_ctx", "mod_w", "mod_b", "norm1_g", "norm2_g", "final_g", "peer_w_q", "peer_keys", "peer_u", "peer_v"):
        shared[k] = np.ascontiguousarray(inp[k], dtype=np.float32)
    if n_mla:
        for k in ("mla_w_down", "mla_g_q", "mla_g_kv", "mla_w_uq", "mla_w_ukv", "mla_w_o"):
            shared[k] = np.ascontiguousarray(inp[k], dtype=np.float32)
    if n_s5:
        for k in ("s5_w_in", "s5_lam_re", "s5_lam_im", "s5_log_dt", "s5_b_re", "s5_b_im", "s5_c_re", "s5_c_im", "s5_d",
                  "s5_w_gate", "s5_w_out"):
            shared[k] = np.ascontiguousarray(inp[k], dtype=np.float32)
    if n_hy:
        for k in ("hy_w_in", "hy_b_in", "hy_short_w", "hy_short_b", "hy_f_w1", "hy_f_b1", "hy_f_freq", "hy_f_w2", "hy_f_b2",
                  "hy_f_w3", "hy_bias", "hy_w_out"):
            shared[k] = np.ascontiguousarray(inp[k], dtype=np.float32)
        shared["hy_log_decay"] = np.ascontiguousarray(inp["hy_log_decay"], dtype=np.float32).reshape(n_hy, 4 * cf.D)
    if n_swa:
        for k in ("swa_w_qkv", "swa_w_o", "swa_sink"):
            shared[k] = np.ascontiguousarray(inp[k], dtype=np.float32)
    maps = []
    for c in range(n_cores):
        m = dict(shared)
        m["x_p"] = np.ascontiguousarray(inp["x_prompt"][c * cf.B_P:(c + 1) * cf.B_P]).reshape(cf.TP, cf.D)
        m["x_s"] = np.ascontiguousarray(inp["x_sample"][c])
        m["c"] = np.ascontiguousarray(inp["c"][c])
        m["cache_mla_ckv"] = np.ascontiguousarray(inp["cache_mla_ckv"][c]) if n_mla else np.zeros((1, cf.PAST, cf.KVR), np.float32)
        m["cache_mla_kpe"] = np.ascontiguousarray(inp["cache_mla_kpe"][c]) if n_mla else np.zeros((1, cf.PAST, cf.ROPE), np.float32)
        m["state_s5_re"] = np.ascontiguousarray(inp["state_s5_re"][c]) if n_s5 else np.zeros((1, 2, cf.G, cf.S5P), np.float32)
        m["state_s5_im"] = np.ascontiguousarray(inp["state_s5_im"][c]) if n_s5 else np.zeros((1, 2, cf.G, cf.S5P), np.float32)
        kvd = cf.SKV * cf.SHD
        m["cache_swa_k"] = np.ascontiguousarray(inp["cache_swa_k"][c]).reshape(-1, cf.PAST, kvd) if n_swa else np.zeros((1, cf.PAST, kvd), np.float32)
        m["cache_swa_v"] = np.ascontiguousarray(inp["cache_swa_v"][c]).reshape(-1, cf.PAST, kvd) if n_swa else np.zeros((1, cf.PAST, kvd), np.float32)
        maps.append(m)
    return maps


def assemble(cf, results, n_cores):
    n_mla, n_s5, n_hy, n_swa = [len(range(k, cf.DEPTH, 4)) for k in range(4)]
    D = cf.D
    y_p = np.concatenate([r["y_p"].reshape(cf.B_P, cf.L_P, D) for r in results], 0)
    y_s = np.stack([r["y_s"] for r in results], 0)
    ckv = np.concatenate([r["o_ckv"].reshape(-1, cf.B_P, cf.L_P, cf.KVR).transpose(1, 0, 2, 3) for r in results], 0)[:, :n_mla]
    kpe = np.concatenate([r["o_kpe"].reshape(-1, cf.B_P, cf.L_P, cf.ROPE).transpose(1, 0, 2, 3) for r in results], 0)[:, :n_mla]
    s5re = np.concatenate([r["o_s5re"].reshape(cf.B_P, -1, 2, cf.G, cf.S5P) for r in results], 0)[:, :n_s5]
    s5im = np.concatenate([r["o_s5im"].reshape(cf.B_P, -1, 2, cf.G, cf.S5P) for r in results], 0)[:, :n_s5]
    swk = np.concatenate([r["o_swak"].reshape(-1, cf.B_P, cf.L_P, cf.SKV, cf.SHD).transpose(1, 0, 2, 3, 4) for r in results], 0)[:, :n_swa]
    swv = np.concatenate([r["o_swav"].reshape(-1, cf.B_P, cf.L_P, cf.SKV, cf.SHD).transpose(1, 0, 2, 3, 4) for r in results], 0)[:, :n_swa]
    return tuple(np.ascontiguousarray(a, dtype=np.float32) for a in (y_p, y_s, ckv, kpe, s5re, s5im, swk, swv))


def kernel(**inputs):
    from concourse.bass_utils import run_bass_kernel_spmd
    cf = Cfg()
    nc, C = build_program(cf)
    maps = make_in_maps(cf, inputs, 8)
    res = run_bass_kernel_spmd(nc, maps, core_ids=list(range(8)))
    return assemble(cf, res.results, 8)


def s5_layer(C, cf, I, j, hT, hTb, X, Xb, MR):
    nc, S = C.nc, C.S
    D, T, TP = cf.D, cf.T, cf.TP
    G, P = cf.G, cf.S5P
    J = G // 2
    assert P == 64
    UT = C.dram("UT", [D, T])
    linear_bf(C, hT, T, D, wcols(I["s5_w_in"][j], 0, D, 512), UT, mode="fm")
    ZT = C.dram("ZT", [D, T])
    seqs = [(s * cf.L_P, cf.L_P, s) for s in range(cf.B_P)] + [(TP, cf.L_S, None)]
    Lmax = max(cf.L_P, cf.L_S)
    nsteps_max = int(math.log2(Lmax))
    with C.phase("s5"):
        src = C.buf("src", sb=False); dst = C.buf("dst", sb=False)
        pb = C.buf("s5par")
        lam = C.sb("lam", [128, 2, 2, J])
        dt = C.sb("dt", [128, 2, J])
        for d in range(2):
            S.dma("sp", lam[:, d, 0, :], I["s5_lam_re"][j, d].rearrange("(j g) p -> (g p) j", g=2), pb, src,
                  allow_slow_non_contiguous=True)
            S.dma("sp", lam[:, d, 1, :], I["s5_lam_im"][j, d].rearrange("(j g) p -> (g p) j", g=2), pb, src,
                  allow_slow_non_contiguous=True)
            ldt = I["s5_log_dt"][j, d].rearrange("(j g) -> g j", g=2)
            for g2 in range(2):
                S.dma("sp", dt[g2 * 64:(g2 + 1) * 64, d, :], ldt[g2].partition_broadcast(64), pb, src,
                      allow_slow_non_contiguous=True)
        S.op("act", lambda: nc.scalar.activation(out=dt[:], in_=dt[:], func=AF.Exp), reads=[pb], writes=[pb])
        mag = C.sb("mag", [128, 2, J]); th = C.sb("th", [128, 2, 2, J])
        ti_ = C.sb("s5ti", [128, 2, 2, J], I32); tf_ = C.sb("s5tf", [128, 2, 2, J])
        S.op("dve", lambda: nc.vector.tensor_tensor(out=mag[:], in0=lam[:, :, 0, :], in1=dt[:], op=ALU.mult), reads=[pb], writes=[pb])
        S.op("act", lambda: nc.scalar.activation(out=mag[:], in_=mag[:], func=AF.Exp), reads=[pb], writes=[pb])
        S.op("dve", lambda: nc.vector.tensor_tensor(out=th[:, :, 1, :], in0=lam[:, :, 1, :], in1=dt[:], op=ALU.mult), reads=[pb], writes=[pb])
        S.op("dve", lambda: nc.vector.tensor_scalar(out=th[:, :, 0, :], in0=th[:, :, 1, :], scalar1=0.5 * PI, scalar2=None,
                                                    op0=ALU.add), reads=[pb], writes=[pb])
        range_reduce(C, th[:], pb, ti_[:], tf_[:], pb)
        S.op("act", lambda: nc.scalar.activation(out=th[:], in_=th[:], func=AF.Sin), reads=[pb], writes=[pb])
        NST = nsteps_max
        apow = C.sb("apow", [128, 2, NST + 1, 3, J])
        for d in range(2):
            S.op("dve", lambda d=d: nc.vector.tensor_tensor(out=apow[:, d, 0, 0, :], in0=mag[:, d, :], in1=th[:, d, 0, :], op=ALU.mult),
                 reads=[pb], writes=[pb])
            S.op("dve", lambda d=d: nc.vector.tensor_tensor(out=apow[:, d, 0, 1, :], in0=mag[:, d, :], in1=th[:, d, 1, :], op=ALU.mult),
                 reads=[pb], writes=[pb])
        sq = C.sb("sq", [128, 2, 3, J])
        for k in range(NST):
            S.op("dve", lambda k=k: nc.vector.tensor_tensor(out=sq[:, :, 0, :], in0=apow[:, :, k, 0, :], in1=apow[:, :, k, 0, :], op=ALU.mult), reads=[pb], writes=[pb])
            S.op("dve", lambda k=k: nc.vector.tensor_tensor(out=sq[:, :, 1, :], in0=apow[:, :, k, 1, :], in1=apow[:, :, k, 1, :], op=ALU.mult), reads=[pb], writes=[pb])
            S.op("dve", lambda k=k: nc.vector.tensor_tensor(out=sq[:, :, 2, :], in0=apow[:, :, k, 0, :], in1=apow[:, :, k, 1, :], op=ALU.mult), reads=[pb], writes=[pb])
            S.op("dve", lambda k=k: nc.vector.tensor_tensor(out=apow[:, :, k + 1, 0, :], in0=sq[:, :, 0, :], in1=sq[:, :, 1, :], op=ALU.subtract), reads=[pb], writes=[pb])
            S.op("dve", lambda k=k: nc.vector.tensor_scalar(out=apow[:, :, k + 1, 1, :], in0=sq[:, :, 2, :], scalar1=2.0, scalar2=None, op0=ALU.mult), reads=[pb], writes=[pb])
        S.op("dve", lambda: nc.vector.tensor_scalar(out=apow[:, :, :, 2, :], in0=apow[:, :, :, 1, :], scalar1=-1.0, scalar2=None, op0=ALU.mult), reads=[pb], writes=[pb])
        cf_ = C.sb("coef", [128, 2, 2, J]); t1 = C.sb("ct1", [128, 2, J]); t2 = C.sb("ct2", [128, 2, J]); den = C.sb("den", [128, 2, J])
        arm1 = C.sb("arm1", [128, 2, J])
        S.op("dve", lambda: nc.vector.tensor_scalar(out=arm1[:], in0=apow[:, :, 0, 0, :], scalar1=-1.0, scalar2=None, op0=ALU.add), reads=[pb], writes=[pb])
        S.op("dve", lambda: nc.vector.tensor_tensor(out=den[:], in0=lam[:, :, 0, :], in1=lam[:, :, 0, :], op=ALU.mult), reads=[pb], writes=[pb])
        S.op("dve", lambda: nc.vector.tensor_tensor(out=t1[:], in0=lam[:, :, 1, :], in1=lam[:, :, 1, :], op=ALU.mult), reads=[pb], writes=[pb])
        S.op("dve", lambda: nc.vector.tensor_tensor(out=den[:], in0=den[:], in1=t1[:], op=ALU.add), reads=[pb], writes=[pb])
        S.op("dve", lambda: nc.vector.reciprocal(out=den[:], in_=den[:]), reads=[pb], writes=[pb])
        S.op("dve", lambda: nc.vector.tensor_tensor(out=t1[:], in0=arm1[:], in1=lam[:, :, 0, :], op=ALU.mult), reads=[pb], writes=[pb])
        S.op("dve", lambda: nc.vector.tensor_tensor(out=t2[:], in0=apow[:, :, 0, 1, :], in1=lam[:, :, 1, :], op=ALU.mult), reads=[pb], writes=[pb])
        S.op("dve", lambda: nc.vector.tensor_tensor(out=t1[:], in0=t1[:], in1=t2[:], op=ALU.add), reads=[pb], writes=[pb])
        S.op("dve", lambda: nc.vector.tensor_tensor(out=cf_[:, :, 0, :], in0=t1[:], in1=den[:], op=ALU.mult), reads=[pb], writes=[pb])
        S.op("dve", lambda: nc.vector.tensor_tensor(out=t1[:], in0=apow[:, :, 0, 1, :], in1=lam[:, :, 0, :], op=ALU.mult), reads=[pb], writes=[pb])
        S.op("dve", lambda: nc.vector.tensor_tensor(out=t2[:], in0=arm1[:], in1=lam[:, :, 1, :], op=ALU.mult), reads=[pb], writes=[pb])
        S.op("dve", lambda: nc.vector.tensor_tensor(out=t1[:], in0=t1[:], in1=t2[:], op=ALU.subtract), reads=[pb], writes=[pb])
        S.op("dve", lambda: nc.vector.tensor_tensor(out=cf_[:, :, 1, :], in0=t1[:], in1=den[:], op=ALU.mult), reads=[pb], writes=[pb])
        Bm = C.sb("Bm", [128, 2, 2, J, 16])
        for d in range(2):
            S.dma("sp", Bm[:, d, 0], I["s5_b_re"][j, d].rearrange("(j g) p c -> (g p) j c", g=2), pb, src)
            S.dma("sp", Bm[:, d, 1], I["s5_b_im"][j, d].rearrange("(j g) p c -> (g p) j c", g=2), pb, src)
        BD = C.sb("BD", [128, J, 2, 2, 2, 16])
        bt = C.sb("bt", [128, 2, J, 16]); bt2 = C.sb("bt2", [128, 2, J, 16])
        S.op("pool", lambda: nc.gpsimd.memset(BD[:], 0.0), writes=[pb])
        crb = cf_[:, :, 0, :].unsqueeze(3).broadcast_to([128, 2, J, 16])
        cib = cf_[:, :, 1, :].unsqueeze(3).broadcast_to([128, 2, J, 16])
        for ri, (xa, xb_, op) in enumerate(((0, 1, ALU.subtract), (1, 0, ALU.add))):
            S.op("dve", lambda xa=xa: nc.vector.tensor_tensor(out=bt[:], in0=Bm[:, :, xa], in1=crb, op=ALU.mult), reads=[pb], writes=[pb])
            S.op("dve", lambda xb_=xb_: nc.vector.tensor_tensor(out=bt2[:], in0=Bm[:, :, xb_], in1=cib, op=ALU.mult), reads=[pb], writes=[pb])
            for g2 in range(2):
                sl = slice(g2 * 64, (g2 + 1) * 64)
                S.op("dve", lambda ri=ri, g2=g2, sl=sl, op=op: nc.vector.tensor_tensor(
                    out=BD[sl, :, :, ri, g2, :], in0=bt[sl].rearrange("p d j c -> p j d c"),
                    in1=bt2[sl].rearrange("p d j c -> p j d c"), op=op), reads=[pb], writes=[pb])
        CD = C.sb("CD", [128, J, 2, 2, 2, 16])
        S.op("pool", lambda: nc.gpsimd.memset(CD[:], 0.0), writes=[pb])
        for d in range(2):
            for ri, nm in enumerate(("s5_c_re", "s5_c_im")):
                cview = I[nm][j, d].rearrange("(j g) c p -> g p j c", g=2)
                for g2 in range(2):
                    for jj in range(J):
                        S.dma("sp", CD[g2 * 64:(g2 + 1) * 64, jj, d, ri, g2, :], cview[g2, :, jj, :], pb, src,
                              allow_slow_non_contiguous=True)
        S.op("dve", lambda: nc.vector.tensor_scalar(out=CD[:, :, :, 1], in0=CD[:, :, :, 1], scalar1=-1.0, scalar2=None, op0=ALU.mult),
             reads=[pb], writes=[pb])
        h0 = C.sb("h0", [128, 2, 2, J]); init = C.sb("init", [128, 2, 2, J])
        for d in range(2):
            S.dma("sp", h0[:, d, 0, :], I["state_s5_re"][j, d].rearrange("(j g) p -> (g p) j", g=2), pb, src, allow_slow_non_contiguous=True)
            S.dma("sp", h0[:, d, 1, :], I["state_s5_im"][j, d].rearrange("(j g) p -> (g p) j", g=2), pb, src, allow_slow_non_contiguous=True)
        ar_ = apow[:, :, 0, 0, :]; ai_ = apow[:, :, 0, 1, :]
        S.op("dve", lambda: nc.vector.tensor_tensor(out=t1[:], in0=ar_, in1=h0[:, :, 0, :], op=ALU.mult), reads=[pb], writes=[pb])
        S.op("dve", lambda: nc.vector.tensor_tensor(out=t2[:], in0=ai_, in1=h0[:, :, 1, :], op=ALU.mult), reads=[pb], writes=[pb])
        S.op("dve", lambda: nc.vector.tensor_tensor(out=init[:, :, 0, :], in0=t1[:], in1=t2[:], op=ALU.subtract), reads=[pb], writes=[pb])
        S.op("dve", lambda: nc.vector.tensor_tensor(out=t1[:], in0=ar_, in1=h0[:, :, 1, :], op=ALU.mult), reads=[pb], writes=[pb])
        S.op("dve", lambda: nc.vector.tensor_tensor(out=t2[:], in0=ai_, in1=h0[:, :, 0, :], op=ALU.mult), reads=[pb], writes=[pb])
        S.op("dve", lambda: nc.vector.tensor_tensor(out=init[:, :, 1, :], in0=t1[:], in1=t2[:], op=ALU.add), reads=[pb], writes=[pb])
        BD_d = C.dram("s5BD", [128, J, 2, 2, 32]); CD_d = C.dram("s5CD", [128, J, 2, 2, 32])
        AP_d = C.dram("s5AP", [128, 2 * (NST + 1) * 3 * J]); IN_d = C.dram("s5IN", [128, 2 * 2 * J])
        S.dma("pool", BD_d.rearrange("p j d r q -> p (j d r q)"), BD[:].rearrange("p j d r g c -> p (j d r g c)"), dst, pb)
        S.dma("pool", CD_d.rearrange("p j d r q -> p (j d r q)"), CD[:].rearrange("p j d r g c -> p (j d r g c)"), dst, pb)
        S.dma("pool", AP_d, apow[:].rearrange("p d k c j -> p (d k c j)"), dst, pb)
        S.dma("pool", IN_d, init[:].rearrange("p d r j -> p (d r j)"), dst, pb)
    with C.phase("s5main"):
        src = C.buf("src", sb=False); dst = C.buf("dst", sb=False)
        pb = C.buf("s5par")
        apow = C.sb("apow", [128, 2, NST + 1, 3, J])
        init = C.sb("init", [128, 2, 2, J])
        dsk = C.sb("dsk", [32, J])
        S.dma("sp", apow[:].rearrange("p d k c j -> p (d k c j)"), AP_d, pb, src)
        S.dma("sp", init[:].rearrange("p d r j -> p (d r j)"), IN_d, pb, src)
        S.dma("sp", dsk[:], I["s5_d"][j].rearrange("(j q) -> q j", q=32), pb, src, allow_slow_non_contiguous=True)
        bdr = Rot(C, "s5bd", [128, 2, 2, 32], 2)
        cdr = Rot(C, "s5cd", [128, 2, 2, 32], 2)
        fin = C.sb("fin", [128, cf.B_P, 2, 2, J]); finb = C.buf("fin")
        ur = Rot(C, "s5u", [32, Lmax], 2)
        btp = Rot(C, "s5btp", [32, 128], 2, psum=True)
        bts = Rot(C, "s5bts", [32, 2, 128], 2)
        bup = Rot(C, "s5bup", [128, 512], 2, psum=True)
        yp = Rot(C, "s5yp", [32, 512], 2, psum=True)
        hbuf = [[[C.sb("s5h%d%d%d" % (d, pp, ri), [128, Lmax]) for ri in range(2)] for pp in range(2)] for d in range(2)]
        hb_ = [[C.buf("s5h") for pp in range(2)] for d in range(2)]
        zr = Rot(C, "s5z", [32, Lmax], 2)
        zt = C.sb("s5zt", [32, Lmax]); ztb = C.buf("s5zt")
        groups = [(0, cf.L_P, cf.B_P, True), (TP, cf.L_S, 1, False)]
        for (t0, L, nseq, is_p) in groups:
            nst = int(math.log2(L))
            assert (1 << nst) == L
            W = nseq * L
            assert W <= Lmax

            def v3(t):
                return t[:, :W].rearrange("p (s l) -> p s l", s=nseq)
            for jj in range(J):
                u, ub = ur.next()
                S.dma("sp", u[:, :W], UT[jj * 32:(jj + 1) * 32, t0:t0 + W], ub, src)
                BDp, BDpb = bdr.next()
                CDp, CDpb = cdr.next()
                S.dma("sp", BDp[:], BD_d[:, jj], BDpb, src)
                S.dma("sp", CDp[:], CD_d[:, jj], CDpb, src)
                for d in range(2):
                    bs, bsb = bts.next()
                    for ri in range(2):
                        tp_, tpb = btp.next()
                        S.op("pe", lambda d=d, ri=ri, tp_=tp_: nc.tensor.transpose(
                            out=tp_[:], in_=BDp[:, d, ri, :], identity=C.ident[:]),
                            reads=[BDpb, C.ident_b], writes=[tpb])
                        S.op("act", lambda ri=ri, tp_=tp_, bs=bs: nc.scalar.copy(out=bs[:, ri, :], in_=tp_[:]), reads=[tpb], writes=[bsb])
                    for ri in range(2):
                        for c0 in range(0, W, 512):
                            n = min(512, W - c0)
                            bp, bpb = bup.next()
                            S.mm_group([lambda c0=c0, n=n, ri=ri, bp=bp, bs=bs: nc.tensor.matmul(
                                bp[:, :n], lhsT=bs[:, ri, :], rhs=u[:, c0:c0 + n], start=True, stop=True)],
                                reads=[bsb, ub], writes=[bpb])
                            S.op("act", lambda ri=ri, d=d, bp=bp, c0=c0, n=n: nc.scalar.copy(
                                out=hbuf[d][0][ri][:, c0:c0 + n], in_=bp[:, :n]), reads=[bpb], writes=[hb_[d][0]])
                    if not is_p:
                        col = (L - 1) if d == 1 else 0
                        for ri in range(2):
                            S.op("dve", lambda ri=ri, d=d, col=col: nc.vector.tensor_tensor(
                                out=hbuf[d][0][ri][:, col:col + 1], in0=hbuf[d][0][ri][:, col:col + 1],
                                in1=init[:, d, ri, jj:jj + 1], op=ALU.add), reads=[pb], writes=[hb_[d][0]])
                cur = 0
                for k in range(nst):
                    s_ = 1 << k
                    for d in range(2):
                        o, n_ = hbuf[d][cur], hbuf[d][1 - cur]
                        ob, nb = hb_[d][cur], hb_[d][1 - cur]
                        ar = apow[:, d, k, 0, jj:jj + 1]; ai = apow[:, d, k, 1, jj:jj + 1]; nai = apow[:, d, k, 2, jj:jj + 1]
                        if d == 0:
                            dsl, ssl, psl = slice(s_, L), slice(0, L - s_), slice(0, s_)
                        else:
                            dsl, ssl, psl = slice(0, L - s_), slice(s_, L), slice(L - s_, L)
                        o0, o1, n0, n1 = v3(o[0]), v3(o[1]), v3(n_[0]), v3(n_[1])
                        S.op("dve", lambda: nc.vector.scalar_tensor_tensor(out=n0[:, :, dsl], in0=o0[:, :, ssl], scalar=ar, in1=o0[:, :, dsl],
                                                                          op0=ALU.mult, op1=ALU.add), reads=[ob, pb], writes=[nb])
                        S.op("dve", lambda: nc.vector.scalar_tensor_tensor(out=n0[:, :, dsl], in0=o1[:, :, ssl], scalar=nai, in1=n0[:, :, dsl],
                                                                          op0=ALU.mult, op1=ALU.add), reads=[ob, pb], writes=[nb])
                        S.op("dve", lambda: nc.vector.scalar_tensor_tensor(out=n1[:, :, dsl], in0=o1[:, :, ssl], scalar=ar, in1=o1[:, :, dsl],
                                                                          op0=ALU.mult, op1=ALU.add), reads=[ob, pb], writes=[nb])
                        S.op("dve", lambda: nc.vector.scalar_tensor_tensor(out=n1[:, :, dsl], in0=o0[:, :, ssl], scalar=ai, in1=n1[:, :, dsl],
                                                                          op0=ALU.mult, op1=ALU.add), reads=[ob, pb], writes=[nb])
                        S.op("act", lambda: nc.scalar.copy(out=n0[:, :, psl], in_=o0[:, :, psl]), reads=[ob], writes=[nb])
                        S.op("act", lambda: nc.scalar.copy(out=n1[:, :, psl], in_=o1[:, :, psl]), reads=[ob], writes=[nb])
                    cur = 1 - cur
                finals = [(hbuf[d][cur], hb_[d][cur]) for d in range(2)]
                if is_p:
                    for d in range(2):
                        col = 0 if d == 1 else L - 1
                        for ri in range(2):
                            S.op("act", lambda ri=ri, d=d, col=col, cur=cur: nc.scalar.copy(
                                out=fin[:, :, d, ri, jj:jj + 1], in_=v3(hbuf[d][cur][ri])[:, :, col:col + 1]),
                                reads=[hb_[d][cur]], writes=[finb])
                z, zb = zr.next()
                for c0 in range(0, W, 512):
                    n = min(512, W - c0)
                    y, yb = yp.next()
                    fns = []
                    for d in range(2):
                        for ri in range(2):
                            fns.append(lambda d=d, ri=ri, c0=c0, n=n, y=y: nc.tensor.matmul(
                                y[:, :n], lhsT=CDp[:, d, ri, :],
                                rhs=finals[d][0][ri][:, c0:c0 + n], start=(d == 0 and ri == 0), stop=(d == 1 and ri == 1)))
                    S.mm_group(fns, reads=[CDpb, finals[0][1], finals[1][1]], writes=[yb])
                    S.op("dve", lambda c0=c0, n=n, y=y: nc.vector.scalar_tensor_tensor(
                        out=z[:, c0:c0 + n], in0=u[:, c0:c0 + n], scalar=dsk[:, jj:jj + 1], in1=y[:, :n],
                        op0=ALU.mult, op1=ALU.add), reads=[ub, yb, pb], writes=[zb])
                gelu_tanh(C, None, z[:, :W], z[:, :W], zt[:, :W], zb, zb, ztb)
                S.dma("pool", ZT[jj * 32:(jj + 1) * 32, t0:t0 + W], z[:, :W], dst, zb)
        for sq_ in range(cf.B_P):
            for d in range(2):
                S.dma("pool", I["o_s5re"][sq_, j, d].rearrange("(jj q) -> q jj", q=128), fin[:, sq_, d, 0, :], dst, finb,
                      allow_slow_non_contiguous=True)
                S.dma("pool", I["o_s5im"][sq_, j, d].rearrange("(jj q) -> q jj", q=128), fin[:, sq_, d, 1, :], dst, finb,
                      allow_slow_non_contiguous=True)
    ZGT = C.dram("ZGT", [D, T])
    linear_bf(C, ZT, T, D, wcols(I["s5_w_gate"][j], 0, D, 512), ZGT, mode="fm", act=AF.Sigmoid, mul_ap=ZT)
    Y = C.dram("Y", [T, D])
    linear_bf(C, ZGT, T, D, wcols(I["s5_w_out"][j], 0, D, 512), Y, mode="tm")
    Yb = C.buf("Y", sb=False)
    residual(C, X, Xb, Y, Yb, D, MR["g1"])


def swa_layer(C, cf, I, j, hT, hTb, X, Xb, MR):
    nc, S = C.nc, C.S
    D, T, TP, TK = cf.D, cf.T, cf.TP, cf.TK
    H, KVH, HD = cf.SH, cf.SKV, cf.SHD
    NQ, NK = H * HD, KVH * HD
    NT = NQ + 2 * NK
    QKV = C.dram("QKV", [T, NT])
    linear_bf(C, hT, T, D, wcols(I["swa_w_qkv"][j], 0, NT, 512), QKV, mode="tm")
    qT = C.dram("sqT", [NQ, T])
    kT = C.dram("skT", [NK, TK])
    Vm = C.dram("sVm", [TK, NK])
    nf = HD // 4
    with C.phase("swa_rows"):
        src = C.buf("src", sb=False); dst = C.buf("dst", sb=False)
        ts = TStore(C)
        qr_ = Rot(C, "sq", [128, NT], 2)
        rt = RopeTab(C, nf, HD // 2, cf.GRID_W)
        rtmp = C.sb("rtmp", [128, 4, H + KVH, nf]); rtmpb = C.buf("rtmp")
        for tt in range(0, T, 128):
            q, qb = qr_.next()
            S.dma("sp", q[:], QKV[tt:tt + 128, :], qb, src)
            if tt < TP:
                S.dma("pool", I["o_swak"][j, tt:tt + 128, :], q[:, NQ:NQ + NK], dst, qb)
                S.dma("pool", I["o_swav"][j, tt:tt + 128, :], q[:, NQ + NK:], dst, qb)
            else:
                rt.build((tt - TP) // 128)
                rt.apply(q[:, :NQ + NK].rearrange("p (h r) -> p h r", h=H + KVH), qb, H + KVH, rtmp, rtmpb)
            S.dma("pool", Vm[tt:tt + 128, :], q[:, NQ + NK:], dst, qb)
            ts.run(q[:, :NQ], qb, NQ, qT, dst, tt, 128)
            ts.run(q[:, NQ:NQ + NK], qb, NK, kT, dst, tt, 128)
        cr = Rot(C, "sc", [128, 2 * NK], 2)
        for tt in range(0, cf.PAST, 128):
            c_, cb = cr.next()
            S.dma("sp", c_[:, :NK], I["cache_swa_k"][j, tt:tt + 128, :], cb, src)
            S.dma("sp", c_[:, NK:], I["cache_swa_v"][j, tt:tt + 128, :], cb, src)
            ts.run(c_[:, :NK], cb, NK, kT, dst, T + tt, 128)
            S.dma("pool", Vm[T + tt:T + tt + 128, :], c_[:, NK:], dst, cb)
    OT = C.dram("sOT", [NQ, T])
    items = []
    G = H // KVH
    nctx = cf.PAST
    for s in range(cf.B_P):
        q0 = s * cf.L_P
        for kv in range(KVH):
            items.append(dict(
                kparts=[(kT[kv * HD:(kv + 1) * HD, q0:q0 + cf.L_P], HD)], v=Vm[q0:q0 + cf.L_P, kv * HD:(kv + 1) * HD],
                Lk=cf.L_P, Lq=cf.L_P,
                qheads=[dict(qparts=[(qT[(kv * G + g) * HD:(kv * G + g + 1) * HD, q0:q0 + cf.L_P], HD)],
                             oT=OT[(kv * G + g) * HD:(kv * G + g + 1) * HD, q0:q0 + cf.L_P], sink=kv * G + g) for g in range(G)],
                segs=(lambda qt: [(0, cf.L_P, None)])))
    L = cf.L_S
    nbq = L // 128

    def segs(qt):
        sg = []
        if qt > 0:
            sg.append(((qt - 1) * 128, 128, "ge"))
        sg.append((qt * 128, 128, None))
        if qt < nbq - 1:
            sg.append(((qt + 1) * 128, 128, "le"))
        sg.append((L, nctx, None))
        return sg
    for kv in range(KVH):
        items.append(dict(
            kparts=[(kT[kv * HD:(kv + 1) * HD, TP:TP + L + nctx], HD)], v=Vm[TP:TP + L + nctx, kv * HD:(kv + 1) * HD],
            Lk=L + nctx, Lq=L,
            qheads=[dict(qparts=[(qT[(kv * G + g) * HD:(kv * G + g + 1) * HD, TP:TP + L], HD)],
                         oT=OT[(kv * G + g) * HD:(kv * G + g + 1) * HD, TP:TP + L], sink=kv * G + g) for g in range(G)],
            segs=segs))
    attention(C, items, float(HD) ** -0.5, HD, sink_ap=I["swa_sink"][j], nsink=H, maxk=max(cf.L_P, 3 * 128 + nctx))
    Y = C.dram("Y", [T, D])
    linear_bf(C, OT, T, NQ, wcols(I["swa_w_o"][j], 0, D, 512), Y, mode="tm")
    Yb = C.buf("Y", sb=False)
    residual(C, X, Xb, Y, Yb, D, MR["g1"])


def ilog2(n):
    k = int(round(math.log2(n)))
    assert (1 << k) == n
    return k


def hy_dft_gen(C, L, Cm, Sm, CmT, NSmT):
    nc, S = C.nc, C.S
    with C.phase("dftgen"):
        dst = C.buf("dst", sb=False)
        b = C.buf("dg")
        odi = C.sb("odi", [128, L], I32); od = C.sb("od", [128, L]); tti = C.sb("tti", [128, L], I32); ttf = C.sb("ttf", [128, L])
        pidx = C.sb("pidx", [128, 1], I32); pf = C.sb("pf", [128, 3])
        S.op("pool", lambda: nc.gpsimd.iota(odi[:], pattern=[[2, L]], base=1, channel_multiplier=0), writes=[b])
        S.op("pool", lambda: nc.gpsimd.iota(tti[:], pattern=[[1, L]], base=0, channel_multiplier=0), writes=[b])
        S.op("pool", lambda: nc.gpsimd.iota(pidx[:], pattern=[[0, 1]], base=0, channel_multiplier=1), writes=[b])
        S.op("dve", lambda: nc.vector.tensor_copy(out=od[:], in_=odi[:]), reads=[b], writes=[b])
        S.op("dve", lambda: nc.vector.tensor_copy(out=ttf[:], in_=tti[:]), reads=[b], writes=[b])
        S.op("dve", lambda: nc.vector.tensor_copy(out=pf[:, 0:1], in_=pidx[:]), reads=[b], writes=[b])
        wr = Rot(C, "dgw", [128, L], 2)
        wi = Rot(C, "dgi", [128, L], 2, dtype=I32)
        wm = Rot(C, "dgm", [128, L], 2)
        wo = Rot(C, "dgo", [128, L], 2, dtype=BF16)
        for r0 in range(0, L, 128):
            S.op("dve", lambda r0=r0: nc.vector.tensor_scalar(out=pf[:, 1:2], in0=pf[:, 0:1], scalar1=float(r0), scalar2=None,
                                                        op0=ALU.add), reads=[b], writes=[b])
            S.op("dve", lambda r0=r0: nc.vector.tensor_scalar(out=pf[:, 2:3], in0=pf[:, 0:1], scalar1=2.0, scalar2=float(2 * r0 + 1),
                                                        op0=ALU.mult, op1=ALU.add), reads=[b], writes=[b])
            for (dst_ap, base, col, shift, sgn) in ((Cm, od, 1, L, 1.0), (Sm, od, 1, 0, 1.0), (CmT, ttf, 2, L, 1.0), (NSmT, ttf, 2, 0, -1.0)):
                w, wb = wr.next(); wi_, wib = wi.next(); m, mb = wm.next()
                S.op("dve", lambda w=w, base=base, col=col, shift=shift: nc.vector.tensor_scalar(
                    out=w[:], in0=base[:], scalar1=pf[:, col:col + 1], scalar2=float(shift), op0=ALU.mult, op1=ALU.add),
                    reads=[b], writes=[wb])
                S.op("dve", lambda w=w, wi_=wi_: nc.vector.tensor_copy(out=wi_[:], in_=w[:]), reads=[wb], writes=[wib])
                S.op("dve", lambda wi_=wi_: nc.vector.tensor_single_scalar(out=wi_[:], in_=wi_[:], scalar=4 * L - 1, op=ALU.bitwise_and),
                     reads=[wib], writes=[wib])
                S.op("dve", lambda w=w, wi_=wi_: nc.vector.tensor_copy(out=w[:], in_=wi_[:]), reads=[wib], writes=[wb])
                S.op("dve", lambda w=w, m=m: nc.vector.tensor_scalar(out=m[:], in0=w[:], scalar1=float(2 * L), scalar2=float(-4 * L),
                                                                 op0=ALU.is_ge, op1=ALU.mult), reads=[wb], writes=[mb])
                S.op("dve", lambda w=w, m=m: nc.vector.tensor_tensor(out=w[:], in0=w[:], in1=m[:], op=ALU.add), reads=[mb], writes=[wb])
                o16, o16b = wo.next()
                S.op("act", lambda w=w, sgn=sgn, o16=o16: nc.scalar.activation(out=o16[:], in_=w[:], func=AF.Sin, scale=sgn * PI / (2 * L)),
                     reads=[wb], writes=[o16b])
                S.dma("sp", dst_ap[r0:r0 + 128, :], o16[:], dst, o16b)


def hy_filter(C, cf, I, j, L, KA, KB):
    nc, S = C.nc, C.S
    D = cf.D
    HH = cf.HY_HID
    NB = cf.HY_BANDS
    HE = 2 * NB + 1
    with C.phase("hyfilt"):
        src = C.buf("src", sb=False); dst = C.buf("dst", sb=False)
        b = C.buf("hf")
        si_ = C.sb("sti", [HH, L], I32); sf_ = C.sb("stf", [HH, L])
        tti = si_[:HE, :]; fti = si_[:HE, :]; fm = sf_[:HE, :]
        tt = C.sb("tt", [HE, L]); ft = C.sb("ft", [HE, L])
        pidx = C.sb("pidx", [128, 1], I32); pi2 = C.sb("pi2", [128, 2], I32); pp = C.sb("pp", [128, 6])
        S.op("pool", lambda: nc.gpsimd.iota(tti[:], pattern=[[1, L]], base=0, channel_multiplier=0), writes=[b])
        S.op("pool", lambda: nc.gpsimd.iota(pidx[:], pattern=[[0, 1]], base=0, channel_multiplier=1), writes=[b])
        S.op("dve", lambda: nc.vector.tensor_copy(out=tt[:], in_=tti[:]), reads=[b], writes=[b])
        S.op("dve", lambda: nc.vector.tensor_single_scalar(out=pi2[:, 0:1], in_=pidx[:], scalar=NB - 1, op=ALU.add), reads=[b], writes=[b])
        S.op("dve", lambda: nc.vector.tensor_single_scalar(out=pi2[:, 0:1], in_=pi2[:, 0:1], scalar=NB - 1, op=ALU.bitwise_and), reads=[b], writes=[b])
        S.op("dve", lambda: nc.vector.tensor_copy(out=pp[:, 0:1], in_=pi2[:, 0:1]), reads=[b], writes=[b])
        S.op("dve", lambda: nc.vector.tensor_copy(out=pp[:, 1:2], in_=pidx[:]), reads=[b], writes=[b])
        S.op("dve", lambda: nc.vector.tensor_scalar(out=pp[:, 2:3], in0=pp[:, 1:2], scalar1=1.0, scalar2=None, op0=ALU.is_ge), reads=[b], writes=[b])
        S.op("dve", lambda: nc.vector.scalar_tensor_tensor(out=pp[:, 3:4], in0=pp[:, 0:1], scalar=1.0, in1=pp[:, 2:3], op0=ALU.add, op1=ALU.mult),
             reads=[b], writes=[b])
        S.op("dve", lambda: nc.vector.tensor_scalar(out=pp[:, 4:5], in0=pp[:, 1:2], scalar1=float(NB + 1), scalar2=float(L // 4), op0=ALU.is_ge,
                                                    op1=ALU.mult), reads=[b], writes=[b])
        S.op("dve", lambda: nc.vector.tensor_scalar(out=ft[:], in0=tt[:], scalar1=pp[:HE, 3:4], scalar2=pp[:HE, 4:5], op0=ALU.mult, op1=ALU.add),
             reads=[b], writes=[b])
        S.op("dve", lambda: nc.vector.tensor_copy(out=fti[:], in_=ft[:]), reads=[b], writes=[b])
        S.op("dve", lambda: nc.vector.tensor_single_scalar(out=fti[:], in_=fti[:], scalar=L - 1, op=ALU.bitwise_and), reads=[b], writes=[b])
        S.op("dve", lambda: nc.vector.tensor_copy(out=ft[:], in_=fti[:]), reads=[b], writes=[b])
        S.op("dve", lambda: nc.vector.tensor_scalar(out=fm[:], in0=ft[:], scalar1=float(L // 2), scalar2=float(-L), op0=ALU.is_ge, op1=ALU.mult),
             reads=[b], writes=[b])
        S.op("dve", lambda: nc.vector.tensor_tensor(out=ft[:], in0=ft[:], in1=fm[:], op=ALU.add), reads=[b], writes=[b])
        S.op("act", lambda: nc.scalar.activation(out=ft[:], in_=ft[:], func=AF.Sin, scale=2 * PI / L), reads=[b], writes=[b])
        S.op("dve", lambda: nc.vector.tensor_scalar(out=ft[0:1, :], in0=tt[0:1, :], scalar1=1.0 / L, scalar2=None, op0=ALU.mult), reads=[b], writes=[b])
        w1 = C.sb("w1", [HE, HH]); w2 = C.sb("w2", [HH, HH]); w3 = C.sb("w3", [HH, 4 * D]); fr = C.sb("fr", [HH, 6])
        S.dma("sp", w1[:], I["hy_f_w1"][j], b, src)
        S.dma("sp", w2[:], I["hy_f_w2"][j], b, src)
        S.dma("sp", w3[:], I["hy_f_w3"][j], b, src)
        S.dma("sp", fr[:, 0:2], I["hy_f_freq"][j].rearrange("two h -> h two"), b, src, allow_slow_non_contiguous=True)
        S.dma("sp", fr[:, 2:3], I["hy_f_b1"][j].rearrange("(h o) -> h o", o=1), b, src, allow_slow_non_contiguous=True)
        S.dma("sp", fr[:, 3:4], I["hy_f_b2"][j].rearrange("(h o) -> h o", o=1), b, src, allow_slow_non_contiguous=True)
        S.op("dve", lambda: nc.vector.tensor_tensor(out=fr[:, 4:6], in0=fr[:, 0:2], in1=fr[:, 2:4], op=ALU.mult), reads=[b], writes=[b])
        s1 = C.sb("s1T", [HH, L]); s2 = C.sb("s2T", [HH, L])
        zp = Rot(C, "hfz", [HH, 512], 2, psum=True)
        for (lw, rhs, out_, k) in ((w1, ft, s1, 0), (w2, s1, s2, 1)):
            for c0 in range(0, L, 512):
                n = min(512, L - c0)
                z, zb = zp.next()
                S.mm_group([lambda z=z, lw=lw, rhs=rhs, c0=c0, n=n: nc.tensor.matmul(z[:, :n], lhsT=lw[:], rhs=rhs[:, c0:c0 + n],
                                                                                start=True, stop=True)], reads=[b], writes=[zb])
                S.op("dve", lambda z=z, out_=out_, c0=c0, n=n, k=k: nc.vector.tensor_scalar(
                    out=out_[:, c0:c0 + n], in0=z[:, :n], scalar1=fr[:, k:k + 1], scalar2=fr[:, 4 + k:5 + k], op0=ALU.mult, op1=ALU.add),
                    reads=[zb, b], writes=[b])
            range_reduce(C, out_[:], b, si_[:], sf_[:], b)
            S.op("act", lambda out_=out_: nc.scalar.activation(out=out_[:], in_=out_[:], func=AF.Sin), reads=[b], writes=[b])
        E, Eb = bc_load(C, "E", I["hy_log_decay"][j], 4 * D)
        S.op("act", lambda: nc.scalar.activation(out=E[:], in_=E[:], func=AF.Exp), reads=[Eb], writes=[Eb])
        FT = C.sb("FT", [128, 4 * D]); FTb = C.buf("FT")
        dec = Rot(C, "dec", [128, 512], 2)
        fp = Rot(C, "hff", [128, 512], 2, psum=True)
        AB = Rot(C, "AB", [128, 2 * D], 1)
        negt = C.sb("negt", [128, 1])
        for t0 in range(0, L, 128):
            S.op("dve", lambda t0=t0: nc.vector.tensor_scalar(out=negt[:], in0=pp[:, 1:2], scalar1=float(t0), scalar2=-1.0 / L, op0=ALU.add,
                                                        op1=ALU.mult), reads=[b], writes=[b])
            for c0 in range(0, 4 * D, 512):
                f, fb = fp.next()
                S.mm_group([lambda f=f, t0=t0, c0=c0: nc.tensor.matmul(f[:], lhsT=s2[:, t0:t0 + 128], rhs=w3[:, c0:c0 + 512], start=True,
                                                                        stop=True)], reads=[b], writes=[fb])
                dc, dcb = dec.next()
                S.op("act", lambda dc=dc, c0=c0: nc.scalar.activation(out=dc[:], in_=E[:, c0:c0 + 512], func=AF.Exp, scale=negt[:, 0:1]),
                     reads=[Eb, b], writes=[dcb])
                S.op("dve", lambda f=f, dc=dc, c0=c0: nc.vector.tensor_tensor(out=FT[:, c0:c0 + 512], in0=f[:], in1=dc[:], op=ALU.mult),
                     reads=[fb, dcb], writes=[FTb])
            if t0 == 0:
                S.op("dve", lambda: nc.vector.memset(FT[0:1, 2 * D:4 * D], 0.0), writes=[FTb])
            A, Ab = AB.next()
            S.op("dve", lambda A=A: nc.vector.tensor_tensor(out=A[:], in0=FT[:, 0:2 * D], in1=FT[:, 2 * D:4 * D], op=ALU.add), reads=[FTb], writes=[Ab])
            S.dma("pool", KA[t0:t0 + 128, :], A[:], dst, Ab)
            B_, Bb = AB.next()
            S.op("dve", lambda B_=B_: nc.vector.tensor_tensor(out=B_[:], in0=FT[:, 2 * D:4 * D], in1=FT[:, 0:2 * D], op=ALU.subtract), reads=[FTb], writes=[Bb])
            S.dma("pool", KB[t0:t0 + 128, :], B_[:], dst, Bb)


class DFTPass:
    def __init__(self, C, L):
        self.C, self.L = C, L
        self.LC = L // 128
        self.mr = Rot(C, "dfm", [128, self.LC, 128], 6, dtype=BF16)
        self.src = C.buf("dfsrc", sb=False)

    def run(self, pairs, ps, psb, ot, n):
        C = self.C
        nc, S = C.nc, C.S
        fns = []
        reads = []
        tot = len(pairs) * self.LC
        k = 0
        for (mat, rhs, rhsb) in pairs:
            m, mb = self.mr.next()
            S.dma("sp", m[:], mat[:, ot * 128:(ot + 1) * 128].rearrange("(c p) m -> p c m", p=128), mb, self.src)
            reads += [mb, rhsb]
            for c in range(self.LC):
                fns.append(lambda m=m, rhs=rhs, c=c, k=k: nc.tensor.matmul(ps[:, :n], lhsT=m[:, c, :], rhs=rhs[:, c, :n],
                                                                           start=(k == 0), stop=(k == tot - 1)))
                k += 1
        S.mm_group(fns, reads=reads, writes=[psb])


def hy_layer(C, cf, I, j, hT, hTb, X, Xb, MR):
    nc, S = C.nc, C.S
    D, T, TP = cf.D, cf.T, cf.TP
    Z = C.dram("hyZ", [T, 3 * D])
    linear_bf(C, hT, T, D, wcols(I["hy_w_in"][j], 0, 3 * D, 512), Z, mode="tm", bias_ap=I["hy_b_in"][j])
    XV = C.dram("hyXV", [T, 3 * D])
    seqs = [(s * cf.L_P, cf.L_P) for s in range(cf.B_P)] + [(TP, cf.L_S)]
    CW = min(D, 2048)
    with C.phase("hyshort"):
        src = C.buf("src", sb=False); dst = C.buf("dst", sb=False)
        cr = Rot(C, "hc", [128, CW], 2); pr = Rot(C, "hp", [128, CW], 2); nr = Rot(C, "hn", [128, CW], 2)
        t1r = Rot(C, "ht1", [128, CW], 2); t2r = Rot(C, "ht2", [128, CW], 2)
        for c0 in range(0, 3 * D, CW):
            ws = []
            for k in range(3):
                ws.append(bc_load(C, "hw%d" % k, I["hy_short_w"][j, k, c0:c0 + CW], CW))
            sbt, sbb = bc_load(C, "hsb", I["hy_short_b"][j, c0:c0 + CW], CW)
            for (s0, L) in seqs:
                for tt in range(s0, s0 + L, 128):
                    cu, cub = cr.next(); pv, pvb = pr.next(); nx, nxb = nr.next()
                    S.dma("sp", cu[:], Z[tt:tt + 128, c0:c0 + CW], cub, src)
                    if tt == s0:
                        S.op("pool", lambda pv=pv: nc.gpsimd.memset(pv[:], 0.0), writes=[pvb])
                        S.dma("sp", pv[1:128, :], Z[tt:tt + 127, c0:c0 + CW], pvb, src)
                    else:
                        S.dma("sp", pv[:], Z[tt - 1:tt + 127, c0:c0 + CW], pvb, src)
                    if tt + 128 == s0 + L:
                        S.op("pool", lambda nx=nx: nc.gpsimd.memset(nx[:], 0.0), writes=[nxb])
                        S.dma("sp", nx[0:127, :], Z[tt + 1:tt + 128, c0:c0 + CW], nxb, src)
                    else:
                        S.dma("sp", nx[:], Z[tt + 1:tt + 129, c0:c0 + CW], nxb, src)
                    a, ab = t1r.next(); b2, b2b = t2r.next()
                    S.op("dve", lambda a=a, pv=pv: nc.vector.tensor_tensor(out=a[:], in0=pv[:], in1=ws[0][0][:], op=ALU.mult), reads=[pvb, ws[0][1]], writes=[ab])
                    S.op("pool", lambda b2=b2, cu=cu: nc.gpsimd.tensor_tensor(out=b2[:], in0=cu[:], in1=ws[1][0][:], op=ALU.mult), reads=[cub, ws[1][1]], writes=[b2b])
                    S.op("dve", lambda a=a, b2=b2: nc.vector.tensor_tensor(out=a[:], in0=a[:], in1=b2[:], op=ALU.add), reads=[b2b], writes=[ab])
                    S.op("pool", lambda b2=b2, nx=nx: nc.gpsimd.tensor_tensor(out=b2[:], in0=nx[:], in1=ws[2][0][:], op=ALU.mult), reads=[nxb, ws[2][1]], writes=[b2b])
                    S.op("dve", lambda a=a, b2=b2: nc.vector.tensor_tensor(out=a[:], in0=a[:], in1=b2[:], op=ALU.add), reads=[b2b], writes=[ab])
                    S.op("dve", lambda a=a: nc.vector.tensor_tensor(out=a[:], in0=a[:], in1=sbt[:], op=ALU.add), reads=[sbb], writes=[ab])
                    S.dma("pool", XV[tt:tt + 128, c0:c0 + CW], a[:], dst, ab)
    V1 = C.dram("hyV1", [T, D])
    V2T = C.dram("hyV2T", [D, T])
    done_L = {}
    for (s0, L) in seqs:
        if L not in done_L:
            Cm = C.dram("Cm", [L, L], BF16); Sm = C.dram("Sm", [L, L], BF16); CmT = C.dram("CmT", [L, L], BF16); NSmT = C.dram("NSmT", [L, L], BF16)
            hy_dft_gen(C, L, Cm, Sm, CmT, NSmT)
            KA = C.dram("KA", [L, 2 * D]); KB = C.dram("KB", [L, 2 * D])
            hy_filter(C, cf, I, j, L, KA, KB)
            KR = C.dram("KR", [L, 2 * D]); KI = C.dram("KI", [L, 2 * D])
            LC = L // 128
            with C.phase("hykhat"):
                src = C.buf("src", sb=False); dst = C.buf("dst", sb=False)
                dp = DFTPass(C, L)
                ar = Rot(C, "ka", [128, LC, 512], 2, dtype=BF16); br = Rot(C, "kb", [128, LC, 512], 2, dtype=BF16)
                pp = Rot(C, "kp", [128, 512], 4, psum=True)
                orr = Rot(C, "ko", [128, 512], 4)
                for c0 in range(0, 2 * D, 512):
                    a, ab = ar.next(); b_, bb = br.next()
                    S.dma("pool", a[:], KA[:, c0:c0 + 512].rearrange("(c p) n -> p c n", p=128), ab, src)
                    S.dma("pool", b_[:], KB[:, c0:c0 + 512].rearrange("(c p) n -> p c n", p=128), bb, src)
                    for ot in range(LC):
                        for (mat, rhs, rb, out_d) in ((Cm, a, ab, KR), (Sm, b_, bb, KI)):
                            p_, pb_ = pp.next()
                            dp.run([(mat, rhs, rb)], p_, pb_, ot, 512)
                            o, ob = orr.next()
                            S.op("act", lambda o=o, p_=p_: nc.scalar.copy(out=o[:], in_=p_[:]), reads=[pb_], writes=[ob])
                            S.dma("sp", out_d[ot * 128:(ot + 1) * 128, c0:c0 + 512], o[:], dst, ob)
            done_L[L] = (Cm, Sm, CmT, NSmT, KR, KI)
    for (s0, L) in seqs:
        Cm, Sm, CmT, NSmT, KR, KI = done_L[L]
        LC = L // 128
        for o_ in range(2):
            Uin = XV[:, 2 * D:3 * D] if o_ == 0 else V1
            gate = XV[:, o_ * D:(o_ + 1) * D]
            with C.phase("hyconv"):
                src = C.buf("src", sb=False); dst = C.buf("dst", sb=False)
                dp = DFTPass(C, L)
                ur = Rot(C, "cu", [128, LC, 512], 2, dtype=BF16)
                uf = Rot(C, "cuf", [128, 512], 2)
                yr = Rot(C, "cyr", [128, LC, 512], 1, dtype=BF16); yi = Rot(C, "cyi", [128, LC, 512], 1, dtype=BF16)
                pp = Rot(C, "cp", [128, 512], 4, psum=True)
                kr = Rot(C, "ckr", [128, 512], 2); ki = Rot(C, "cki", [128, 512], 2)
                tr = Rot(C, "ct", [128, 512], 4)
                gr_ = Rot(C, "cg", [128, 512], 2)
                orr = Rot(C, "co", [128, 512], 3)
                bia, biab = bc_load(C, "cb", I["hy_bias"][j, o_], D)
                ts = TStore(C) if o_ == 1 else None
                for c0 in range(0, D, 512):
                    n = min(512, D - c0)
                    u, ub = ur.next()
                    S.dma("pool", u[:, :, :n], Uin[s0:s0 + L, c0:c0 + n].rearrange("(c p) n -> p c n", p=128), ub, src)
                    Yr, Yrb = yr.next(); Yi, Yib = yi.next()
                    for ot in range(LC):
                        pc, pcb = pp.next(); ps_, psb_ = pp.next()
                        dp.run([(Cm, u, ub)], pc, pcb, ot, n)
                        dp.run([(Sm, u, ub)], ps_, psb_, ot, n)
                        k_r, krb = kr.next(); k_i, kib = ki.next()
                        S.dma("sp", k_r[:, :n], KR[ot * 128:(ot + 1) * 128, o_ * D + c0:o_ * D + c0 + n], krb, src)
                        S.dma("sp", k_i[:, :n], KI[ot * 128:(ot + 1) * 128, o_ * D + c0:o_ * D + c0 + n], kib, src)
                        ta, tab = tr.next(); tb_, tbb = tr.next()
                        S.op("dve", lambda: nc.vector.tensor_tensor(out=ta[:, :n], in0=pc[:, :n], in1=k_r[:, :n], op=ALU.mult), reads=[pcb, krb], writes=[tab])
                        S.op("dve", lambda: nc.vector.tensor_tensor(out=tb_[:, :n], in0=ps_[:, :n], in1=k_i[:, :n], op=ALU.mult), reads=[psb_, kib], writes=[tbb])
                        S.op("pool", lambda: nc.gpsimd.tensor_tensor(out=Yr[:, ot, :n], in0=ta[:, :n], in1=tb_[:, :n], op=ALU.add), reads=[tab, tbb], writes=[Yrb])
                        tc_, tcb = tr.next(); td, tdb = tr.next()
                        S.op("dve", lambda: nc.vector.tensor_tensor(out=tc_[:, :n], in0=pc[:, :n], in1=k_i[:, :n], op=ALU.mult), reads=[pcb, kib], writes=[tcb])
                        S.op("dve", lambda: nc.vector.tensor_tensor(out=td[:, :n], in0=ps_[:, :n], in1=k_r[:, :n], op=ALU.mult), reads=[psb_, krb], writes=[tdb])
                        S.op("pool", lambda: nc.gpsimd.tensor_tensor(out=Yi[:, ot, :n], in0=tc_[:, :n], in1=td[:, :n], op=ALU.subtract), reads=[tcb, tdb], writes=[Yib])
                    for ot in range(LC):
                        py, pyb = pp.next()
                        dp.run([(CmT, Yr, Yrb), (NSmT, Yi, Yib)], py, pyb, ot, n)
                        g, gb = gr_.next()
                        S.dma("sp", g[:, :n], gate[s0 + ot * 128:s0 + (ot + 1) * 128, c0:c0 + n], gb, src)
                        ta, tab = tr.next()
                        u32, u32b = uf.next()
                        S.dma("sp", u32[:, :n], Uin[s0 + ot * 128:s0 + (ot + 1) * 128, c0:c0 + n], u32b, src)
                        S.op("dve", lambda: nc.vector.tensor_tensor(out=ta[:, :n], in0=u32[:, :n], in1=bia[:, c0:c0 + n], op=ALU.mult), reads=[u32b, biab], writes=[tab])
                        S.op("dve", lambda: nc.vector.scalar_tensor_tensor(out=ta[:, :n], in0=py[:, :n], scalar=1.0 / L, in1=ta[:, :n], op0=ALU.mult,
                                                                          op1=ALU.add), reads=[pyb], writes=[tab])
                        o, ob = orr.next()
                        S.op("pool", lambda: nc.gpsimd.tensor_tensor(out=o[:, :n], in0=ta[:, :n], in1=g[:, :n], op=ALU.mult), reads=[tab, gb], writes=[ob])
                        if o_ == 0:
                            S.dma("sp", V1[s0 + ot * 128:s0 + (ot + 1) * 128, c0:c0 + n], o[:, :n], dst, ob)
                        else:
                            ts.run(o, ob, n, V2T, dst, s0 + ot * 128, 128, col0=c0)
    Y = C.dram("Y", [T, D])
    linear_bf(C, V2T, T, D, wcols(I["hy_w_out"][j], 0, D, 512), Y, mode="tm")
    Yb = C.buf("Y", sb=False)
    residual(C, X, Xb, Y, Yb, D, MR["g1"])
```
